# Optimizing a Trainium2 kernel written in Bass

```python
import math
import jax, jax.numpy as jnp
from jax import lax
import numpy as np

D_MODEL = 1024
BATCH = 16
SEQ = 2048
DEPTH = 4
DEC_BATCH = 128
DEC_SEQ = 4
PAST_LEN = 8192
PAGE_SIZE = 128

N_AB = (DEPTH + 1) // 2
N_C = DEPTH // 2
MLA_HEADS = 8
Q_LORA = 384
KV_LORA = 256
QK_NOPE = 64
QK_ROPE = 32
V_DIM = 64
ROPE_THETA = 10000.0
Q_BLOCK = 128
ATTN_SCALE = 1.0 / math.sqrt(QK_NOPE + QK_ROPE)
LRU_WIDTH = 512
LRU_HEADS = 8
LRU_HEAD_DIM = LRU_WIDTH // LRU_HEADS
CONV_WIDTH = 4
LRU_C = 8.0
MIX_AB = MLA_HEADS * V_DIM + LRU_WIDTH
IN_AB = Q_LORA + KV_LORA + QK_ROPE + 2 * LRU_WIDTH
SPLIT_AB = [Q_LORA, Q_LORA + KV_LORA, Q_LORA + KV_LORA + QK_ROPE, Q_LORA + KV_LORA + QK_ROPE + LRU_WIDTH]
CHUNK = 128
GMLP_WIDTH = 1024
GMLP_GROUPS = 8
GMLP_GROUP_DIM = GMLP_WIDTH // GMLP_GROUPS
FFN_HIDDEN = -(-8 * D_MODEL // (3 * 256)) * 256
PLE_DIM = 256
EPS = 1e-6

kernel_name = 'hybrid_mla_rglru_gmlp_step'


def rmsnorm(x, g):
    xf = x.astype(jnp.float32)
    y = xf * lax.rsqrt(jnp.mean(xf * xf, axis=-1, keepdims=True) + EPS)
    return (y * g.astype(jnp.float32)).astype(x.dtype)


def layernorm(x, g, b):
    xf = x.astype(jnp.float32)
    xc = xf - jnp.mean(xf, axis=-1, keepdims=True)
    y = xc * lax.rsqrt(jnp.mean(xc * xc, axis=-1, keepdims=True) + EPS)
    return (y * g.astype(jnp.float32) + b.astype(jnp.float32)).astype(x.dtype)


def rope(x, pos):
    half = x.shape[-1] // 2
    inv = jnp.exp(-math.log(ROPE_THETA) * jnp.arange(half, dtype=jnp.float32) / half)
    ang = pos.astype(jnp.float32)[:, None] * inv[None, :]
    shp = (pos.shape[0],) + (1,) * (x.ndim - 3) + (half,)
    cos = jnp.cos(ang).reshape(shp)
    sin = jnp.sin(ang).reshape(shp)
    xf = x.astype(jnp.float32)
    x1, x2 = xf[..., :half], xf[..., half:]
    return jnp.concatenate([x1 * cos - x2 * sin, x2 * cos + x1 * sin], axis=-1).astype(x.dtype)


def mla_project(zq, zkv, zpe, pos, g_q, g_kv, w_uq, w_uk):
    b, t, _ = zq.shape
    q = (rmsnorm(zq, g_q) @ w_uq).reshape(b, t, MLA_HEADS, QK_NOPE + QK_ROPE)
    q_pe = rope(q[..., QK_NOPE:], pos)
    q_lat = jnp.einsum('bthn,hnc->bthc', q[..., :QK_NOPE], w_uk)
    c_kv = rmsnorm(zkv, g_kv)
    k_pe = rope(zpe, pos)
    return q_lat, q_pe, c_kv, k_pe


def mla_prompt_attend(q_lat, q_pe, c_kv, k_pe):
    b, s, h, c = q_lat.shape
    nb = s // Q_BLOCK
    ql = q_lat.reshape(b, nb, Q_BLOCK, h, c).transpose(1, 0, 2, 3, 4)
    qp = q_pe.reshape(b, nb, Q_BLOCK, h, QK_ROPE).transpose(1, 0, 2, 3, 4)
    key_pos = jnp.arange(s)

    def block(args):
        ql_b, qp_b, bi = args
        sc = (jnp.einsum('bqhc,bkc->bhqk', ql_b, c_kv) + jnp.einsum('bqhr,bkr->bhqk', qp_b, k_pe)).astype(jnp.float32) * ATTN_SCALE
        q_pos = bi * Q_BLOCK + jnp.arange(Q_BLOCK)
        sc = jnp.where(key_pos[None, :] <= q_pos[:, None], sc, -jnp.inf)
        pr = jax.nn.softmax(sc, axis=-1).astype(c_kv.dtype)
        return jnp.einsum('bhqk,bkc->bqhc', pr, c_kv)

    o = lax.map(block, (ql, qp, jnp.arange(nb)))
    return o.transpose(1, 0, 2, 3, 4).reshape(b, s, h, c)


def mla_sample_attend(q_lat, q_pe, c_new, pe_new, past_c, past_pe):
    t = q_lat.shape[1]
    n_past = past_c.shape[1]
    s_past = (jnp.einsum('bthc,blc->bhtl', q_lat, past_c) + jnp.einsum('bthr,blr->bhtl', q_pe, past_pe)).astype(jnp.float32) * ATTN_SCALE
    s_new = (jnp.einsum('bthc,bsc->bhts', q_lat, c_new) + jnp.einsum('bthr,bsr->bhts', q_pe, pe_new)).astype(jnp.float32) * ATTN_SCALE
    causal = jnp.tril(jnp.ones((t, t), dtype=bool))
    s_new = jnp.where(causal, s_new, -jnp.inf)
    pr = jax.nn.softmax(jnp.concatenate([s_past, s_new], axis=-1), axis=-1).astype(c_new.dtype)
    return jnp.einsum('bhtl,blc->bthc', pr[..., :n_past], past_c) + jnp.einsum('bhts,bsc->bthc', pr[..., n_past:], c_new)


def rglru_block(zx, zg, conv_buf, h0, conv_w, conv_b, w_rg, b_rg, w_ig, b_ig, lam):
    b, t, w = zx.shape
    xpad = jnp.concatenate([conv_buf.astype(zx.dtype), zx], axis=1)
    xc = conv_b + xpad[:, 0:t] * conv_w[0]
    for k in range(1, CONV_WIDTH):
        xc = xc + xpad[:, k:k + t] * conv_w[k]
    new_buf = xpad[:, t:]
    xh = xc.reshape(b, t, LRU_HEADS, LRU_HEAD_DIM)
    r = jax.nn.sigmoid(jnp.einsum('bthi,hij->bthj', xh, w_rg).reshape(b, t, w) + b_rg).astype(jnp.float32)
    ig = jax.nn.sigmoid(jnp.einsum('bthi,hij->bthj', xh, w_ig).reshape(b, t, w) + b_ig).astype(jnp.float32)
    log_a = -LRU_C * r * jax.nn.softplus(-lam.astype(jnp.float32))
    a = jnp.exp(log_a)
    bx = jnp.sqrt(-jnp.expm1(2.0 * log_a)) * ig * xc.astype(jnp.float32)

    def step(h, ab):
        a_t, b_t = ab
        h = a_t * h + b_t
        return h, h

    h_last, hs = lax.scan(step, h0.astype(jnp.float32), (a.transpose(1, 0, 2), bx.transpose(1, 0, 2)))
    y = hs.transpose(1, 0, 2).astype(zx.dtype) * jax.nn.gelu(zg)
    return y, h_last.astype(h0.dtype), new_buf


def chunk_gmlp(xn, w_in, ln_g, ln_b, w_s, b_s, w_out):
    b, t, _ = xn.shape
    z = jax.nn.gelu(xn @ w_in)
    u, v = z[..., :GMLP_WIDTH], z[..., GMLP_WIDTH:]
    v = layernorm(v, ln_g, ln_b)
    L = min(t, CHUNK)
    nch = t // L
    wm = (w_s[:, :L, :L] * jnp.tril(jnp.ones((L, L), w_s.dtype))).astype(v.dtype)
    vg = v.reshape(b, nch, L, GMLP_GROUPS, GMLP_GROUP_DIM)
    sp = jnp.einsum('gts,bcsgd->bctgd', wm, vg) + b_s[:, :L].T[:, :, None].astype(v.dtype)
    y = (u * sp.reshape(b, t, GMLP_WIDTH)) @ w_out
    return y, v


def trunk(x, p, pos, prm, past):
    b, t, _ = x.shape
    h = x
    ckv, kpe, lru, conv, vrows = [], [], [], [], []
    for i in range(DEPTH):
        j = i // 2
        xn = rmsnorm(h, prm['g_mix'][i])
        if i % 2 == 0:
            z = xn @ prm['w_in_ab'][j]
            zq, zkv, zpe, zx, zg = jnp.split(z, SPLIT_AB, axis=-1)
            q_lat, q_pe, c_kv, k_pe = mla_project(zq, zkv, zpe, pos, prm['g_qnorm'][j], prm['g_kvnorm'][j], prm['w_uq'][j], prm['w_uk'][j])
            if past is None:
                o_lat = mla_prompt_attend(q_lat, q_pe, c_kv, k_pe)
                buf0 = jnp.zeros((b, CONV_WIDTH - 1, LRU_WIDTH), x.dtype)
                h0 = jnp.zeros((b, LRU_WIDTH), x.dtype)
            else:
                pt = past['page_table']
                past_c = past['cache_ckv'][j, pt].reshape(b, -1, KV_LORA).astype(c_kv.dtype)
                past_pe = past['cache_kpe'][j, pt].reshape(b, -1, QK_ROPE).astype(k_pe.dtype)
                o_lat = mla_sample_attend(q_lat, q_pe, c_kv, k_pe, past_c, past_pe)
                buf0 = past['state_conv'][j]
                h0 = past['state_lru'][j]
            attn = jnp.einsum('bthc,hcv->bthv', o_lat, prm['w_uv'][j]).reshape(b, t, MLA_HEADS * V_DIM)
            lru_y, h_last, buf = rglru_block(zx, zg, buf0, h0, prm['conv_w'][j], prm['conv_b'][j], prm['w_rg'][j], prm['b_rg'][j], prm['w_ig'][j], prm['b_ig'][j], prm['lru_lambda'][j])
            mix = jnp.concatenate([attn, lru_y], axis=-1) @ prm['w_out_ab'][j]
            ckv.append(c_kv)
            kpe.append(k_pe)
            lru.append(h_last)
            conv.append(buf)
        else:
            mix, v = chunk_gmlp(xn, prm['w_in_c'][j], prm['ln_g_c'][j], prm['ln_b_c'][j], prm['w_s'][j], prm['b_s'][j], prm['w_out_c'][j])
            if past is not None:
                vrows.append(v)
        h = h + mix
        hn = rmsnorm(h, prm['g_ffn'][i])
        h = h + (jax.nn.silu(hn @ prm['w_gate'][i]) * (hn @ prm['w_up'][i])) @ prm['w_down'][i]
        gate = jax.nn.sigmoid(rmsnorm(h, prm['g_pe'][i]) @ prm['w_pg'][i])
        h = h + (p[i] @ prm['w_pe'][i]) * gate
    return rmsnorm(h, prm['g_final']), ckv, kpe, lru, conv, vrows


def setup_inputs(seed: int = 0) -> dict:
    key = jax.random.key(seed)
    ks = iter(jax.random.split(key, 48))
    f32 = jnp.float32
    n_pages = PAST_LEN // PAGE_SIZE
    n_used = DEC_BATCH * n_pages
    n_phys = n_used + n_used // 4

    def nrm(shape, scale=1.0):
        return jax.random.normal(next(ks), shape, f32) * scale

    def gain(shape):
        return 1.0 + nrm(shape, 0.05)

    x_prompt = nrm((BATCH, SEQ, D_MODEL))
    x_sample = nrm((DEC_BATCH, DEC_SEQ, D_MODEL))
    cache_ckv = nrm((N_AB, n_phys, PAGE_SIZE, KV_LORA))
    cache_kpe = nrm((N_AB, n_phys, PAGE_SIZE, QK_ROPE))
    state_lru = nrm((N_AB, DEC_BATCH, LRU_WIDTH))
    state_conv = nrm((N_AB, DEC_BATCH, CONV_WIDTH - 1, LRU_WIDTH))
    perm = jax.random.permutation(next(ks), n_phys)
    page_table = perm[:n_used].reshape(DEC_BATCH, n_pages).astype(jnp.int32)
    p_prompt = nrm((DEPTH, BATCH, SEQ, PLE_DIM))
    p_sample = nrm((DEPTH, DEC_BATCH, DEC_SEQ, PLE_DIM))
    a0 = jax.random.uniform(next(ks), (N_AB, LRU_WIDTH), f32, 0.9, 0.999)
    s0 = a0 ** (1.0 / LRU_C)
    lru_lambda = jnp.log(s0) - jnp.log1p(-s0)
    return {
        'x_prompt': x_prompt,
        'x_sample': x_sample,
        'cache_ckv': cache_ckv,
        'cache_kpe': cache_kpe,
        'state_lru': state_lru,
        'state_conv': state_conv,
        'page_table': page_table,
        'p_prompt': p_prompt,
        'p_sample': p_sample,
        'g_mix': gain((DEPTH, D_MODEL)),
        'g_ffn': gain((DEPTH, D_MODEL)),
        'g_pe': gain((DEPTH, D_MODEL)),
        'g_final': gain((D_MODEL,)),
        'w_in_ab': nrm((N_AB, D_MODEL, IN_AB), D_MODEL ** -0.5),
        'g_qnorm': gain((N_AB, Q_LORA)),
        'g_kvnorm': gain((N_AB, KV_LORA)),
        'w_uq': nrm((N_AB, Q_LORA, MLA_HEADS * (QK_NOPE + QK_ROPE)), Q_LORA ** -0.5),
        'w_uk': nrm((N_AB, MLA_HEADS, QK_NOPE, KV_LORA), KV_LORA ** -0.5),
        'w_uv': nrm((N_AB, MLA_HEADS, KV_LORA, V_DIM), KV_LORA ** -0.5),
        'conv_w': nrm((N_AB, CONV_WIDTH, LRU_WIDTH), CONV_WIDTH ** -0.5),
        'conv_b': nrm((N_AB, LRU_WIDTH), 0.01),
        'w_rg': nrm((N_AB, LRU_HEADS, LRU_HEAD_DIM, LRU_HEAD_DIM), LRU_HEAD_DIM ** -0.5),
        'b_rg': nrm((N_AB, LRU_WIDTH), 0.01),
        'w_ig': nrm((N_AB, LRU_HEADS, LRU_HEAD_DIM, LRU_HEAD_DIM), LRU_HEAD_DIM ** -0.5),
        'b_ig': nrm((N_AB, LRU_WIDTH), 0.01),
        'lru_lambda': lru_lambda,
        'w_out_ab': nrm((N_AB, MIX_AB, D_MODEL), MIX_AB ** -0.5),
        'w_in_c': nrm((N_C, D_MODEL, 2 * GMLP_WIDTH), D_MODEL ** -0.5),
        'ln_g_c': gain((N_C, GMLP_WIDTH)),
        'ln_b_c': nrm((N_C, GMLP_WIDTH), 0.01),
        'w_s': nrm((N_C, GMLP_GROUPS, CHUNK, CHUNK), CHUNK ** -0.5),
        'b_s': gain((N_C, GMLP_GROUPS, CHUNK)),
        'w_out_c': nrm((N_C, GMLP_WIDTH, D_MODEL), GMLP_WIDTH ** -0.5),
        'w_gate': nrm((DEPTH, D_MODEL, FFN_HIDDEN), D_MODEL ** -0.5),
        'w_up': nrm((DEPTH, D_MODEL, FFN_HIDDEN), D_MODEL ** -0.5),
        'w_down': nrm((DEPTH, FFN_HIDDEN, D_MODEL), FFN_HIDDEN ** -0.5),
        'w_pe': nrm((DEPTH, PLE_DIM, D_MODEL), PLE_DIM ** -0.5),
        'w_pg': nrm((DEPTH, D_MODEL, D_MODEL), D_MODEL ** -0.5),
    }


def reference(x_prompt, x_sample, cache_ckv, cache_kpe, state_lru, state_conv, page_table, p_prompt, p_sample,
              g_mix, g_ffn, g_pe, g_final, w_in_ab, g_qnorm, g_kvnorm, w_uq, w_uk, w_uv, conv_w, conv_b,
              w_rg, b_rg, w_ig, b_ig, lru_lambda, w_out_ab, w_in_c, ln_g_c, ln_b_c, w_s, b_s, w_out_c,
              w_gate, w_up, w_down, w_pe, w_pg):
    prm = {
        'g_mix': g_mix, 'g_ffn': g_ffn, 'g_pe': g_pe, 'g_final': g_final,
        'w_in_ab': w_in_ab, 'g_qnorm': g_qnorm, 'g_kvnorm': g_kvnorm, 'w_uq': w_uq, 'w_uk': w_uk, 'w_uv': w_uv,
        'conv_w': conv_w, 'conv_b': conv_b, 'w_rg': w_rg, 'b_rg': b_rg, 'w_ig': w_ig, 'b_ig': b_ig,
        'lru_lambda': lru_lambda, 'w_out_ab': w_out_ab,
        'w_in_c': w_in_c, 'ln_g_c': ln_g_c, 'ln_b_c': ln_b_c, 'w_s': w_s, 'b_s': b_s, 'w_out_c': w_out_c,
        'w_gate': w_gate, 'w_up': w_up, 'w_down': w_down, 'w_pe': w_pe, 'w_pg': w_pg,
    }
    past = {'cache_ckv': cache_ckv, 'cache_kpe': cache_kpe, 'state_lru': state_lru,
            'state_conv': state_conv, 'page_table': page_table}
    pos_prompt = jnp.arange(x_prompt.shape[1], dtype=jnp.int32)
    past_len = page_table.shape[1] * PAGE_SIZE
    pos_sample = past_len + jnp.arange(x_sample.shape[1], dtype=jnp.int32)
    y_prompt, ckv_p, kpe_p, lru_p, conv_p, _ = trunk(x_prompt, p_prompt, pos_prompt, prm, None)
    y_sample, ckv_s, kpe_s, lru_s, conv_s, v_s = trunk(x_sample, p_sample, pos_sample, prm, past)
    return (y_prompt, y_sample,
            jnp.stack(ckv_p), jnp.stack(kpe_p), jnp.stack(lru_p), jnp.stack(conv_p),
            jnp.stack(ckv_s), jnp.stack(kpe_s), jnp.stack(lru_s), jnp.stack(conv_s),
            jnp.stack(v_s))
```

```python
import math
import contextlib
import numpy as np
import concourse.bass as bass
import concourse.mybir as mybir
from concourse.bass_utils import run_bass_kernel_spmd

F32 = mybir.dt.float32
BF16 = mybir.dt.bfloat16
I32 = mybir.dt.int32
AF = mybir.ActivationFunctionType
ALU = mybir.AluOpType

NCORES = 8
D = 1024
SEQ = 2048
NTP = 1024
NSB = 16
NTS = 64
DEPTH = 4
QL, KVL, ROPE = 384, 256, 32
LRU = 512
INAB = 1696
FFN = 2816
PLE = 256
EPS = 1e-6
ATTN_SCALE = 1.0 / math.sqrt(96.0)
PAST = 8192
NPHYS = 10240


class _Op:
    __slots__ = ("eng", "fn", "deps", "dma", "cnt", "marked", "pos", "waits", "nparts")


class Prog:
    ENGS = ("pe", "act", "dve", "pool", "sp")

    def __init__(self, nc):
        self.nc = nc
        self.streams = {e: [] for e in self.ENGS}
        self.last_w = {}
        self.readers = {}
        self.dma_last = {}
        self.dma_cnt = {}
        self.out_dmas = []

    def op(self, eng, fn, reads=(), writes=(), dma_key=None, nparts=1, is_out=False, extra_deps=()):
        o = _Op()
        o.eng = eng
        o.fn = fn
        o.dma = dma_key
        o.marked = False
        o.waits = None
        o.nparts = nparts
        deps = list(extra_deps)
        for k in reads:
            w = self.last_w.get(k)
            if w is not None:
                deps.append(w)
        for k in writes:
            w = self.last_w.get(k)
            if w is not None:
                deps.append(w)
            r = self.readers.get(k)
            if r:
                deps.extend(r.values())
        if dma_key is not None:
            p = self.dma_last.get(dma_key)
            if p is not None:
                deps.append(p)
            self.dma_last[dma_key] = o
            c = self.dma_cnt.get(dma_key, 0) + 16 * nparts
            self.dma_cnt[dma_key] = c
            o.cnt = c
        else:
            o.cnt = None
        o.deps = deps
        o.pos = len(self.streams[eng])
        self.streams[eng].append(o)
        for k in writes:
            self.last_w[k] = o
            self.readers[k] = {}
        for k in reads:
            rk = ("dma", id(o)) if dma_key is not None else eng
            self.readers.setdefault(k, {})[rk] = o
        if is_out:
            self.out_dmas.append(o)
        return o

    def pe(self, fn, reads=(), writes=()):
        return self.op("pe", fn, reads, writes)

    def act(self, fn, reads=(), writes=()):
        return self.op("act", fn, reads, writes)

    def dve(self, fn, reads=(), writes=()):
        return self.op("dve", fn, reads, writes)

    def pool(self, fn, reads=(), writes=()):
        return self.op("pool", fn, reads, writes)

    def dma(self, q, key, fn, reads=(), writes=(), nparts=1, is_out=False):
        return self.op(q, fn, reads, writes, dma_key=key, nparts=nparts, is_out=is_out)

    def barrier(self):
        lasts = [s[-1] for s in self.streams.values() if s] + list(self.dma_last.values())
        for eng in self.ENGS:
            self.op(eng, None, extra_deps=lasts)

    def finalize(self):
        self.op("sp", None, extra_deps=list(self.out_dmas))
        for eng in self.ENGS:
            waited = {}
            for o in self.streams[eng]:
                ws = []
                for d in o.deps:
                    if d is o:
                        continue
                    if d.dma is not None:
                        key = ("dma", d.dma)
                        if waited.get(key, 0) >= d.cnt:
                            continue
                        waited[key] = d.cnt
                        ws.append(d)
                    else:
                        if d.fn is None:
                            continue
                        if d.eng == eng and eng == "pe":
                            continue
                        if waited.get(d.eng, -1) >= d.pos:
                            continue
                        waited[d.eng] = d.pos
                        d.marked = True
                        ws.append(d)
                o.waits = ws
        for eng in self.ENGS:
            c = 0
            for o in self.streams[eng]:
                if o.dma is None and o.marked:
                    c += 1
                    o.cnt = c

    def emit(self, es):
        nc = self.nc
        self.finalize()
        esem = {e: es.enter_context(nc.semaphore("s_" + e)) for e in self.ENGS}
        dsem = {}
        for k in self.dma_cnt:
            dsem[k] = es.enter_context(nc.semaphore("d_%d" % len(dsem)))
        block = es.enter_context(nc.Block())

        def run(eng, e):
            for o in self.streams[eng]:
                need = {}
                for d in o.waits:
                    s = dsem[d.dma] if d.dma is not None else esem[d.eng]
                    key = id(s)
                    if key not in need or need[key][1] < d.cnt:
                        need[key] = (s, d.cnt)
                for s, v in need.values():
                    e.wait_ge(s, v)
                if o.fn is None:
                    continue
                if o.dma is not None:
                    o.fn(e, dsem[o.dma])
                else:
                    ins = o.fn(e)
                    if o.marked:
                        ins.then_inc(esem[eng], 1)

        @block.tensor
        def _(e):
            run("pe", e)

        @block.scalar
        def _(e):
            run("act", e)

        @block.vector
        def _(e):
            run("dve", e)

        @block.gpsimd
        def _(e):
            run("pool", e)

        @block.sync
        def _(e):
            run("sp", e)


VC = {}
_c = 0
for _n, _w in (("g_mix", 32), ("g_ffn", 32), ("g_pe", 32), ("g_q", 6), ("conv_w", 32), ("conv_b", 8),
               ("b_rg", 8), ("b_ig", 8), ("lam", 8), ("eps", 1), ("one", 1), ("pm8", 1)):
    VC[_n] = _c
    _c += _w
NV = 256


class Builder:
    def __init__(self, nc, do_sample=True):
        self.nc = nc
        self.P = Prog(nc)
        self.es = contextlib.ExitStack()
        self.do_sample = do_sample
        self.cnt = {}
        self.bank_i = 0
        self.held = set()
        self.pf = {}

    def din(self, name, shape, dt=F32):
        return self.nc.dram_tensor(name, list(shape), dt, kind="ExternalInput").ap()

    def dout(self, name, shape):
        return self.nc.dram_tensor(name, list(shape), F32, kind="ExternalOutput").ap()

    def view(self, off, shape, dt):
        n = 1
        for x in shape:
            n *= x
        esz = 4 if dt in (F32, I32) else 2
        a = self.arena[:, off // 2: off // 2 + n * esz // 2]
        if dt != BF16:
            a = a.bitcast(dt)
        if len(shape) == 2:
            a = a.rearrange("p (a b) -> p a b", a=shape[0])
        elif len(shape) == 3:
            a = a.rearrange("p (a b c) -> p a b c", a=shape[0], b=shape[1])
        return a

    def rot(self, name, n):
        i = self.cnt.get(name, 0)
        self.cnt[name] = i + 1
        return i % n

    def bank(self, hold=False):
        while True:
            b = self.bank_i
            self.bank_i = (b + 1) % 8
            if b not in self.held:
                break
        if hold:
            self.held.add(b)
        return b

    def release(self, b):
        self.held.discard(b)

    def evac_eng(self):
        return "act" if self.rot("evac", 2) == 0 else "dve"

    def ld(self, q, key, out, in_, wkeys, rkeys=()):
        wkeys = list(wkeys) + [("W8b", k[1]) for k in wkeys if isinstance(k, tuple) and k[0] == "W8"]
        self.P.dma(q, key, lambda e, s: e.dma_start(out=out, in_=in_).then_inc(s, 16), reads=rkeys, writes=wkeys)

    def st(self, key, out, in_, rkeys):
        self.P.dma("sp", key, lambda e, s: e.dma_start(out=out, in_=in_).then_inc(s, 16), reads=rkeys, is_out=True)

    def w8(self):
        i = self.rot("w8", 4)
        return i, self.W8[i], ("W8", i)

    def w4(self):
        i = self.rot("w4", 4)
        return i, self.W4[i], ("W4", i)

    def load_w8(self, wd, c0, n):
        i, t, k = self.w8()
        src = wd[:, c0:c0 + n].rearrange("(k p) n -> p k n", p=128)
        self.ld("pool", ("w8", i), t[:, :, 0:n], src, [k])
        return t, k

    def mm(self, b, n, lhsT, rhs, start, stop, rkeys, m=128, c0=0):
        ps = self.PS[b]
        self.P.pe(lambda e: e.matmul(ps[0:m, c0:c0 + n], lhsT, rhs, start=start, stop=stop),
                  reads=rkeys, writes=[("ps", b)])

    def build(self):
        nc, P, es = self.nc, self.P, self.es
        d = {}
        self.d = d
        d["xp"] = self.din("xp", [2, SEQ, D])
        d["pp"] = self.din("pp", [DEPTH, 2, SEQ, PLE])
        d["xs"] = self.din("xs", [NTS, D])
        d["pps"] = self.din("pps", [DEPTH, NTS, PLE])
        import os
        nph = int(os.environ.get("MK_NPH", str(NPHYS if self.do_sample else 2)))
        d["cache_ckv"] = self.din("cache_ckv", [2, nph * 8, 16 * KVL])
        d["cache_kpe"] = self.din("cache_kpe", [2, nph * 8, 16 * ROPE])
        d["st_lru"] = self.din("st_lru", [2, NSB, LRU])
        d["st_conv"] = self.din("st_conv", [2, NSB * 3, LRU])
        d["ptab"] = self.din("ptab", [128, NSB * 4], I32)
        d["w_in_ab"] = self.din("w_in_ab", [2, D, INAB])
        d["wuq"] = self.din("wuq", [2, QL, 1024])
        d["wuk"] = self.din("wuk", [2, KVL, 768])
        d["wukn"] = self.din("wukn", [2, 64, 8 * KVL])
        d["wuv"] = self.din("wuv", [2, KVL, 512])
        d["wuvh"] = self.din("wuvh", [2, KVL, 512])
        d["wgate"] = self.din("wgate", [2, 128, 1024])
        d["w_out_ab"] = self.din("w_out_ab", [2, D, D])
        d["w_in_c"] = self.din("w_in_c", [2, D, 2048])
        d["wsT"] = self.din("wsT", [2, 128, 1024])
        d["wsTs"] = self.din("wsTs", [2, 64, 512])
        d["w_out_c"] = self.din("w_out_c", [2, D, D])
        d["w_gate"] = self.din("w_gate", [DEPTH, D, FFN])
        d["w_up"] = self.din("w_up", [DEPTH, D, FFN])
        d["w_down"] = self.din("w_down", [DEPTH, FFN, D])
        d["w_pe"] = self.din("w_pe", [DEPTH, PLE, D])
        d["w_pg"] = self.din("w_pg", [DEPTH, D, D])
        d["vecs"] = self.din("vecs", [128, NV])
        d["gkv"] = self.din("gkv", [2, 128, KVL])
        d["gfin"] = self.din("gfin", [128, D])
        d["lng"] = self.din("lng", [2, 128, D])
        d["lnb"] = self.din("lnb", [2, 128, D])
        d["bsb"] = self.din("bsb", [2, 128, 1024])
        d["bsbs"] = self.din("bsbs", [2, 128, 512])
        d["ident"] = self.din("ident", [128, 128])
        d["ones"] = self.din("ones", [128, 128])
        d["maskc"] = self.din("maskc", [128, 128])
        d["masks"] = self.din("masks", [64, 64])
        d["maskn"] = self.din("maskn", [64, 512])
        d["pesel"] = self.din("pesel", [32, 96])
        d["cosq"] = self.din("cosq", [32, SEQ + NTS])
        d["sinq"] = self.din("sinq", [32, SEQ + NTS])
        d["cosk"] = self.din("cosk", [SEQ + NTS, 32])
        d["sink"] = self.din("sink", [SEQ + NTS, 32])
        o = {}
        self.o = o
        o["y_p"] = self.dout("y_p", [2, SEQ, D])
        o["y_s"] = self.dout("y_s", [NTS, D])
        o["ckv_p"] = self.dout("ckv_p", [2, 2, SEQ, KVL])
        o["kpe_p"] = self.dout("kpe_p", [2, 2, SEQ, ROPE])
        o["lru_p"] = self.dout("lru_p", [2, 2, LRU])
        o["conv_p"] = self.dout("conv_p", [2, 2, 3, LRU])
        o["ckv_s"] = self.dout("ckv_s", [2, NTS, KVL])
        o["kpe_s"] = self.dout("kpe_s", [2, NTS, ROPE])
        o["lru_s"] = self.dout("lru_s", [2, NSB, LRU])
        o["conv_s"] = self.dout("conv_s", [2, NSB, 3, LRU])
        o["v_s"] = self.dout("v_s", [2, NTS, 1024])

        self.arena = es.enter_context(nc.sbuf_tensor("arena", [128, 102400], BF16))
        self.PS = [es.enter_context(nc.psum_tensor("ps%d" % i, [128, 512], F32)) for i in range(8)]
        V = self.view
        self.H = V(0, [8, 1024], F32)
        self.XN = V(32768, [8, 1024], BF16)
        self.CKVT = V(49152, [2, 2, 2048], BF16)
        self.KPET = V(65536, [2, 2048], BF16)
        C0 = 73728
        self.IDF = V(C0, [128], F32)
        self.IDB = V(C0 + 512, [128], BF16)
        self.ONB = V(C0 + 768, [128], BF16)
        self.MKB = V(C0 + 1024, [128], BF16)
        self.PSEL = V(C0 + 1280, [96], BF16)
        self.VEC = V(C0 + 1536, [NV], F32)
        self.LAMC = V(C0 + 2560, [16], F32)
        self.STL = V(C0 + 2624, [8], F32)
        self.HIST = V(C0 + 2656, [8, 3], F32)
        self.MKS = V(C0 + 2752, [64], BF16)
        self.TMPV = V(C0 + 2880, [16], F32)
        self.W8 = [V(81920 + 8192 * i, [8, 512], BF16) for i in range(4)]
        self.W4 = [V(114688 + 4096 * i, [2, 1024], BF16) for i in range(4)]
        self.S0 = 131072

        self.load_consts()
        P.barrier()
        import os
        self.dbg_ph = os.environ.get("MK_PHASES", "kv,qn,attn,lru,gmlp,ffn,peg").split(",")
        self.dbg_layers = int(os.environ.get("MK_LAYERS", "4"))
        nun = int(os.environ.get("MK_UNITS", "4"))
        units = [dict(kind="p", seq=sq, u=u, NT=NTP, pos0=u * NTP) for sq in range(2) for u in range(2)][:nun]
        for U in units:
            self.run_unit(U)
        if self.do_sample:
            self.run_unit(dict(kind="s", NT=NTS, pos0=SEQ))
        P.emit(es)

    def vc(self, name, i=0):
        c = VC[name] + i
        return self.VEC[:, c:c + 1]

    def load_consts(self):
        P, d = self.P, self.d
        self.ld("sp", "c0", self.IDF, d["ident"], ["IDF"])
        self.ld("pool", "c1", self.IDB, d["ident"], ["IDB"])
        self.ld("pool", "c2", self.ONB, d["ones"], ["ONB"])
        self.ld("pool", "c3", self.MKB, d["maskc"], ["MKB"])
        self.ld("pool", "c4", self.PSEL[0:32, :], d["pesel"], ["PSEL"])
        self.ld("sp", "c5", self.VEC, d["vecs"], ["VEC"])
        self.ld("pool", "c6", self.MKS[0:64, :], d["masks"], ["MKS"])
        lam = self.VEC[:, VC["lam"]:VC["lam"] + 8]
        tv = self.TMPV[:, 0:8]
        lc = self.LAMC
        P.act(lambda e: e.activation(out=tv, in_=lam, func=AF.Exp, scale=-1.0), reads=["VEC"], writes=["TMPV"])
        P.act(lambda e: e.activation(out=tv, in_=tv, func=AF.Ln, bias=self.vc("one"), scale=1.0),
              reads=["TMPV", "VEC"], writes=["TMPV"])
        P.dve(lambda e: e.tensor_scalar(lc[:, 0:8], tv, -8.0, None, ALU.mult), reads=["TMPV"], writes=["LAMC"])
        P.dve(lambda e: e.tensor_scalar(lc[:, 8:16], tv, -16.0, None, ALU.mult), reads=["TMPV"], writes=["LAMC"])

    def run_unit(self, U):
        P = self.P
        NT = U["NT"]
        U["tiles"] = [(t0, min(512, NT - t0)) for t0 in range(0, NT, 512)]
        U["blocks"] = [(t0, min(128, NT - t0)) for t0 in range(0, NT, 128)]
        U["tag"] = "%s%s%s" % (U["kind"], U.get("seq", ""), U.get("u", ""))
        self.load_x(U)
        ph = self.dbg_ph
        for i in range(self.dbg_layers):
            j = i // 2
            self.norm_h(U, VC["g_mix"] + 8 * i)
            if i % 2 == 0:
                if "kv" in ph:
                    self.kv_phase(U, j)
                if "qn" in ph:
                    self.qn_phase(U, j)
                if "attn" in ph:
                    if U["kind"] == "p":
                        self.pf["attn"] = self.pre_attn_prompt(j)
                    P.barrier()
                    if U["kind"] == "p":
                        self.attn_prompt(U, j)
                    else:
                        self.attn_sample(U, j)
                if "lru" in ph:
                    d_ = self.d
                    self.pf["lru"] = (self.load_w8(d_["w_in_ab"][j], 672, 128), self.load_w8(d_["w_in_ab"][j], 1184, 128))
                P.barrier()
                if "lru" in ph:
                    self.lru_phase(U, j)
            else:
                if "gmlp" in ph:
                    self.pf["gmlp"] = self.load_w8(self.d["w_in_c"][j], 0, 512)
                    P.barrier()
                    self.gmlp_phase(U, j)
                    P.barrier()
            self.norm_h(U, VC["g_ffn"] + 8 * i)
            if "ffn" in ph:
                self.ffn_phase(U, i)
            self.norm_h(U, VC["g_pe"] + 8 * i)
            if "peg" in ph:
                self.peg_phase(U, i)
        self.final_phase(U)

    def load_x(self, U):
        P, d = self.P, self.d
        S0 = self.S0
        XT = [self.view(S0 + 51328 + 4096 * i, [1024], F32) for i in range(2)]
        for (t0, tn) in U["blocks"]:
            bi = self.rot("xt", 2)
            xt = XT[bi]
            if U["kind"] == "p":
                src = d["xp"][U["seq"], U["pos0"] + t0: U["pos0"] + t0 + tn, :]
            else:
                src = d["xs"][t0:t0 + tn, :]
            self.ld("sp", ("xt", bi), xt[0:tn, :], src, [("XT", bi)])
            for half in range(2):
                b = self.bank()
                ps = self.PS[b]
                for q in range(4):
                    m = half * 4 + q
                    P.pe(lambda e, ps=ps, q=q, m=m, xt=xt, tn=tn: e.transpose(
                        ps[:, q * 128:q * 128 + tn], xt[0:tn, m * 128:(m + 1) * 128], self.IDF[0:tn, 0:tn]),
                        reads=[("XT", bi), "IDF"], writes=[("ps", b)])
                hv = self.H[:, half * 4:half * 4 + 4, t0:t0 + tn]
                pv = ps[:, :].rearrange("p (a b) -> p a b", a=4)[:, :, 0:tn]
                eng = self.evac_eng()
                if eng == "act":
                    P.act(lambda e, hv=hv, pv=pv: e.activation(out=hv, in_=pv, func=AF.Copy),
                          reads=[("ps", b)], writes=[("Hm", half * 4 + q_, (t0 // 512) * 512) for q_ in range(4)])
                else:
                    P.dve(lambda e, hv=hv, pv=pv: e.tensor_copy(out=hv, in_=pv),
                          reads=[("ps", b)], writes=[("Hm", half * 4 + q_, (t0 // 512) * 512) for q_ in range(4)])

    def norm_h(self, U, gcol):
        P = self.P
        S0 = self.S0
        SQ = self.view(S0, [8, 512], BF16)
        RS = [self.view(S0 + 8192 + 2048 * i, [512], F32) for i in range(2)]
        for (t0, tn) in U["tiles"]:
            hv = self.H[:, :, t0:t0 + tn]
            sq = SQ[:, :, 0:tn]
            P.act(lambda e, hv=hv, sq=sq: e.activation(out=sq, in_=hv, func=AF.Square),
                  reads=[("Hm", m_, t0) for m_ in range(8)], writes=["SQ"])
            b = self.bank()
            for k in range(8):
                self.mm(b, tn, self.ONB, SQ[:, k, 0:tn], k == 0, k == 7, ["SQ", "ONB"])
            ri = self.rot("rs", 2)
            rs = RS[ri][:, 0:tn]
            ps = self.PS[b][:, 0:tn]
            P.act(lambda e, rs=rs, ps=ps: e.activation(out=rs, in_=ps, func=AF.Sqrt, bias=self.vc("eps"),
                                                       scale=1.0 / D), reads=[("ps", b), "VEC"], writes=[("RS", ri)])
            P.dve(lambda e, rs=rs: e.reciprocal(out=rs, in_=rs), reads=[("RS", ri)], writes=[("RS", ri)])
            for k in range(8):
                xo = self.XN[:, k, t0:t0 + tn]
                hi = self.H[:, k, t0:t0 + tn]
                g = self.VEC[:, gcol + k:gcol + k + 1]
                P.dve(lambda e, xo=xo, hi=hi, g=g, rs=rs: e.scalar_tensor_tensor(
                    out=xo, in0=hi, scalar=g, in1=rs, op0=ALU.mult, op1=ALU.mult),
                    reads=[("RS", ri), "VEC", ("Hm", k, t0)], writes=[("XN", k, t0)])

    def kv_phase(self, U, j):
        P, d, o = self.P, self.d, self.o
        S0 = self.S0
        NT = U["NT"]
        sample = U["kind"] == "s"
        wkv, wk = self.load_w8(d["w_in_ab"][j], QL, KVL + ROPE)
        KV0 = S0 + 33792
        GKV = self.view(KV0, [KVL], F32)
        COSK = self.view(KV0 + 1024, [8, 32], F32)
        SINK = self.view(KV0 + 2048, [8, 32], F32)
        self.ld("sp", "gkv", GKV, d["gkv"][j], ["GKV"])
        p0 = U["pos0"]
        nb = len(U["blocks"])
        bn = U["blocks"][0][1]
        self.ld("sp", "cosk", COSK[0:bn, 0:nb, :], d["cosk"][p0:p0 + NT, :].rearrange("(b p) r -> p b r", p=bn), ["COSK"])
        self.ld("sp", "sink", SINK[0:bn, 0:nb, :], d["sink"][p0:p0 + NT, :].rearrange("(b p) r -> p b r", p=bn), ["SINK"])
        base = KV0 + 3072
        SETS = []
        for i in range(2):
            bo = base + 4096 * i
            SETS.append(dict(CKVF=self.view(bo, [KVL], F32), KPEF=self.view(bo + 1024, [32], F32),
                             CKVB=self.view(bo + 1152, [KVL], BF16), KPEB=self.view(bo + 2560, [128], BF16),
                             T1=self.view(bo + 1728, [32], F32), T2=self.view(bo + 1856, [32], F32),
                             SS=self.view(bo + 1984, [2], F32), JUNK=self.view(bo + 2048, [KVL], BF16)))
        for i in range(2):
            kb_ = SETS[i]["KPEB"]
            P.dve(lambda e, kb_=kb_: e.memset(kb_, 0.0), writes=[("KV", i, "KPEB")])
        if sample:
            ckT = self.view(S0 + 65280, [2, 64], BF16)
            kpT = self.view(S0 + 65536, [64], BF16)
            self.SCKT, self.SKPT = ckT, kpT
        for bi, (t0, tn) in enumerate(U["blocks"]):
            si = self.rot("kvset", 2)
            S = SETS[si]
            sk = lambda n: ("KV", si, n)
            b = self.bank()
            for k in range(8):
                self.mm(b, KVL + ROPE, self.XN[:, k, t0:t0 + tn], wkv[:, k, 0:KVL + ROPE], k == 0, k == 7,
                        [("XN", k, (t0 // 512) * 512), wk], m=tn)
            ps = self.PS[b]
            P.act(lambda e, S=S, ps=ps, tn=tn: e.activation(out=S["JUNK"][0:tn, :], in_=ps[0:tn, 0:KVL], func=AF.Square,
                                                            accum_out=S["SS"][0:tn, 0:1]),
                  reads=[("ps", b)], writes=[sk("JUNK"), sk("SS")])
            P.act(lambda e, S=S, tn=tn: e.activation(out=S["SS"][0:tn, 1:2], in_=S["SS"][0:tn, 0:1], func=AF.Sqrt,
                                                     bias=self.VEC[0:tn, VC["eps"]:VC["eps"] + 1], scale=1.0 / KVL),
                  reads=[sk("SS"), "VEC"], writes=[sk("SS1")])
            P.dve(lambda e, S=S, tn=tn: e.reciprocal(out=S["SS"][0:tn, 1:2], in_=S["SS"][0:tn, 1:2]),
                  reads=[sk("SS1")], writes=[sk("SS1")])
            P.dve(lambda e, S=S, ps=ps, tn=tn: e.scalar_tensor_tensor(
                out=S["CKVF"][0:tn, :], in0=ps[0:tn, 0:KVL], scalar=S["SS"][0:tn, 1:2], in1=GKV[0:tn, :],
                op0=ALU.mult, op1=ALU.mult), reads=[("ps", b), sk("SS1"), "GKV"], writes=[sk("CKVF")])
            P.act(lambda e, S=S, tn=tn: e.activation(out=S["CKVB"][0:tn, :], in_=S["CKVF"][0:tn, :], func=AF.Copy),
                  reads=[sk("CKVF")], writes=[sk("CKVB")])
            import os
            kvl = int(os.environ.get("MK_KV", "9"))
            if kvl < 2:
                self.st(("ock", si), o["ckv_p"][j, U["seq"], p0 + t0:p0 + t0 + tn, :], S["CKVF"][0:tn, :], [sk("CKVF")])
                continue
            zpe = ps[0:tn, KVL:KVL + ROPE]
            ck = COSK[0:tn, bi, :]
            sn = SINK[0:tn, bi, :]
            P.dve(lambda e, S=S, zpe=zpe, ck=ck, tn=tn: e.tensor_tensor(out=S["T1"][0:tn, :], in0=zpe, in1=ck, op=ALU.mult),
                  reads=[("ps", b), "COSK"], writes=[sk("T1")])
            P.dve(lambda e, S=S, ps=ps, sn=sn, tn=tn: e.tensor_tensor(
                out=S["T2"][0:tn, 0:16], in0=ps[0:tn, KVL + 16:KVL + 32], in1=sn[:, 0:16], op=ALU.mult),
                reads=[("ps", b), "SINK"], writes=[sk("T2")])
            P.dve(lambda e, S=S, ps=ps, sn=sn, tn=tn: e.tensor_tensor(
                out=S["T2"][0:tn, 16:32], in0=ps[0:tn, KVL:KVL + 16], in1=sn[:, 16:32], op=ALU.mult),
                reads=[("ps", b), "SINK"], writes=[sk("T2")])
            P.dve(lambda e, S=S, tn=tn: e.tensor_tensor(out=S["KPEF"][0:tn, :], in0=S["T1"][0:tn, :], in1=S["T2"][0:tn, :],
                                                        op=ALU.add), reads=[sk("T1"), sk("T2")], writes=[sk("KPEF")])
            P.act(lambda e, S=S, tn=tn: e.activation(out=S["KPEB"][0:tn, 0:32], in_=S["KPEF"][0:tn, :], func=AF.Copy),
                  reads=[sk("KPEF")], writes=[sk("KPEB")])
            if sample:
                oc = o["ckv_s"][j, t0:t0 + tn, :]
                ok = o["kpe_s"][j, t0:t0 + tn, :]
            else:
                oc = o["ckv_p"][j, U["seq"], p0 + t0:p0 + t0 + tn, :]
                ok = o["kpe_p"][j, U["seq"], p0 + t0:p0 + t0 + tn, :]
            self.st(("ock", si), oc, S["CKVF"][0:tn, :], [sk("CKVF")])
            self.st(("okp", si), ok, S["KPEF"][0:tn, :], [sk("KPEF")])
            if kvl < 3:
                continue
            b2 = self.bank()
            pb = self.PS[b2][:, 0:256].bitcast(BF16)
            for cc in range(2):
                P.pe(lambda e, S=S, pb=pb, cc=cc, tn=tn: e.transpose(
                    pb[:, cc * 128:cc * 128 + tn], S["CKVB"][0:tn, cc * 128:(cc + 1) * 128], self.IDB[0:tn, 0:tn]),
                    reads=[sk("CKVB"), "IDB"], writes=[("ps", b2)])
            P.pe(lambda e, S=S, pb=pb, tn=tn: e.transpose(pb[:, 256:256 + tn], S["KPEB"][0:tn, :], self.IDB[0:tn, 0:tn]),
                 reads=[sk("KPEB"), "IDB"], writes=[("ps", b2)])
            if sample:
                SN = self.view(S0 + 65664, [384], BF16)
                self.SNEW = SN
                P.dve(lambda e, SN=SN: e.memset(SN[0:64, 256:384], 1.0), writes=["SNEWp"])
                P.act(lambda e, S=S, SN=SN, tn=tn: e.activation(out=SN[0:tn, 0:256], in_=S["CKVF"][0:tn, :], func=AF.Copy),
                      reads=[sk("CKVF")], writes=["SNEWc"])
                P.act(lambda e, S=S, SN=SN, tn=tn: e.activation(out=SN[0:tn, 256:288], in_=S["KPEF"][0:tn, :], func=AF.Copy),
                      reads=[sk("KPEF"), "SNEWp"], writes=["SNEWk"])
                dst_c = ckT[:, :, t0:t0 + tn]
                dst_k = kpT[0:32, t0:t0 + tn]
            else:
                dst_c = self.CKVT[:, j, :, p0 + t0:p0 + t0 + tn]
                dst_k = self.KPET[0:32, j, p0 + t0:p0 + t0 + tn]
            src_c = pb[:, 0:256].rearrange("p (a b) -> p a b", a=2)[:, :, 0:tn]
            P.dve(lambda e, dst_c=dst_c, src_c=src_c: e.tensor_copy(out=dst_c, in_=src_c),
                  reads=[("ps", b2)], writes=[("CKVT", j, t0)])
            P.dve(lambda e, dst_k=dst_k, pb=pb, tn=tn: e.tensor_copy(out=dst_k, in_=pb[0:32, 256:256 + tn]),
                  reads=[("ps", b2)], writes=[("KPET", j, t0)])

    def qn_phase(self, U, j):
        P, d = self.P, self.d
        S0 = self.S0
        wq, wk = self.load_w8(d["w_in_ab"][j], 0, QL)
        self.QN = self.view(S0 + 66560, [3, 1024], BF16)
        SQ3 = self.view(S0 + 46208, [3, 512], BF16)
        RS = self.view(S0 + 49280, [512], F32)
        for (t0, tn) in U["tiles"]:
            bs = []
            for c in range(3):
                b = self.bank()
                bs.append(b)
                for k in range(8):
                    self.mm(b, tn, wq[:, k, c * 128:(c + 1) * 128], self.XN[:, k, t0:t0 + tn], k == 0, k == 7,
                            [("XN", k, t0), wk])
                ps = self.PS[b][:, 0:tn]
                sq = SQ3[:, c, 0:tn]
                P.act(lambda e, sq=sq, ps=ps: e.activation(out=sq, in_=ps, func=AF.Square), reads=[("ps", b)],
                      writes=[("SQ3", c)])
            b4 = self.bank()
            for c in range(3):
                self.mm(b4, tn, self.ONB, SQ3[:, c, 0:tn], c == 0, c == 2, [("SQ3", c), "ONB"])
            rs = RS[:, 0:tn]
            ps4 = self.PS[b4][:, 0:tn]
            P.act(lambda e, rs=rs, ps4=ps4: e.activation(out=rs, in_=ps4, func=AF.Sqrt, bias=self.vc("eps"), scale=1.0 / QL),
                  reads=[("ps", b4), "VEC"], writes=["RSQ"])
            P.dve(lambda e, rs=rs: e.reciprocal(out=rs, in_=rs), reads=["RSQ"], writes=["RSQ"])
            for c in range(3):
                ps = self.PS[bs[c]][:, 0:tn]
                qo = self.QN[:, c, t0:t0 + tn]
                g = self.vc("g_q", j * 3 + c)
                P.dve(lambda e, qo=qo, ps=ps, g=g, rs=rs: e.scalar_tensor_tensor(
                    out=qo, in0=ps, scalar=g, in1=rs, op0=ALU.mult, op1=ALU.mult),
                    reads=[("ps", bs[c]), "RSQ", "VEC"], writes=[("QN", c, t0)])

    def q_head(self, wq, wqk, h, t0, tn, QG, hl, cq, sq, TQ, qkey):
        P = self.P
        b = self.bank()
        for c in range(3):
            self.mm(b, tn, wq[:, c, h * 128:(h + 1) * 128], self.QN[:, c, t0:t0 + tn], c == 0, c == 2,
                    [("QN", c, t0), wqk])
        ps = self.PS[b]
        qn_ = QG[0:64, hl, 0:tn]
        P.act(lambda e: e.activation(out=qn_, in_=ps[0:64, 0:tn], func=AF.Copy), reads=[("ps", b)], writes=[qkey])
        t1 = TQ[0][64:96, 0:tn]
        t2 = TQ[1][64:96, 0:tn]
        P.dve(lambda e: e.tensor_tensor(out=t1, in0=ps[64:96, 0:tn], in1=cq, op=ALU.mult),
              reads=[("ps", b), "COSQ"], writes=["TQ0"])
        P.dve(lambda e: e.tensor_tensor(out=t2, in0=ps[96:128, 0:tn], in1=sq, op=ALU.mult),
              reads=[("ps", b), "SINQ"], writes=["TQ1"])
        qp = QG[64:96, hl, 0:tn]
        P.dve(lambda e: e.tensor_tensor(out=qp, in0=t1, in1=t2, op=ALU.add), reads=["TQ0", "TQ1"], writes=[qkey])
        return b

    def attn_prompt(self, U, j):
        P, d = self.P, self.d
        S0 = self.S0
        u = U["u"]
        p0 = U["pos0"]
        nkeys = p0 + NTP
        nkb = nkeys // 128
        KG = self.view(S0, [4, 2048], BF16)
        VG = self.view(S0 + 16384, [16, 4, 128], BF16)
        QGs = [self.view(S0 + 32768 + 4096 * i, [4, 512], BF16) for i in range(1)]
        OG = self.view(S0 + 36864, [4, 512], BF16)
        EXPT = [self.view(S0 + 40960 + 1024 * i, [512], BF16) for i in range(3)]
        COSQ = self.view(S0 + 44032, [512], F32)
        SINQ = self.view(S0 + 46080, [512], F32)
        RC = self.view(S0 + 48128, [512], F32)
        TQ = [self.view(S0 + 50176 + 2048 * i, [512], F32) for i in range(2)]
        pf = self.pf.pop("attn", None) or self.pre_attn_prompt(j)
        i0, WQv, wqk, i1, WUK, WUV, wkk = pf
        vg_ones = VG[:, :, :, 64:128]
        P.pool(lambda e: e.memset(vg_ones, 1.0), writes=["VGones"])
        for g in range(2):
            i2, _, wok = self.w8()
            WOA = self.arena_w8_view(i2, [4, 1024])
            self.ld("pool", ("w8", i2), WOA[0:64, :, :],
                    d["w_out_ab"][j][g * 256:(g + 1) * 256, :].rearrange("(h v) n -> v h n", v=64), [wok])
            for hl in range(4):
                h = g * 4 + hl
                for k0 in range(0, nkeys, 512):
                    b = self.bank()
                    for cc in range(2):
                        self.mm(b, 512, WUK[:, cc, h * 96:(h + 1) * 96], self.CKVT[:, j, cc, k0:k0 + 512], cc == 0, False,
                                [wkk, ("CKVT", j, "all")], m=96)
                    self.mm(b, 512, self.PSEL[0:32, :], self.KPET[0:32, j, k0:k0 + 512], False, True,
                            ["PSEL", ("KPET", j, "all")], m=96)
                    kg = KG[0:96, hl, k0:k0 + 512]
                    ps = self.PS[b][0:96, :]
                    eng = self.evac_eng()
                    if eng == "act":
                        P.act(lambda e, kg=kg, ps=ps: e.activation(out=kg, in_=ps, func=AF.Copy), reads=[("ps", b)],
                              writes=[("KG", hl)])
                    else:
                        P.dve(lambda e, kg=kg, ps=ps: e.tensor_copy(out=kg, in_=ps), reads=[("ps", b)], writes=[("KG", hl)])
            for kb in range(nkb):
                b = self.bank()
                for cc in range(2):
                    self.mm(b, 256, self.CKVT[:, j, cc, kb * 128:(kb + 1) * 128], WUV[:, cc, g * 256:(g + 1) * 256],
                            cc == 0, cc == 1, [("W8b", i1), ("CKVT", j, "all")])
                vg = VG[:, kb, :, 0:64]
                ps = self.PS[b][:, 0:256].rearrange("p (a b) -> p a b", a=4)
                eng = self.evac_eng()
                if eng == "act":
                    P.act(lambda e, vg=vg, ps=ps: e.activation(out=vg, in_=ps, func=AF.Copy), reads=[("ps", b)],
                          writes=[("VG", kb)])
                else:
                    P.dve(lambda e, vg=vg, ps=ps: e.tensor_copy(out=vg, in_=ps), reads=[("ps", b)], writes=[("VG", kb)])
            for (t0, tn) in U["tiles"]:
                gq = (p0 + t0) // 512
                self.ld("sp", "cosq", COSQ[64:96, :], d["cosq"][:, p0 + t0:p0 + t0 + 512], ["COSQ"])
                self.ld("sp", "sinq", SINQ[64:96, :], d["sinq"][:, p0 + t0:p0 + t0 + 512], ["SINQ"])
                QG = QGs[0]
                for hl in range(4):
                    h = g * 4 + hl
                    self.q_head(WQv, wqk, h, t0, tn, QG, hl, COSQ[64:96, 0:tn], SINQ[64:96, 0:tn], TQ, ("QG", hl))
                for hl in range(4):
                    bo = self.bank(hold=True)
                    nblk = gq * 4 + 4
                    pend = None
                    for kb in range(nblk):
                        jd = kb - gq * 4
                        c0 = 0 if jd < 0 else jd * 128
                        n = 512 - c0
                        bsc = self.bank()
                        self.mm(bsc, n, KG[0:96, hl, kb * 128:(kb + 1) * 128], QG[0:96, hl, c0:512], True, True,
                                [("KG", hl), ("QG", hl)], c0=c0)
                        ei = self.rot("expt", 3)
                        ex = EXPT[ei]
                        pss = self.PS[bsc][:, c0:512]
                        exv = ex[:, c0:512]
                        P.act(lambda e, exv=exv, pss=pss: e.activation(out=exv, in_=pss, func=AF.Exp, scale=ATTN_SCALE),
                              reads=[("ps", bsc)], writes=[("EXPT", ei)])
                        if jd >= 0:
                            dv = ex[:, c0:c0 + 128]
                            P.dve(lambda e, dv=dv: e.tensor_tensor(out=dv, in0=dv, in1=self.MKB, op=ALU.mult),
                                  reads=[("EXPT", ei), "MKB"], writes=[("EXPT", ei)])
                        if pend is not None:
                            pk, pexv, pei, pc0, pn = pend
                            self.mm(bo, pn, VG[:, pk, hl, :], pexv, pk == 0, False,
                                    [("VG", pk), "VGones", ("EXPT", pei)], c0=pc0)
                        pend = (kb, exv, ei, c0, n)
                    pk, pexv, pei, pc0, pn = pend
                    self.mm(bo, pn, VG[:, pk, hl, :], pexv, pk == 0, True, [("VG", pk), "VGones", ("EXPT", pei)], c0=pc0)
                    pso = self.PS[bo]
                    rc = RC[64:128, :]
                    P.dve(lambda e, rc=rc, pso=pso: e.reciprocal(out=rc, in_=pso[64:128, :]), reads=[("ps", bo)], writes=["RC"])
                    og = OG[0:64, hl, :]
                    P.dve(lambda e, og=og, pso=pso, rc=rc: e.tensor_tensor(out=og, in0=pso[0:64, :], in1=rc, op=ALU.mult),
                          reads=[("ps", bo), "RC"], writes=[("OG", hl)])
                    self.release(bo)
                for m in range(8):
                    b = self.bank()
                    for hl in range(4):
                        self.mm(b, tn, WOA[0:64, hl, m * 128:(m + 1) * 128], OG[0:64, hl, 0:tn], hl == 0, hl == 3,
                                [wok, ("OG", hl)])
                    self.h_add(b, m, t0, tn)

    def pre_attn_prompt(self, j):
        d = self.d
        i0, _, wqk = self.w8()
        WQv = self.arena_w8_view(i0, [3, 1024])
        self.ld("pool", ("w8", i0), WQv, d["wuq"][j].rearrange("(k p) n -> p k n", p=128), [wqk])
        i1, _, wkk = self.w8()
        WUK = self.arena_w8_view(i1, [2, 768])
        WUV = self.arena_w8_view(i1, [2, 512], off=3072)
        self.ld("pool", ("w8", i1), WUK, d["wuk"][j].rearrange("(k p) n -> p k n", p=128), [wkk])
        self.ld("pool", ("w8b", i1), WUV, d["wuv"][j].rearrange("(k p) n -> p k n", p=128), [("W8b", i1)])
        return (i0, WQv, wqk, i1, WUK, WUV, wkk)

    def arena_w8_view(self, i, shape, off=0):
        return self.view(81920 + 8192 * i + off, shape, BF16)

    def h_add(self, b, m, t0, tn):
        hv = self.H[:, m, t0:t0 + tn]
        ps = self.PS[b][:, 0:tn]
        self.P.dve(lambda e: e.tensor_tensor(out=hv, in0=ps, in1=hv, op=ALU.add),
                   reads=[("ps", b)], writes=[("Hm", m, t0)])

    def lru_phase(self, U, j):
        P, d, o = self.P, self.d, self.o
        S0 = self.S0
        NT = U["NT"]
        sample = U["kind"] == "s"
        L0 = S0 + 38912
        LY = self.view(L0, [4, 1024], BF16)
        WG = self.view(L0 + 8192, [8, 128], BF16)
        self.ld("pool", "wg", WG, d["wgate"][j].rearrange("p (a b) -> p a b", a=8), ["WG"])
        o0 = L0 + 10240
        ZX = self.view(o0, [520], F32)
        XC = self.view(o0 + 2080, [512], F32)
        R = self.view(o0 + 4128, [512], F32)
        IG = self.view(o0 + 6176, [512], F32)
        A = self.view(o0 + 8224, [512], F32)
        BX = self.view(o0 + 10272, [512], F32)
        HS = self.view(o0 + 12320, [512], F32)
        XCB = self.view(o0 + 14368, [512], BF16)
        GZ = self.view(o0 + 15392, [512], BF16)
        for c in range(4):
            pfw = self.pf.pop("lru", None) if c == 0 else None
            if pfw is not None:
                (wzx, kzx), (wzg, kzg) = pfw
            else:
                wzx, kzx = self.load_w8(d["w_in_ab"][j], 672 + c * 128, 128)
                wzg, kzg = self.load_w8(d["w_in_ab"][j], 1184 + c * 128, 128)
            jc = j * 4 + c
            cwk = [self.vc("conv_w", j * 16 + k * 4 + c) for k in range(4)]
            cb = self.vc("conv_b", jc)
            brg = self.vc("b_rg", jc)
            big = self.vc("b_ig", jc)
            lc1 = self.LAMC[:, jc:jc + 1]
            lc2 = self.LAMC[:, 8 + jc:9 + jc]
            one = self.vc("one")
            for (t0, tn) in U["tiles"]:
                b = self.bank()
                for k in range(8):
                    self.mm(b, tn, wzx[:, k, 0:128], self.XN[:, k, t0:t0 + tn], k == 0, k == 7, [("XN", k, t0), kzx])
                ps = self.PS[b][:, 0:tn]
                if not sample:
                    zxv = ZX[:, 3:3 + tn]
                    P.act(lambda e, zxv=zxv, ps=ps: e.activation(out=zxv, in_=ps, func=AF.Copy), reads=[("ps", b)], writes=["ZX"])
                    hist = self.HIST[:, jc, :]
                    if U["u"] == 0 and t0 == 0:
                        P.dve(lambda e: e.memset(ZX[:, 0:3], 0.0), writes=["ZX"])
                    else:
                        P.dve(lambda e, hist=hist: e.tensor_copy(out=ZX[:, 0:3], in_=hist), reads=[("HIST", jc)], writes=["ZX"])
                    xs = [ZX[:, k:k + tn] for k in range(4)]
                    xcv = XC[:, 0:tn]
                else:
                    zx3 = ZX[:, 0:112].rearrange("p (b t) -> p b t", t=7)
                    P.act(lambda e, zx3=zx3, ps=ps: e.activation(out=zx3[:, :, 3:7], in_=ps.rearrange("p (b t) -> p b t", t=4),
                                                                 func=AF.Copy), reads=[("ps", b)], writes=["ZX"])
                    sc = self.view(o0 + 16416, [128], F32)[0:48, :]
                    self.ld("sp", "stc", sc, d["st_conv"][j][:, c * 128:(c + 1) * 128], ["SC"])
                    bt = self.bank()
                    pst = self.PS[bt]
                    P.pe(lambda e, pst=pst, sc=sc: e.transpose(pst[:, 0:48], sc, self.IDF[0:48, 0:48]),
                         reads=["SC", "IDF"], writes=[("ps", bt)])
                    P.dve(lambda e, zx3=zx3, pst=pst: e.tensor_copy(out=zx3[:, :, 0:3],
                                                                   in_=pst[:, 0:48].rearrange("p (b t) -> p b t", t=3)),
                          reads=[("ps", bt)], writes=["ZX"])
                    xs = [zx3[:, :, k:k + 4] for k in range(4)]
                    xcv = XC[:, 0:64].rearrange("p (b t) -> p b t", t=4)
                P.dve(lambda e, xcv=xcv, x3=xs[3], w3=cwk[3], cb=cb: e.tensor_scalar(xcv, x3, w3, cb, ALU.mult, ALU.add),
                      reads=["ZX", "VEC"], writes=["XC"])
                for k in (2, 1, 0):
                    P.dve(lambda e, xcv=xcv, xk=xs[k], wk_=cwk[k]: e.scalar_tensor_tensor(
                        out=xcv, in0=xk, scalar=wk_, in1=xcv, op0=ALU.mult, op1=ALU.add), reads=["ZX", "VEC"], writes=["XC"])
                xc2 = XC[:, 0:tn]
                P.act(lambda e, xc2=xc2, tn=tn: e.activation(out=XCB[:, 0:tn], in_=xc2, func=AF.Copy), reads=["XC"], writes=["XCB"])
                if not sample:
                    P.dve(lambda e, hist=hist, tn=tn: e.tensor_copy(out=hist, in_=ZX[:, tn:tn + 3]), reads=["ZX"], writes=[("HIST", jc)])
                    if U["u"] == 1 and t0 + tn == NT:
                        dst = o["conv_p"][j, U["seq"], :, c * 128:(c + 1) * 128].rearrange("t p -> p t")
                        self.P.dma("sp", "oconv", lambda e, s, dst=dst, hist=hist: self._nc_dma(e, s, dst, hist),
                                   reads=[("HIST", jc)], is_out=True)
                else:
                    CS = self.view(o0 + 17440, [48], F32)
                    OS = self.view(o0 + 17632, [128], F32)
                    P.dve(lambda e, zx3=zx3, CS=CS: e.tensor_copy(out=CS.rearrange("p (b t) -> p b t", t=3), in_=zx3[:, :, 4:7]),
                          reads=["ZX"], writes=["CS"])
                    bt2 = self.bank()
                    pst2 = self.PS[bt2]
                    P.pe(lambda e, pst2=pst2, CS=CS: e.transpose(pst2[0:48, 0:128], CS, self.IDF), reads=["CS", "IDF"],
                         writes=[("ps", bt2)])
                    P.dve(lambda e, pst2=pst2, OS=OS: e.tensor_copy(out=OS[0:48, :], in_=pst2[0:48, 0:128]), reads=[("ps", bt2)],
                          writes=["OS"])
                    dst = o["conv_s"][j].rearrange("b t n -> (b t) n")[:, c * 128:(c + 1) * 128]
                    self.P.dma("sp", "oconv", lambda e, s, dst=dst, OS=OS: e.dma_start(out=dst, in_=OS[0:48, :]).then_inc(s, 16),
                               reads=["OS"], is_out=True)
                b2 = self.bank()
                for k in range(8):
                    self.mm(b2, tn, wzg[:, k, 0:128], self.XN[:, k, t0:t0 + tn], k == 0, k == 7, [("XN", k, t0), kzg])
                ps2 = self.PS[b2][:, 0:tn]
                P.act(lambda e, ps2=ps2, tn=tn: e.activation(out=GZ[:, 0:tn], in_=ps2, func=AF.Gelu_apprx_tanh), reads=[("ps", b2)],
                      writes=["GZ"])
                b3 = self.bank()
                self.mm(b3, tn, WG[:, c * 2, :], XCB[:, 0:tn], True, True, ["WG", "XCB"])
                b4 = self.bank()
                self.mm(b4, tn, WG[:, c * 2 + 1, :], XCB[:, 0:tn], True, True, ["WG", "XCB"])
                ps3 = self.PS[b3][:, 0:tn]
                ps4 = self.PS[b4][:, 0:tn]
                rv, iv, av, bv, hv = R[:, 0:tn], IG[:, 0:tn], A[:, 0:tn], BX[:, 0:tn], HS[:, 0:tn]
                P.act(lambda e, rv=rv, ps3=ps3, brg=brg: e.activation(out=rv, in_=ps3, func=AF.Sigmoid, bias=brg),
                      reads=[("ps", b3), "VEC"], writes=["R"])
                P.act(lambda e, iv=iv, ps4=ps4, big=big: e.activation(out=iv, in_=ps4, func=AF.Sigmoid, bias=big),
                      reads=[("ps", b4), "VEC"], writes=["IG"])
                P.act(lambda e, av=av, rv=rv, lc1=lc1: e.activation(out=av, in_=rv, func=AF.Exp, scale=lc1),
                      reads=["R", "LAMC"], writes=["A"])
                P.act(lambda e, rv=rv, lc2=lc2: e.activation(out=rv, in_=rv, func=AF.Exp, scale=lc2),
                      reads=["R", "LAMC"], writes=["R"])
                P.act(lambda e, rv=rv, one=one: e.activation(out=rv, in_=rv, func=AF.Sqrt, bias=one, scale=-1.0),
                      reads=["R", "VEC"], writes=["R"])
                P.dve(lambda e, bv=bv, rv=rv, iv=iv: e.tensor_tensor(out=bv, in0=rv, in1=iv, op=ALU.mult), reads=["R", "IG"],
                      writes=["BX"])
                P.dve(lambda e, bv=bv, xc2=xc2: e.tensor_tensor(out=bv, in0=bv, in1=xc2, op=ALU.mult), reads=["BX", "XC"],
                      writes=["BX"])
                ly = LY[:, c, t0:t0 + tn]
                if not sample:
                    st = self.STL[:, jc:jc + 1]
                    if U["u"] == 0 and t0 == 0:
                        P.dve(lambda e, hv=hv, av=av, bv=bv: e.tensor_tensor_scan(
                            out=hv, data0=av, data1=bv, initial=0.0, op0=ALU.mult, op1=ALU.add), reads=["A", "BX"], writes=["HS"])
                    else:
                        P.dve(lambda e, hv=hv, av=av, bv=bv, st=st: e.tensor_tensor_scan(
                            out=hv, data0=av, data1=bv, initial=st, op0=ALU.mult, op1=ALU.add),
                            reads=["A", "BX", ("STL", jc)], writes=["HS"])
                    P.dve(lambda e, st=st, hv=hv, tn=tn: e.tensor_copy(out=st, in_=hv[:, tn - 1:tn]), reads=["HS"],
                          writes=[("STL", jc)])
                    if U["u"] == 1 and t0 + tn == NT:
                        dst = o["lru_p"][j, U["seq"], c * 128:(c + 1) * 128].rearrange("(p a) -> p a", a=1)
                        self.P.dma("sp", "olru", lambda e, s, dst=dst, st=st: self._nc_dma(e, s, dst, st),
                                   reads=[("STL", jc)], is_out=True)
                else:
                    sl = self.view(o0 + 16928, [128], F32)[0:16, :]
                    self.ld("sp", "stl", sl, d["st_lru"][j][:, c * 128:(c + 1) * 128], ["SL"])
                    bt = self.bank()
                    pst = self.PS[bt]
                    P.pe(lambda e, pst=pst, sl=sl: e.transpose(pst[:, 0:16], sl, self.IDF[0:16, 0:16]), reads=["SL", "IDF"],
                         writes=[("ps", bt)])
                    h3 = HS[:, 0:64].rearrange("p (b t) -> p b t", t=4)
                    a3 = A[:, 0:64].rearrange("p (b t) -> p b t", t=4)
                    b3v = BX[:, 0:64].rearrange("p (b t) -> p b t", t=4)
                    for t in range(4):
                        prev = pst[:, 0:16] if t == 0 else h3[:, :, t - 1]
                        rk = [("ps", bt)] if t == 0 else ["HS"]
                        P.dve(lambda e, t=t, prev=prev: e.tensor_tensor(out=h3[:, :, t], in0=prev, in1=a3[:, :, t], op=ALU.mult),
                              reads=rk + ["A"], writes=["HS"])
                        P.dve(lambda e, t=t: e.tensor_tensor(out=h3[:, :, t], in0=h3[:, :, t], in1=b3v[:, :, t], op=ALU.add),
                              reads=["HS", "BX"], writes=["HS"])
                    CL = self.view(o0 + 18144, [16], F32)
                    OL = self.view(o0 + 18208, [128], F32)
                    P.dve(lambda e, CL=CL, h3=h3: e.tensor_copy(out=CL, in_=h3[:, :, 3]), reads=["HS"], writes=["CL"])
                    bt3 = self.bank()
                    pst3 = self.PS[bt3]
                    P.pe(lambda e, pst3=pst3, CL=CL: e.transpose(pst3[0:16, 0:128], CL, self.IDF), reads=["CL", "IDF"],
                         writes=[("ps", bt3)])
                    P.dve(lambda e, pst3=pst3, OL=OL: e.tensor_copy(out=OL[0:16, :], in_=pst3[0:16, 0:128]), reads=[("ps", bt3)],
                          writes=["OL"])
                    dst = o["lru_s"][j, :, c * 128:(c + 1) * 128]
                    self.P.dma("sp", "olru", lambda e, s, dst=dst, OL=OL: e.dma_start(out=dst, in_=OL[0:16, :]).then_inc(s, 16),
                               reads=["OL"], is_out=True)
                P.dve(lambda e, ly=ly, hv=hv, tn=tn: e.tensor_tensor(out=ly, in0=hv, in1=GZ[:, 0:tn], op=ALU.mult), reads=["HS", "GZ"],
                      writes=[("LY", c)])
        for (t0, tn) in U["tiles"]:
            pass
        for mh in range(2):
            i0, _, k0 = self.w8()
            W = self.arena_w8_view(i0, [4, 512])
            self.ld("pool", ("w8", i0), W, d["w_out_ab"][j][512:1024, mh * 512:(mh + 1) * 512].rearrange("(k p) n -> p k n", p=128),
                    [k0])
            for (t0, tn) in U["tiles"]:
                for mm_ in range(4):
                    m = mh * 4 + mm_
                    b = self.bank()
                    for c in range(4):
                        self.mm(b, tn, W[:, c, mm_ * 128:(mm_ + 1) * 128], LY[:, c, t0:t0 + tn], c == 0, c == 3, [k0, ("LY", c)])
                    self.h_add(b, m, t0, tn)

    def _nc_dma(self, e, s, dst, src):
        with self.nc.allow_non_contiguous_dma(reason="tiny strided state rows"):
            e.dma_start(out=dst, in_=src).then_inc(s, 16)

    def gmlp_phase(self, U, j):
        P, d, o = self.P, self.d, self.o
        S0 = self.S0
        NT = U["NT"]
        sample = U["kind"] == "s"
        UU = self.view(S0, [8, 1024], BF16)
        VT = [self.view(S0 + 16384 + 2048 * i, [1024], BF16) for i in range(2)]
        VN = [self.view(S0 + 20480 + 4096 * i, [1024], F32) for i in range(2)]
        LNG = self.view(S0 + 28672, [1024], F32)
        LNB = self.view(S0 + 32768, [1024], F32)
        BSB = self.view(S0 + 36864, [8, 128], F32)
        WS = self.view(S0 + 40960, [8, 128], BF16)
        ST = self.view(S0 + 43008, [8], F32)
        self.ld("sp", "lng", LNG, d["lng"][j], ["LNG"])
        self.ld("sp", "lnb", LNB, d["lnb"][j], ["LNB"])
        L = 64 if sample else 128
        if sample:
            self.ld("sp", "bsb", BSB[:, :, 0:64], d["bsbs"][j].rearrange("p (g t) -> p g t", g=8), ["BSB"])
            self.ld("pool", "ws", WS[0:64, :, 0:64], d["wsTs"][j].rearrange("p (g t) -> p g t", g=8), ["WS"])
            mk = self.MKS[0:64, :]
        else:
            self.ld("sp", "bsb", BSB, d["bsb"][j].rearrange("p (g t) -> p g t", g=8), ["BSB"])
            self.ld("pool", "ws", WS, d["wsT"][j].rearrange("p (g t) -> p g t", g=8), ["WS"])
            mk = self.MKB
        for g in range(8):
            wv = WS[0:L, g, 0:L]
            P.dve(lambda e, wv=wv: e.tensor_tensor(out=wv, in0=wv, in1=mk, op=ALU.mult), reads=["WS", "MKB", "MKS"], writes=["WS"])
        for mh in range(2):
            pfw = self.pf.pop("gmlp", None) if mh == 0 else None
            wu, ku = pfw if pfw is not None else self.load_w8(d["w_in_c"][j], mh * 512, 512)
            for (t0, tn) in U["tiles"]:
                for mm_ in range(4):
                    m = mh * 4 + mm_
                    b = self.bank()
                    for k in range(8):
                        self.mm(b, tn, wu[:, k, mm_ * 128:(mm_ + 1) * 128], self.XN[:, k, t0:t0 + tn], k == 0, k == 7,
                                [("XN", k, t0), ku])
                    ps = self.PS[b][:, 0:tn]
                    uo = UU[:, m, t0:t0 + tn]
                    P.act(lambda e, uo=uo, ps=ps: e.activation(out=uo, in_=ps, func=AF.Gelu_apprx_tanh), reads=[("ps", b)],
                          writes=[("UU", m, t0)])
        wv0, kv0 = self.load_w8(d["w_in_c"][j], 1024, 512)
        wv1, kv1 = self.load_w8(d["w_in_c"][j], 1536, 512)
        for bi, (t0, tn) in enumerate(U["blocks"]):
            vi = self.rot("vn", 2)
            vn = VN[vi]
            vt = VT[vi]
            for hh, (wv, kv) in enumerate(((wv0, kv0), (wv1, kv1))):
                b = self.bank()
                for k in range(8):
                    self.mm(b, 512, self.XN[:, k, t0:t0 + tn], wv[:, k, :], k == 0, k == 7, [("XN", k, (t0 // 512) * 512), kv], m=tn)
                ps = self.PS[b][0:tn, :]
                P.act(lambda e, vn=vn, ps=ps, hh=hh, tn=tn: e.activation(out=vn[0:tn, hh * 512:(hh + 1) * 512], in_=ps,
                                                                         func=AF.Gelu_apprx_tanh),
                      reads=[("ps", b)], writes=[("VN", vi)])
            st = ST
            vnv = vn[0:tn, :]
            P.dve(lambda e, vnv=vnv, tn=tn, vt=vt: e.tensor_scalar(out=vt[0:tn, :], in0=vnv, scalar1=1.0, scalar2=0.0, op0=ALU.mult,
                                                            op1=ALU.add, accum_out=st[0:tn, 0:1]),
                  reads=[("VN", vi)], writes=[("VT", vi), "ST0"])
            P.dve(lambda e, tn=tn: e.tensor_scalar(st[0:tn, 1:2], st[0:tn, 0:1], -1.0 / 1024, None, ALU.mult), reads=["ST0"],
                  writes=["ST1"])
            P.dve(lambda e, vnv=vnv, tn=tn: e.tensor_scalar(vnv, vnv, st[0:tn, 1:2], None, ALU.add), reads=[("VN", vi), "ST1"],
                  writes=[("VN", vi)])
            P.act(lambda e, vnv=vnv, tn=tn, vt=vt: e.activation(out=vt[0:tn, :], in_=vnv, func=AF.Square, accum_out=st[0:tn, 2:3]),
                  reads=[("VN", vi)], writes=[("VT", vi), "ST2"])
            P.act(lambda e, tn=tn: e.activation(out=st[0:tn, 3:4], in_=st[0:tn, 2:3], func=AF.Sqrt,
                                                bias=self.VEC[0:tn, VC["eps"]:VC["eps"] + 1], scale=1.0 / 1024),
                  reads=["ST2", "VEC"], writes=["ST3"])
            P.dve(lambda e, tn=tn: e.reciprocal(out=st[0:tn, 3:4], in_=st[0:tn, 3:4]), reads=["ST3"], writes=["ST3"])
            P.dve(lambda e, vnv=vnv, tn=tn: e.scalar_tensor_tensor(out=vnv, in0=vnv, scalar=st[0:tn, 3:4], in1=LNG[0:tn, :],
                                                                   op0=ALU.mult, op1=ALU.mult),
                  reads=[("VN", vi), "ST3", "LNG"], writes=[("VN", vi)])
            P.dve(lambda e, vnv=vnv, tn=tn: e.tensor_tensor(out=vnv, in0=vnv, in1=LNB[0:tn, :], op=ALU.add),
                  reads=[("VN", vi), "LNB"], writes=[("VN", vi)])
            P.act(lambda e, vnv=vnv, tn=tn, vt=vt: e.activation(out=vt[0:tn, :], in_=vnv, func=AF.Copy), reads=[("VN", vi)],
                  writes=[("VT", vi)])
            if sample:
                self.st(("ovs", vi), o["v_s"][j, t0:t0 + tn, :], vnv, [("VN", vi)])
            for g in range(8):
                b = self.bank()
                self.mm(b, tn, vt[0:tn, g * 128:(g + 1) * 128], WS[0:tn, g, 0:tn], True, True, [("VT", vi), "WS"])
                ps = self.PS[b][:, 0:tn]
                uo = UU[:, g, t0:t0 + tn]
                ti = self.rot("sp_t", 2)
                tmp = self.view(S0 + 43520 + 512 * ti, [128], F32)[:, 0:tn]
                P.dve(lambda e, tmp=tmp, ps=ps, g=g, tn=tn: e.tensor_tensor(out=tmp, in0=ps, in1=BSB[:, g, 0:tn], op=ALU.add),
                      reads=[("ps", b), "BSB"], writes=[("SPT", ti)])
                P.dve(lambda e, uo=uo, tmp=tmp: e.tensor_tensor(out=uo, in0=uo, in1=tmp, op=ALU.mult),
                      reads=[("SPT", ti), ("UU", g, (t0 // 512) * 512)], writes=[("UU", g, (t0 // 512) * 512)])
        for mh in range(2):
            wo, ko = self.load_w8(d["w_out_c"][j], mh * 512, 512)
            for (t0, tn) in U["tiles"]:
                for mm_ in range(4):
                    m = mh * 4 + mm_
                    b = self.bank()
                    for k in range(8):
                        self.mm(b, tn, wo[:, k, mm_ * 128:(mm_ + 1) * 128], UU[:, k, t0:t0 + tn], k == 0, k == 7,
                                [("UU", k, t0), ko])
                    self.h_add(b, m, t0, tn)

    def ffn_phase(self, U, i):
        P, d = self.P, self.d
        S0 = self.S0
        HID = [self.view(S0 + 12288 + 4096 * q, [4, 512], BF16) for q in range(2)]
        SG = [self.view(S0 + 20480 + 2048 * q, [512], F32) for q in range(2)]
        blocks = [(c0, min(512, FFN - c0)) for c0 in range(0, FFN, 512)]
        nblk = len(blocks)

        def issue(bk):
            c0, n = blocks[bk]
            ia, _, ka = self.w8()
            WG_ = self.arena_w8_view(ia, [8, 512])
            self.ld("pool", ("w8", ia), WG_[:, :, 0:n], d["w_gate"][i][:, c0:c0 + n].rearrange("(k p) n -> p k n", p=128), [ka])
            ib, _, kb_ = self.w8()
            WU_ = self.arena_w8_view(ib, [8, 512])
            self.ld("pool", ("w8", ib), WU_[:, :, 0:n], d["w_up"][i][:, c0:c0 + n].rearrange("(k p) n -> p k n", p=128), [kb_])
            return [WG_, ka, WU_, kb_, None, n // 128]

        def issue_d(bk):
            c0, n = blocks[bk]
            wd = []
            for r0 in range(0, n, 256):
                i1, W4, k1 = self.w4()
                self.ld("pool", ("w4", i1), W4, d["w_down"][i][c0 + r0:c0 + r0 + 256, :].rearrange("(k p) n -> p k n", p=128), [k1])
                wd.append((W4, k1))
            wts[bk][4] = wd

        steps = [(bk, t0, tn) for bk in range(nblk) for (t0, tn) in U["tiles"]]
        wts = {}
        wts[0] = issue(0)
        issue_d(0)

        def gate_up(bk, t0, tn):
            WG_, ka, WU_, kb_, wd, nch = wts[bk]
            hi = self.rot("hid", 2)
            hid = HID[hi]
            for hc in range(nch):
                bg = self.bank()
                for k in range(8):
                    self.mm(bg, tn, WG_[:, k, hc * 128:(hc + 1) * 128], self.XN[:, k, t0:t0 + tn], k == 0, k == 7, [("XN", k, t0), ka])
                bu = self.bank()
                for k in range(8):
                    self.mm(bu, tn, WU_[:, k, hc * 128:(hc + 1) * 128], self.XN[:, k, t0:t0 + tn], k == 0, k == 7, [("XN", k, t0), kb_])
                si = self.rot("sg", 2)
                sg = SG[si][:, 0:tn]
                psg = self.PS[bg][:, 0:tn]
                psu = self.PS[bu][:, 0:tn]
                P.act(lambda e, sg=sg, psg=psg: e.activation(out=sg, in_=psg, func=AF.Silu), reads=[("ps", bg)], writes=[("SG", si)])
                ho = hid[:, hc, 0:tn]
                P.dve(lambda e, ho=ho, psu=psu, sg=sg: e.tensor_tensor(out=ho, in0=psu, in1=sg, op=ALU.mult),
                      reads=[("ps", bu), ("SG", si)], writes=[("HID", hi, hc)])
            return hi, hid

        def down(bk, t0, tn, hi, hid):
            WG_, ka, WU_, kb_, wd, nch = wts[bk]
            for m in range(8):
                b = self.bank()
                for hc in range(nch):
                    W4, k1 = wd[hc // 2]
                    self.mm(b, tn, W4[:, hc % 2, m * 128:(m + 1) * 128], hid[:, hc, 0:tn], hc == 0, hc == nch - 1, [k1, ("HID", hi, hc)])
                self.h_add(b, m, t0, tn)

        prev = None
        for si_, (bk, t0, tn) in enumerate(steps):
            if t0 == 0 and bk + 1 < nblk:
                wts[bk + 1] = issue(bk + 1)
            cur = gate_up(bk, t0, tn)
            import os
            if os.environ.get("MK_NOPIPE", "0") == "1":
                down(bk, t0, tn, *cur)
                continue
            if prev is not None:
                down(*prev)
            if t0 == 0 and bk + 1 < nblk:
                issue_d(bk + 1)
            prev = (bk, t0, tn) + cur
        if prev is not None:
            down(*prev)

    def peg_phase(self, U, i):
        P, d = self.P, self.d
        S0 = self.S0
        NT = U["NT"]
        PT = self.view(S0 + 24576, [2, 1024], BF16)
        PTOK = [self.view(S0 + 28672 + 512 * q, [256], BF16) for q in range(2)]
        GATE = [self.view(S0 + 29696 + 2048 * q, [512], F32) for q in range(2)]
        for (t0, tn) in U["blocks"]:
            pi = self.rot("ptok", 2)
            pt = PTOK[pi]
            if U["kind"] == "p":
                src = d["pp"][i, U["seq"], U["pos0"] + t0:U["pos0"] + t0 + tn, :]
            else:
                src = d["pps"][i, t0:t0 + tn, :]
            self.ld("pool", ("ptok", pi), pt[0:tn, :], src, [("PTOK", pi)])
            b = self.bank()
            pb = self.PS[b][:, 0:256].bitcast(BF16)
            for cc in range(2):
                P.pe(lambda e, pb=pb, pt=pt, cc=cc, tn=tn: e.transpose(pb[:, cc * 128:cc * 128 + tn], pt[0:tn, cc * 128:(cc + 1) * 128],
                                                                      self.IDB[0:tn, 0:tn]),
                     reads=[("PTOK", pi), "IDB"], writes=[("ps", b)])
            dst = PT[:, :, t0:t0 + tn]
            srcv = pb[:, 0:256].rearrange("p (a b) -> p a b", a=2)[:, :, 0:tn]
            P.dve(lambda e, dst=dst, srcv=srcv: e.tensor_copy(out=dst, in_=srcv), reads=[("ps", b)], writes=[("PT", t0)])
        i1, W4, k1 = self.w4()
        self.ld("pool", ("w4", i1), W4, d["w_pe"][i].rearrange("(k p) n -> p k n", p=128), [k1])
        for mh in range(2):
            wg, kg = self.load_w8(d["w_pg"][i], mh * 512, 512)
            for (t0, tn) in U["tiles"]:
                for mm_ in range(4):
                    m = mh * 4 + mm_
                    b = self.bank()
                    for k in range(8):
                        self.mm(b, tn, wg[:, k, mm_ * 128:(mm_ + 1) * 128], self.XN[:, k, t0:t0 + tn], k == 0, k == 7, [("XN", k, t0), kg])
                    b2 = self.bank()
                    for k in range(2):
                        self.mm(b2, tn, W4[:, k, m * 128:(m + 1) * 128], PT[:, k, t0:t0 + tn], k == 0, k == 1,
                                [k1] + [("PT", tb) for tb in range(t0, t0 + tn, 128)])
                    gi = self.rot("gate", 2)
                    gt = GATE[gi][:, 0:tn]
                    ps = self.PS[b][:, 0:tn]
                    ps2 = self.PS[b2][:, 0:tn]
                    P.act(lambda e, gt=gt, ps=ps: e.activation(out=gt, in_=ps, func=AF.Sigmoid), reads=[("ps", b)], writes=[("GATE", gi)])
                    P.dve(lambda e, gt=gt, ps2=ps2: e.tensor_tensor(out=gt, in0=ps2, in1=gt, op=ALU.mult), reads=[("ps", b2), ("GATE", gi)],
                          writes=[("GATE", gi)])
                    hv = self.H[:, m, t0:t0 + tn]
                    P.dve(lambda e, hv=hv, gt=gt: e.tensor_tensor(out=hv, in0=hv, in1=gt, op=ALU.add), reads=[("GATE", gi)],
                          writes=[("Hm", m, t0)])

    def final_phase(self, U):
        P, d, o = self.P, self.d, self.o
        S0 = self.S0
        XT = [self.view(S0 + 51328 + 4096 * i, [1024], F32) for i in range(2)]
        GF = self.view(S0 + 59520, [1024], F32)
        JK = self.view(S0 + 63616, [1024], BF16)
        SS = self.view(S0 + 65664, [4], F32)
        self.ld("sp", "gfin", GF, d["gfin"], ["GF"])
        for (t0, tn) in U["blocks"]:
            xi = self.rot("xt", 2)
            xt = XT[xi]
            for half in range(2):
                b = self.bank()
                ps = self.PS[b]
                for q in range(4):
                    m = half * 4 + q
                    P.pe(lambda e, ps=ps, q=q, m=m, t0=t0, tn=tn: e.transpose(ps[0:tn, q * 128:(q + 1) * 128], self.H[:, m, t0:t0 + tn],
                                                                            self.IDF),
                         reads=["IDF", ("Hm", m, (t0 // 512) * 512)], writes=[("ps", b)])
                xo = xt[0:tn, half * 512:(half + 1) * 512]
                pv = ps[0:tn, :]
                P.act(lambda e, xo=xo, pv=pv: e.activation(out=xo, in_=pv, func=AF.Copy), reads=[("ps", b)], writes=[("XT", xi)])
            xv = xt[0:tn, :]
            P.act(lambda e, xv=xv, tn=tn: e.activation(out=JK[0:tn, :], in_=xv, func=AF.Square, accum_out=SS[0:tn, 0:1]),
                  reads=[("XT", xi)], writes=["JK", "SSF"])
            P.act(lambda e, tn=tn: e.activation(out=SS[0:tn, 1:2], in_=SS[0:tn, 0:1], func=AF.Sqrt,
                                                bias=self.VEC[0:tn, VC["eps"]:VC["eps"] + 1], scale=1.0 / D),
                  reads=["SSF", "VEC"], writes=["SSF1"])
            P.dve(lambda e, tn=tn: e.reciprocal(out=SS[0:tn, 1:2], in_=SS[0:tn, 1:2]), reads=["SSF1"], writes=["SSF1"])
            P.dve(lambda e, xv=xv, tn=tn: e.scalar_tensor_tensor(out=xv, in0=xv, scalar=SS[0:tn, 1:2], in1=GF[0:tn, :],
                                                                 op0=ALU.mult, op1=ALU.mult),
                  reads=[("XT", xi), "SSF1", "GF"], writes=[("XT", xi)])
            if U["kind"] == "p":
                dst = o["y_p"][U["seq"], U["pos0"] + t0:U["pos0"] + t0 + tn, :]
            else:
                dst = o["y_s"][t0:t0 + tn, :]
            self.st(("oy", xi), dst, xv, [("XT", xi)])

    def attn_sample(self, U, j):
        P, d = self.P, self.d
        S0 = self.S0
        XT = self.view(S0, [3, 2048], BF16)
        G32C = [self.view(S0 + 12288 + 16384 * i, [16, 256], F32) for i in range(2)]
        G32K = [self.view(S0 + 45056 + 2048 * i, [16, 32], F32) for i in range(2)]
        XB = self.view(S0 + 49152, [16, 384], BF16)
        ES = [self.view(S0 + 61440 + 1024 * i, [512], BF16) for i in range(2)]
        EN = self.view(S0 + 63488, [512], BF16)
        OLB = self.view(S0 + 64512, [256], BF16)
        RCP = self.view(S0 + 65024, [2], F32)
        w4a, w4b, w4c = 114688, 114688 + 4096, 114688 + 8192
        QLT = self.view(w4a, [2, 16, 32], BF16)
        QPT = self.view(w4a + 2048, [16, 32], BF16)
        QG = self.view(w4a + 3072, [8, 64], BF16)
        OLT = self.view(w4b, [2, 8, 64], BF16)
        OG = self.view(w4b + 2048, [8, 64], BF16)
        MKN = self.view(w4b + 3072, [512], BF16)
        IDX = self.view(w4c, [64], I32)
        PTB = self.view(w4c + 256, [64], I32)
        TF = self.view(w4c + 512, [64], F32)
        COSQ = self.view(w4c + 768, [64], F32)
        SINQ = self.view(w4c + 1024, [64], F32)
        TQ = [self.view(w4c + 1280 + 256 * i, [64], F32) for i in range(2)]
        i0, _, wqk = self.w8()
        WQv = self.arena_w8_view(i0, [3, 1024])
        self.ld("pool", ("w8", i0), WQv, d["wuq"][j].rearrange("(k p) n -> p k n", p=128), [wqk])
        i1, _, wkk = self.w8()
        WUKN = self.arena_w8_view(i1, [8, 256])
        WUVH = self.arena_w8_view(i1, [2, 512], off=4096)
        self.ld("pool", ("w8", i1), WUKN[0:64, :, :], d["wukn"][j].rearrange("n (h c) -> n h c", h=8), [wkk])
        self.ld("pool", ("w8b", i1), WUVH, d["wuvh"][j].rearrange("(k p) n -> p k n", p=128), [("W8b", i1)])
        WOAs = []
        for g in range(2):
            i2, _, wok = self.w8()
            WOA = self.arena_w8_view(i2, [4, 1024])
            self.ld("pool", ("w8", i2), WOA[0:64, :, :],
                    d["w_out_ab"][j][g * 256:(g + 1) * 256, :].rearrange("(h v) n -> v h n", v=64), [wok])
            WOAs.append((WOA, wok))
        self.ld("pool", "mkn", MKN[0:64, :], d["maskn"], ["MKN"])
        self.ld("sp", "cosq", COSQ[64:96, :], d["cosq"][:, SEQ:SEQ + 64], ["COSQ"])
        self.ld("sp", "sinq", SINQ[64:96, :], d["sinq"][:, SEQ:SEQ + 64], ["SINQ"])
        self.ld("sp", "ptb", PTB, d["ptab"], ["PTB"])
        P.dve(lambda e: e.tensor_copy(out=TF, in_=PTB), reads=["PTB"], writes=["TF"])
        pm8 = self.vc("pm8")
        nrows = d["cache_ckv"].shape[1]
        P.dve(lambda e: e.tensor_scalar(TF, TF, 8.0, pm8, ALU.mult, ALU.add), reads=["TF", "VEC"], writes=["TF"])
        P.dve(lambda e: e.tensor_scalar(IDX, TF, float(j * nrows), None, ALU.add), reads=["TF"], writes=["IDX"])
        P.dve(lambda e: e.memset(XB, 1.0), writes=["XB"])
        for h in range(8):
            self.q_head(WQv, wqk, h, 0, 64, QG, h, COSQ[64:96, :], SINQ[64:96, :], TQ, ("QGs", h))
        for h in range(8):
            for cc in range(2):
                b = self.bank()
                self.mm(b, 64, WUKN[0:64, h, cc * 128:(cc + 1) * 128], QG[0:64, h, :], True, True, [wkk, ("QGs", h)])
                dst = QLT[:, cc, :, :].rearrange("p b (t h) -> p b t h", h=8)[:, :, :, h]
                src = self.PS[b][:, 0:64].rearrange("p (b t) -> p b t", t=4)
                P.dve(lambda e, dst=dst, src=src: e.tensor_copy(out=dst, in_=src), reads=[("ps", b)], writes=["QLT"])
            dstp = QPT[0:32, :, :].rearrange("p b (t h) -> p b t h", h=8)[:, :, :, h]
            srcp = QG[64:96, h, :].rearrange("p (b t) -> p b t", t=4)
            P.act(lambda e, dstp=dstp, srcp=srcp: e.activation(out=dstp, in_=srcp, func=AF.Copy), reads=[("QGs", h)], writes=["QPT"])
        bn_ = self.bank()
        for b_ in range(NSB):
            for cc in range(2):
                self.mm(bn_, 32, self.SCKT[:, cc, 0:64], QLT[:, cc, b_, :], cc == 0, False, ["QLT"], m=64, c0=b_ * 32)
            self.mm(bn_, 32, self.SKPT[0:32, 0:64], QPT[0:32, b_, :], False, True, ["QPT"], m=64, c0=b_ * 32)
        psn = self.PS[bn_][0:64, :]
        P.act(lambda e: e.activation(out=EN[0:64, :], in_=psn, func=AF.Exp, scale=ATTN_SCALE), reads=[("ps", bn_)], writes=["EN"])
        P.dve(lambda e: e.tensor_tensor(out=EN[0:64, :], in0=EN[0:64, :], in1=MKN[0:64, :], op=ALU.mult), reads=["EN", "MKN"],
              writes=["EN"])
        ckv_rows = d["cache_ckv"].rearrange("l r c -> (l r) c")
        kpe_rows = d["cache_kpe"].rearrange("l r c -> (l r) c")
        for b_ in range(NSB):
            bo = self.bank(hold=True)
            pso = self.PS[bo]
            first = True
            for i in range(4):
                col = b_ * 4 + i
                gi = self.rot("g32", 2)
                gc, gk = G32C[gi], G32K[gi]
                idx = IDX[:, col:col + 1]
                P.dma("pool", ("gc", gi), lambda e, s, gc=gc, idx=idx: e.indirect_dma_start(
                    out=gc.rearrange("p a b -> p (a b)"), out_offset=None, in_=ckv_rows,
                    in_offset=bass.IndirectOffsetOnAxis(ap=idx, axis=0)).then_inc(s, 16),
                    reads=["IDX"], writes=[("G32C", gi)])
                P.dma("pool", ("gk", gi), lambda e, s, gk=gk, idx=idx: e.indirect_dma_start(
                    out=gk.rearrange("p a b -> p (a b)"), out_offset=None, in_=kpe_rows,
                    in_offset=bass.IndirectOffsetOnAxis(ap=idx, axis=0)).then_inc(s, 16),
                    reads=["IDX"], writes=[("G32K", gi)])
                P.act(lambda e, gc=gc: e.activation(out=XB[:, :, 0:256], in_=gc, func=AF.Copy), reads=[("G32C", gi)], writes=["XB"])
                P.dve(lambda e, gk=gk: e.tensor_copy(out=XB[:, :, 256:288], in_=gk), reads=[("G32K", gi)], writes=["XB"])
                for r2 in range(8):
                    bt = self.bank()
                    pb = self.PS[bt][:, 0:384].bitcast(BF16)
                    for rr in range(2):
                        r = r2 * 2 + rr
                        for w in range(3):
                            P.pe(lambda e, pb=pb, r=r, rr=rr, w=w: e.transpose(
                                pb[:, (rr * 3 + w) * 128:(rr * 3 + w + 1) * 128], XB[:, r, w * 128:(w + 1) * 128], self.IDB),
                                reads=["XB", "IDB"], writes=[("ps", bt)])
                    dst = XT[:, :, r2 * 256:(r2 + 1) * 256].rearrange("p w (r q) -> p r w q", r=2)
                    src = pb.rearrange("p (r w q) -> p r w q", r=2, w=3)
                    P.dve(lambda e, dst=dst, src=src: e.tensor_copy(out=dst, in_=src), reads=[("ps", bt)], writes=["XT"])
                bs_ = self.bank()
                for r in range(16):
                    for cc in range(2):
                        self.mm(bs_, 32, XT[:, cc, r * 128:(r + 1) * 128], QLT[:, cc, b_, :], cc == 0, False, ["XT", "QLT"], c0=r * 32)
                    self.mm(bs_, 32, XT[0:32, 2, r * 128:(r + 1) * 128], QPT[0:32, b_, :], False, True, ["XT", "QPT"], c0=r * 32)
                ei = self.rot("es", 2)
                es_ = ES[ei]
                pss = self.PS[bs_][:, :]
                P.act(lambda e, es_=es_, pss=pss: e.activation(out=es_, in_=pss, func=AF.Exp, scale=ATTN_SCALE), reads=[("ps", bs_)],
                      writes=[("ES", ei)])
                for r in range(16):
                    self.mm(bo, 289, es_[:, r * 32:(r + 1) * 32], XB[:, r, 0:289], first, False, [("ES", ei), "XB"], m=32)
                    first = False
            self.mm(bo, 289, EN[0:64, b_ * 32:(b_ + 1) * 32], self.SNEW[0:64, 0:289], False, True,
                    ["EN", "SNEWc", "SNEWk", "SNEWp"], m=32)
            P.dve(lambda e, pso=pso: e.reciprocal(out=RCP[0:32, 0:1], in_=pso[0:32, 288:289]), reads=[("ps", bo)], writes=["RCP"])
            P.dve(lambda e, pso=pso: e.tensor_scalar(OLB[0:32, :], pso[0:32, 0:256], RCP[0:32, 0:1], None, ALU.mult),
                  reads=[("ps", bo), "RCP"], writes=["OLB"])
            self.release(bo)
            bt = self.bank()
            pb = self.PS[bt][:, 0:64].bitcast(BF16)
            for cc in range(2):
                P.pe(lambda e, pb=pb, cc=cc: e.transpose(pb[:, cc * 32:(cc + 1) * 32], OLB[0:32, cc * 128:(cc + 1) * 128],
                                                         self.IDB[0:32, 0:32]), reads=["OLB", "IDB"], writes=[("ps", bt)])
            dst = OLT[:, :, :, b_ * 4:(b_ + 1) * 4]
            src = pb[:, 0:64].rearrange("p (c t h) -> p c h t", c=2, t=4)
            P.dve(lambda e, dst=dst, src=src: e.tensor_copy(out=dst, in_=src), reads=[("ps", bt)], writes=["OLT"])
        for h in range(8):
            b = self.bank()
            for cc in range(2):
                self.mm(b, 64, WUVH[:, cc, h * 64:(h + 1) * 64], OLT[:, cc, h, :], cc == 0, cc == 1, [("W8b", i1), "OLT"], m=64)
            ps = self.PS[b][0:64, 0:64]
            og = OG[0:64, h, :]
            P.act(lambda e, og=og, ps=ps: e.activation(out=og, in_=ps, func=AF.Copy), reads=[("ps", b)], writes=[("OGs", h)])
        for m in range(8):
            b = self.bank()
            for h in range(8):
                WOA, wok = WOAs[h // 4]
                self.mm(b, 64, WOA[0:64, h % 4, m * 128:(m + 1) * 128], OG[0:64, h, :], h == 0, h == 7, [wok, ("OGs", h)])
            self.h_add(b, m, 0, 64)


def _host_consts():
    c = {}
    c["ident"] = np.eye(128, dtype=np.float32)
    c["ones"] = np.ones((128, 128), np.float32)
    k = np.arange(128)
    c["maskc"] = (k[:, None] <= k[None, :]).astype(np.float32)
    s = np.arange(64)
    c["masks"] = ((s[:, None] // 4 == s[None, :] // 4) & (s[:, None] <= s[None, :])).astype(np.float32)
    mn = np.zeros((64, 16, 4, 8), np.float32)
    for sp_ in range(64):
        for t in range(4):
            if sp_ % 4 <= t:
                mn[sp_, sp_ // 4, t, :] = 1.0
    c["maskn"] = mn.reshape(64, 512)
    ps = np.zeros((32, 96), np.float32)
    ps[np.arange(32), 64 + np.arange(32)] = 1.0
    c["pesel"] = ps
    half = 16
    inv = np.exp(-math.log(10000.0) * np.arange(half, dtype=np.float32) / half).astype(np.float32)
    pos = np.concatenate([np.arange(SEQ), np.repeat(PAST + np.arange(4)[None, :], NSB, 0).reshape(-1)]).astype(np.float32)
    ang = (pos[:, None] * inv[None, :]).astype(np.float32)
    cos = np.cos(ang).astype(np.float32)
    sin = np.sin(ang).astype(np.float32)
    c["cosk"] = np.concatenate([cos, cos], 1)
    c["sink"] = np.concatenate([-sin, sin], 1)
    c["cosq"] = np.ascontiguousarray(c["cosk"].T)
    c["sinq"] = np.ascontiguousarray(c["sink"].T)
    return c


def _prep_shared(inp):
    f = lambda a: np.ascontiguousarray(np.asarray(a, dtype=np.float32))
    w = {}
    w["w_in_ab"] = f(inp["w_in_ab"])
    wuq = np.asarray(inp["w_uq"], np.float32).reshape(2, QL, 8, 96)
    swap = np.concatenate([wuq[..., 80:96], wuq[..., 64:80]], -1)
    w["wuq"] = f(np.concatenate([wuq, swap], -1).reshape(2, QL, 1024))
    wuk = np.asarray(inp["w_uk"], np.float32)
    t = np.zeros((2, KVL, 8, 96), np.float32)
    t[:, :, :, 0:64] = np.transpose(wuk, (0, 3, 1, 2))
    w["wuk"] = f(t.reshape(2, KVL, 768))
    w["wukn"] = f(np.transpose(wuk, (0, 2, 1, 3)).reshape(2, 64, 8 * KVL))
    wuv = np.asarray(inp["w_uv"], np.float32)
    w["wuv"] = f(np.transpose(wuv, (0, 2, 1, 3)).reshape(2, KVL, 512))
    w["wuvh"] = w["wuv"]
    wg = np.zeros((2, 128, 4, 2, 128), np.float32)
    for nm, q in (("w_rg", 0), ("w_ig", 1)):
        a = np.asarray(inp[nm], np.float32)
        for c in range(4):
            for hh in range(2):
                wg[:, hh * 64:(hh + 1) * 64, c, q, hh * 64:(hh + 1) * 64] = a[:, 2 * c + hh]
    w["wgate"] = f(wg.reshape(2, 128, 1024))
    w["w_out_ab"] = f(inp["w_out_ab"])
    w["w_in_c"] = f(inp["w_in_c"])
    ws = np.asarray(inp["w_s"], np.float32)
    w["wsT"] = f(np.transpose(ws, (0, 3, 1, 2)).reshape(2, 128, 1024))
    wss = np.transpose(ws[:, :, :4, :4], (0, 3, 1, 2))
    w["wsTs"] = f(np.tile(wss, (1, 16, 1, 16)).reshape(2, 64, 512))
    w["w_out_c"] = f(inp["w_out_c"])
    for k in ("w_gate", "w_up", "w_down", "w_pe", "w_pg"):
        w[k] = f(inp[k])
    vec = np.zeros((128, NV), np.float32)

    def put(name, arr):
        arr = np.asarray(arr, np.float32)
        n = arr.shape[0]
        x = arr.reshape(n, -1, 128)
        k = x.shape[1]
        vec[:, VC[name]:VC[name] + n * k] = np.transpose(x, (2, 0, 1)).reshape(128, n * k)

    put("g_mix", inp["g_mix"])
    put("g_ffn", inp["g_ffn"])
    put("g_pe", inp["g_pe"])
    put("g_q", inp["g_qnorm"])
    put("conv_w", np.asarray(inp["conv_w"]).reshape(8, LRU))
    put("conv_b", inp["conv_b"])
    put("b_rg", inp["b_rg"])
    put("b_ig", inp["b_ig"])
    put("lam", inp["lru_lambda"])
    vec[:, VC["eps"]] = EPS
    vec[:, VC["one"]] = 1.0
    vec[:, VC["pm8"]] = np.arange(128) % 8
    w["vecs"] = vec
    rep = lambda a: f(np.broadcast_to(np.asarray(a, np.float32)[..., None, :], a.shape[:-1] + (128, a.shape[-1])))
    w["gkv"] = rep(np.asarray(inp["g_kvnorm"]))
    w["gfin"] = rep(np.asarray(inp["g_final"]))
    w["lng"] = rep(np.asarray(inp["ln_g_c"]))
    w["lnb"] = rep(np.asarray(inp["ln_b_c"]))
    bs = np.asarray(inp["b_s"], np.float32)
    w["bsb"] = rep(bs.reshape(2, 1024))
    w["bsbs"] = rep(np.tile(bs[:, :, :4], (1, 1, 16)).reshape(2, 512))
    w["cache_ckv"] = np.asarray(inp["cache_ckv"], np.float32).reshape(2, -1, 16 * KVL)
    w["cache_kpe"] = np.asarray(inp["cache_kpe"], np.float32).reshape(2, -1, 16 * ROPE)
    w.update(_host_consts())
    return w


_NC_CACHE = {}


def _get_nc(do_sample=True):
    if do_sample not in _NC_CACHE:
        nc = bass.Bass("TRN2", target_bir_lowering=False)
        b = Builder(nc, do_sample)
        b.build()
        _NC_CACHE[do_sample] = nc
    return _NC_CACHE[do_sample]


def kernel(**inp):
    return _run(inp, True, NCORES)


def _run(inp, do_sample, ncores):
    shared = _prep_shared(inp)
    xp = np.asarray(inp["x_prompt"], np.float32)
    pp = np.asarray(inp["p_prompt"], np.float32)
    xs = np.asarray(inp["x_sample"], np.float32)
    pps = np.asarray(inp["p_sample"], np.float32)
    stl = np.asarray(inp["state_lru"], np.float32)
    stc = np.asarray(inp["state_conv"], np.float32)
    pt = np.asarray(inp["page_table"], np.int32)
    in_maps = []
    for c in range(ncores):
        m = dict(shared)
        m["xp"] = np.ascontiguousarray(xp[2 * c:2 * c + 2])
        m["pp"] = np.ascontiguousarray(pp[:, 2 * c:2 * c + 2])
        m["xs"] = np.ascontiguousarray(xs[NSB * c:NSB * (c + 1)].reshape(NTS, D))
        m["pps"] = np.ascontiguousarray(pps[:, NSB * c:NSB * (c + 1)].reshape(DEPTH, NTS, PLE))
        m["st_lru"] = np.ascontiguousarray(stl[:, NSB * c:NSB * (c + 1)])
        m["st_conv"] = np.ascontiguousarray(stc[:, NSB * c:NSB * (c + 1)].reshape(2, NSB * 3, LRU))
        ptc = pt[NSB * c:NSB * (c + 1)].reshape(NSB, 4, 16)
        m["ptab"] = np.ascontiguousarray(np.repeat(np.transpose(ptc, (2, 0, 1)), 8, axis=0).reshape(128, NSB * 4))
        in_maps.append(m)
    nc = _get_nc(do_sample)
    res = run_bass_kernel_spmd(nc, in_maps, core_ids=list(range(ncores)))
    R = res.results
    cat = lambda k, ax: np.concatenate([np.asarray(r[k], np.float32) for r in R], axis=ax)
    y_p = cat("y_p", 0)
    y_s = cat("y_s", 0).reshape(-1, 4, D)
    ckv_p = cat("ckv_p", 1)
    kpe_p = cat("kpe_p", 1)
    lru_p = cat("lru_p", 1)
    conv_p = cat("conv_p", 1)
    ckv_s = cat("ckv_s", 1).reshape(2, -1, 4, KVL)
    kpe_s = cat("kpe_s", 1).reshape(2, -1, 4, ROPE)
    lru_s = cat("lru_s", 1)
    conv_s = cat("conv_s", 1)
    v_s = cat("v_s", 1).reshape(2, -1, 4, 1024)
    return (y_p, y_s, ckv_p, kpe_p, lru_p, conv_p, ckv_s, kpe_s, lru_s, conv_s, v_s)
```

```python
import math
import contextlib
import numpy as np
import concourse.bass as bass
import concourse.mybir as mybir
from concourse.bass_utils import run_bass_kernel_spmd

F32 = mybir.dt.float32
BF16 = mybir.dt.bfloat16
I32 = mybir.dt.int32
AF = mybir.ActivationFunctionType
ALU = mybir.AluOpType

NCORES = 8
D = 1024
SEQ = 2048
NTP = 1024
NSB = 16
NTS = 64
DEPTH = 4
QL, KVL, ROPE = 384, 256, 32
LRU = 512
INAB = 1696
FFN = 2816
PLE = 256
EPS = 1e-6
ATTN_SCALE = 1.0 / math.sqrt(96.0)
PAST = 8192
NPHYS = 10240


class _Op:
    __slots__ = ("eng", "fn", "deps", "dma", "cnt", "marked", "pos", "waits", "nparts")


class Prog:
    ENGS = ("pe", "act", "dve", "pool", "sp")

    def __init__(self, nc):
        self.nc = nc
        self.streams = {e: [] for e in self.ENGS}
        self.last_w = {}
        self.readers = {}
        self.dma_last = {}
        self.dma_cnt = {}
        self.out_dmas = []

    def op(self, eng, fn, reads=(), writes=(), dma_key=None, nparts=1, is_out=False, extra_deps=()):
        o = _Op()
        o.eng = eng
        o.fn = fn
        o.dma = dma_key
        o.marked = False
        o.waits = None
        o.nparts = nparts
        deps = list(extra_deps)
        for k in reads:
            w = self.last_w.get(k)
            if w is not None:
                deps.append(w)
        for k in writes:
            w = self.last_w.get(k)
            if w is not None:
                deps.append(w)
            r = self.readers.get(k)
            if r:
                deps.extend(r.values())
        if dma_key is not None:
            p = self.dma_last.get(dma_key)
            if p is not None:
                deps.append(p)
            self.dma_last[dma_key] = o
            c = self.dma_cnt.get(dma_key, 0) + 16 * nparts
            self.dma_cnt[dma_key] = c
            o.cnt = c
        else:
            o.cnt = None
        o.deps = deps
        o.pos = len(self.streams[eng])
        self.streams[eng].append(o)
        for k in writes:
            self.last_w[k] = o
            self.readers[k] = {}
        for k in reads:
            rk = ("dma", id(o)) if dma_key is not None else eng
            self.readers.setdefault(k, {})[rk] = o
        if is_out:
            self.out_dmas.append(o)
        return o

    def pe(self, fn, reads=(), writes=()):
        return self.op("pe", fn, reads, writes)

    def act(self, fn, reads=(), writes=()):
        return self.op("act", fn, reads, writes)

    def dve(self, fn, reads=(), writes=()):
        return self.op("dve", fn, reads, writes)

    def pool(self, fn, reads=(), writes=()):
        return self.op("pool", fn, reads, writes)

    def dma(self, q, key, fn, reads=(), writes=(), nparts=1, is_out=False):
        return self.op(q, fn, reads, writes, dma_key=key, nparts=nparts, is_out=is_out)

    def barrier(self):
        lasts = [s[-1] for s in self.streams.values() if s] + list(self.dma_last.values())
        for eng in self.ENGS:
            self.op(eng, None, extra_deps=lasts)

    def finalize(self):
        self.op("sp", None, extra_deps=list(self.out_dmas))
        for eng in self.ENGS:
            waited = {}
            for o in self.streams[eng]:
                ws = []
                for d in o.deps:
                    if d is o:
                        continue
                    if d.dma is not None:
                        key = ("dma", d.dma)
                        if waited.get(key, 0) >= d.cnt:
                            continue
                        waited[key] = d.cnt
                        ws.append(d)
                    else:
                        if d.fn is None:
                            continue
                        if d.eng == eng and eng == "pe":
                            continue
                        if waited.get(d.eng, -1) >= d.pos:
                            continue
                        waited[d.eng] = d.pos
                        d.marked = True
                        ws.append(d)
                o.waits = ws
        for eng in self.ENGS:
            c = 0
            for o in self.streams[eng]:
                if o.dma is None and o.marked:
                    c += 1
                    o.cnt = c

    def emit(self, es):
        nc = self.nc
        self.finalize()
        esem = {e: es.enter_context(nc.semaphore("s_" + e)) for e in self.ENGS}
        dsem = {}
        for k in self.dma_cnt:
            dsem[k] = es.enter_context(nc.semaphore("d_%d" % len(dsem)))
        block = es.enter_context(nc.Block())

        def run(eng, e):
            for o in self.streams[eng]:
                need = {}
                for d in o.waits:
                    s = dsem[d.dma] if d.dma is not None else esem[d.eng]
                    key = id(s)
                    if key not in need or need[key][1] < d.cnt:
                        need[key] = (s, d.cnt)
                for s, v in need.values():
                    e.wait_ge(s, v)
                if o.fn is None:
                    continue
                if o.dma is not None:
                    o.fn(e, dsem[o.dma])
                else:
                    ins = o.fn(e)
                    if o.marked:
                        ins.then_inc(esem[eng], 1)

        @block.tensor
        def _(e):
            run("pe", e)

        @block.scalar
        def _(e):
            run("act", e)

        @block.vector
        def _(e):
            run("dve", e)

        @block.gpsimd
        def _(e):
            run("pool", e)

        @block.sync
        def _(e):
            run("sp", e)


VC = {}
_c = 0
for _n, _w in (("g_mix", 32), ("g_ffn", 32), ("g_pe", 32), ("g_q", 6), ("conv_w", 32), ("conv_b", 8),
               ("b_rg", 8), ("b_ig", 8), ("lam", 8), ("eps", 1), ("one", 1), ("pm8", 1)):
    VC[_n] = _c
    _c += _w
NV = 256


class Builder:
    def __init__(self, nc, do_sample=True):
        self.nc = nc
        self.P = Prog(nc)
        self.es = contextlib.ExitStack()
        self.do_sample = do_sample
        self.cnt = {}
        self.bank_i = 0
        self.held = set()
        self.pf = {}

    def din(self, name, shape, dt=F32):
        return self.nc.dram_tensor(name, list(shape), dt, kind="ExternalInput").ap()

    def dout(self, name, shape):
        return self.nc.dram_tensor(name, list(shape), F32, kind="ExternalOutput").ap()

    def view(self, off, shape, dt):
        n = 1
        for x in shape:
            n *= x
        esz = 4 if dt in (F32, I32) else 2
        a = self.arena[:, off // 2: off // 2 + n * esz // 2]
        if dt != BF16:
            a = a.bitcast(dt)
        if len(shape) == 2:
            a = a.rearrange("p (a b) -> p a b", a=shape[0])
        elif len(shape) == 3:
            a = a.rearrange("p (a b c) -> p a b c", a=shape[0], b=shape[1])
        return a

    def rot(self, name, n):
        i = self.cnt.get(name, 0)
        self.cnt[name] = i + 1
        return i % n

    def bank(self, hold=False):
        while True:
            b = self.bank_i
            self.bank_i = (b + 1) % 8
            if b not in self.held:
                break
        if hold:
            self.held.add(b)
        return b

    def release(self, b):
        self.held.discard(b)

    def evac_eng(self):
        return "act" if self.rot("evac", 2) == 0 else "dve"

    def ld(self, q, key, out, in_, wkeys, rkeys=()):
        wkeys = list(wkeys) + [("W8b", k[1]) for k in wkeys if isinstance(k, tuple) and k[0] == "W8"]
        self.P.dma(q, key, lambda e, s: e.dma_start(out=out, in_=in_).then_inc(s, 16), reads=rkeys, writes=wkeys)

    def st(self, key, out, in_, rkeys):
        self.P.dma("sp", key, lambda e, s: e.dma_start(out=out, in_=in_).then_inc(s, 16), reads=rkeys, is_out=True)

    def w8(self):
        i = self.rot("w8", 4)
        return i, self.W8[i], ("W8", i)

    def w4(self):
        i = self.rot("w4", 4)
        return i, self.W4[i], ("W4", i)

    def load_w8(self, wd, c0, n):
        i, t, k = self.w8()
        src = wd[:, c0:c0 + n].rearrange("(k p) n -> p k n", p=128)
        self.ld("pool", ("w8", i), t[:, :, 0:n], src, [k])
        return t, k

    def mm(self, b, n, lhsT, rhs, start, stop, rkeys, m=128, c0=0):
        ps = self.PS[b]
        self.P.pe(lambda e: e.matmul(ps[0:m, c0:c0 + n], lhsT, rhs, start=start, stop=stop),
                  reads=rkeys, writes=[("ps", b)])

    def build(self):
        nc, P, es = self.nc, self.P, self.es
        d = {}
        self.d = d
        d["xp"] = self.din("xp", [2, SEQ, D])
        d["pp"] = self.din("pp", [DEPTH, 2, SEQ, PLE])
        d["xs"] = self.din("xs", [NTS, D])
        d["pps"] = self.din("pps", [DEPTH, NTS, PLE])
        import os
        nph = int(os.environ.get("MK_NPH", str(NPHYS if self.do_sample else 2)))
        d["cache_ckv"] = self.din("cache_ckv", [2, nph * 8, 16 * KVL])
        d["cache_kpe"] = self.din("cache_kpe", [2, nph * 8, 16 * ROPE])
        d["st_lru"] = self.din("st_lru", [2, NSB, LRU])
        d["st_conv"] = self.din("st_conv", [2, NSB * 3, LRU])
        d["ptab"] = self.din("ptab", [128, NSB * 4], I32)
        d["w_in_ab"] = self.din("w_in_ab", [2, D, INAB])
        d["wuq"] = self.din("wuq", [2, QL, 1024])
        d["wuk"] = self.din("wuk", [2, KVL, 768])
        d["wukn"] = self.din("wukn", [2, 64, 8 * KVL])
        d["wuv"] = self.din("wuv", [2, KVL, 512])
        d["wuvh"] = self.din("wuvh", [2, KVL, 512])
        d["wgate"] = self.din("wgate", [2, 128, 1024])
        d["w_out_ab"] = self.din("w_out_ab", [2, D, D])
        d["w_in_c"] = self.din("w_in_c", [2, D, 2048])
        d["wsT"] = self.din("wsT", [2, 128, 1024])
        d["wsTs"] = self.din("wsTs", [2, 64, 512])
        d["w_out_c"] = self.din("w_out_c", [2, D, D])
        d["w_gate"] = self.din("w_gate", [DEPTH, D, FFN])
        d["w_up"] = self.din("w_up", [DEPTH, D, FFN])
        d["w_down"] = self.din("w_down", [DEPTH, FFN, D])
        d["w_pe"] = self.din("w_pe", [DEPTH, PLE, D])
        d["w_pg"] = self.din("w_pg", [DEPTH, D, D])
        d["vecs"] = self.din("vecs", [128, NV])
        d["gkv"] = self.din("gkv", [2, 128, KVL])
        d["gfin"] = self.din("gfin", [128, D])
        d["lng"] = self.din("lng", [2, 128, D])
        d["lnb"] = self.din("lnb", [2, 128, D])
        d["bsb"] = self.din("bsb", [2, 128, 1024])
        d["bsbs"] = self.din("bsbs", [2, 128, 512])
        d["ident"] = self.din("ident", [128, 128])
        d["ones"] = self.din("ones", [128, 128])
        d["maskc"] = self.din("maskc", [128, 128])
        d["masks"] = self.din("masks", [64, 64])
        d["maskn"] = self.din("maskn", [64, 512])
        d["pesel"] = self.din("pesel", [32, 96])
        d["cosq"] = self.din("cosq", [32, SEQ + NTS])
        d["sinq"] = self.din("sinq", [32, SEQ + NTS])
        d["cosk"] = self.din("cosk", [SEQ + NTS, 32])
        d["sink"] = self.din("sink", [SEQ + NTS, 32])
        o = {}
        self.o = o
        o["y_p"] = self.dout("y_p", [2, SEQ, D])
        o["y_s"] = self.dout("y_s", [NTS, D])
        o["ckv_p"] = self.dout("ckv_p", [2, 2, SEQ, KVL])
        o["kpe_p"] = self.dout("kpe_p", [2, 2, SEQ, ROPE])
        o["lru_p"] = self.dout("lru_p", [2, 2, LRU])
        o["conv_p"] = self.dout("conv_p", [2, 2, 3, LRU])
        o["ckv_s"] = self.dout("ckv_s", [2, NTS, KVL])
        o["kpe_s"] = self.dout("kpe_s", [2, NTS, ROPE])
        o["lru_s"] = self.dout("lru_s", [2, NSB, LRU])
        o["conv_s"] = self.dout("conv_s", [2, NSB, 3, LRU])
        o["v_s"] = self.dout("v_s", [2, NTS, 1024])

        self.arena = es.enter_context(nc.sbuf_tensor("arena", [128, 102400], BF16))
        self.PS = [es.enter_context(nc.psum_tensor("ps%d" % i, [128, 512], F32)) for i in range(8)]
        V = self.view
        self.H = V(0, [8, 1024], F32)
        self.XN = V(32768, [8, 1024], BF16)
        self.CKVT = V(49152, [2, 2, 2048], BF16)
        self.KPET = V(65536, [2, 2048], BF16)
        C0 = 73728
        self.IDF = V(C0, [128], F32)
        self.IDB = V(C0 + 512, [128], BF16)
        self.ONB = V(C0 + 768, [128], BF16)
        self.MKB = V(C0 + 1024, [128], BF16)
        self.PSEL = V(C0 + 1280, [96], BF16)
        self.VEC = V(C0 + 1536, [NV], F32)
        self.LAMC = V(C0 + 2560, [16], F32)
        self.STL = V(C0 + 2624, [8], F32)
        self.HIST = V(C0 + 2656, [8, 3], F32)
        self.MKS = V(C0 + 2752, [64], BF16)
        self.TMPV = V(C0 + 2880, [16], F32)
        self.W8 = [V(81920 + 8192 * i, [8, 512], BF16) for i in range(4)]
        self.W4 = [V(114688 + 4096 * i, [2, 1024], BF16) for i in range(4)]
        self.S0 = 131072

        self.load_consts()
        P.barrier()
        import os
        self.dbg_ph = os.environ.get("MK_PHASES", "kv,qn,attn,lru,gmlp,ffn,peg").split(",")
        self.dbg_layers = int(os.environ.get("MK_LAYERS", "4"))
        nun = int(os.environ.get("MK_UNITS", "4"))
        units = [dict(kind="p", seq=sq, u=u, NT=NTP, pos0=u * NTP) for sq in range(2) for u in range(2)][:nun]
        for U in units:
            self.run_unit(U)
        if self.do_sample:
            self.run_unit(dict(kind="s", NT=NTS, pos0=SEQ))
        P.emit(es)

    def vc(self, name, i=0):
        c = VC[name] + i
        return self.VEC[:, c:c + 1]

    def load_consts(self):
        P, d = self.P, self.d
        self.ld("sp", "c0", self.IDF, d["ident"], ["IDF"])
        self.ld("pool", "c1", self.IDB, d["ident"], ["IDB"])
        self.ld("pool", "c2", self.ONB, d["ones"], ["ONB"])
        self.ld("pool", "c3", self.MKB, d["maskc"], ["MKB"])
        self.ld("pool", "c4", self.PSEL[0:32, :], d["pesel"], ["PSEL"])
        self.ld("sp", "c5", self.VEC, d["vecs"], ["VEC"])
        self.ld("pool", "c6", self.MKS[0:64, :], d["masks"], ["MKS"])
        lam = self.VEC[:, VC["lam"]:VC["lam"] + 8]
        tv = self.TMPV[:, 0:8]
        lc = self.LAMC
        P.act(lambda e: e.activation(out=tv, in_=lam, func=AF.Exp, scale=-1.0), reads=["VEC"], writes=["TMPV"])
        P.act(lambda e: e.activation(out=tv, in_=tv, func=AF.Ln, bias=self.vc("one"), scale=1.0),
              reads=["TMPV", "VEC"], writes=["TMPV"])
        P.dve(lambda e: e.tensor_scalar(lc[:, 0:8], tv, -8.0, None, ALU.mult), reads=["TMPV"], writes=["LAMC"])
        P.dve(lambda e: e.tensor_scalar(lc[:, 8:16], tv, -16.0, None, ALU.mult), reads=["TMPV"], writes=["LAMC"])

    def run_unit(self, U):
        P = self.P
        NT = U["NT"]
        U["tiles"] = [(t0, min(512, NT - t0)) for t0 in range(0, NT, 512)]
        U["blocks"] = [(t0, min(128, NT - t0)) for t0 in range(0, NT, 128)]
        U["tag"] = "%s%s%s" % (U["kind"], U.get("seq", ""), U.get("u", ""))
        self.load_x(U)
        ph = self.dbg_ph
        for i in range(self.dbg_layers):
            j = i // 2
            self.norm_h(U, VC["g_mix"] + 8 * i)
            if i % 2 == 0:
                if "kv" in ph:
                    self.kv_phase(U, j)
                if "qn" in ph:
                    self.qn_phase(U, j)
                if "attn" in ph:
                    if U["kind"] == "p":
                        self.pf["attn"] = self.pre_attn_prompt(j)
                    P.barrier()
                    if U["kind"] == "p":
                        self.attn_prompt(U, j)
                    else:
                        self.attn_sample(U, j)
                if "lru" in ph:
                    d_ = self.d
                    self.pf["lru"] = (self.load_w8(d_["w_in_ab"][j], 672, 128), self.load_w8(d_["w_in_ab"][j], 1184, 128))
                P.barrier()
                if "lru" in ph:
                    self.lru_phase(U, j)
            else:
                if "gmlp" in ph:
                    self.pf["gmlp"] = self.load_w8(self.d["w_in_c"][j], 0, 512)
                    P.barrier()
                    self.gmlp_phase(U, j)
                    P.barrier()
            self.norm_h(U, VC["g_ffn"] + 8 * i)
            if "ffn" in ph:
                self.ffn_phase(U, i)
            self.norm_h(U, VC["g_pe"] + 8 * i)
            if "peg" in ph:
                self.peg_phase(U, i)
        self.final_phase(U)

    def load_x(self, U):
        P, d = self.P, self.d
        S0 = self.S0
        XT = [self.view(S0 + 51328 + 4096 * i, [1024], F32) for i in range(2)]
        for (t0, tn) in U["blocks"]:
            bi = self.rot("xt", 2)
            xt = XT[bi]
            if U["kind"] == "p":
                src = d["xp"][U["seq"], U["pos0"] + t0: U["pos0"] + t0 + tn, :]
            else:
                src = d["xs"][t0:t0 + tn, :]
            self.ld("sp", ("xt", bi), xt[0:tn, :], src, [("XT", bi)])
            for half in range(2):
                b = self.bank()
                ps = self.PS[b]
                for q in range(4):
                    m = half * 4 + q
                    P.pe(lambda e, ps=ps, q=q, m=m, xt=xt, tn=tn: e.transpose(
                        ps[:, q * 128:q * 128 + tn], xt[0:tn, m * 128:(m + 1) * 128], self.IDF[0:tn, 0:tn]),
                        reads=[("XT", bi), "IDF"], writes=[("ps", b)])
                hv = self.H[:, half * 4:half * 4 + 4, t0:t0 + tn]
                pv = ps[:, :].rearrange("p (a b) -> p a b", a=4)[:, :, 0:tn]
                eng = self.evac_eng()
                if eng == "act":
                    P.act(lambda e, hv=hv, pv=pv: e.activation(out=hv, in_=pv, func=AF.Copy),
                          reads=[("ps", b)], writes=[("Hm", half * 4 + q_, (t0 // 512) * 512) for q_ in range(4)])
                else:
                    P.dve(lambda e, hv=hv, pv=pv: e.tensor_copy(out=hv, in_=pv),
                          reads=[("ps", b)], writes=[("Hm", half * 4 + q_, (t0 // 512) * 512) for q_ in range(4)])

    def norm_h(self, U, gcol):
        P = self.P
        S0 = self.S0
        SQ = self.view(S0, [8, 512], BF16)
        RS = [self.view(S0 + 8192 + 2048 * i, [512], F32) for i in range(2)]
        for (t0, tn) in U["tiles"]:
            hv = self.H[:, :, t0:t0 + tn]
            sq = SQ[:, :, 0:tn]
            P.act(lambda e, hv=hv, sq=sq: e.activation(out=sq, in_=hv, func=AF.Square),
                  reads=[("Hm", m_, t0) for m_ in range(8)], writes=["SQ"])
            b = self.bank()
            for k in range(8):
                self.mm(b, tn, self.ONB, SQ[:, k, 0:tn], k == 0, k == 7, ["SQ", "ONB"])
            ri = self.rot("rs", 2)
            rs = RS[ri][:, 0:tn]
            ps = self.PS[b][:, 0:tn]
            P.act(lambda e, rs=rs, ps=ps: e.activation(out=rs, in_=ps, func=AF.Sqrt, bias=self.vc("eps"),
                                                       scale=1.0 / D), reads=[("ps", b), "VEC"], writes=[("RS", ri)])
            P.dve(lambda e, rs=rs: e.reciprocal(out=rs, in_=rs), reads=[("RS", ri)], writes=[("RS", ri)])
            for k in range(8):
                xo = self.XN[:, k, t0:t0 + tn]
                hi = self.H[:, k, t0:t0 + tn]
                g = self.VEC[:, gcol + k:gcol + k + 1]
                P.dve(lambda e, xo=xo, hi=hi, g=g, rs=rs: e.scalar_tensor_tensor(
                    out=xo, in0=hi, scalar=g, in1=rs, op0=ALU.mult, op1=ALU.mult),
                    reads=[("RS", ri), "VEC", ("Hm", k, t0)], writes=[("XN", k, t0)])

    def kv_phase(self, U, j):
        P, d, o = self.P, self.d, self.o
        S0 = self.S0
        NT = U["NT"]
        sample = U["kind"] == "s"
        wkv, wk = self.load_w8(d["w_in_ab"][j], QL, KVL + ROPE)
        KV0 = S0 + 33792
        GKV = self.view(KV0, [KVL], F32)
        COSK = self.view(KV0 + 1024, [8, 32], F32)
        SINK = self.view(KV0 + 2048, [8, 32], F32)
        self.ld("sp", "gkv", GKV, d["gkv"][j], ["GKV"])
        p0 = U["pos0"]
        nb = len(U["blocks"])
        bn = U["blocks"][0][1]
        self.ld("sp", "cosk", COSK[0:bn, 0:nb, :], d["cosk"][p0:p0 + NT, :].rearrange("(b p) r -> p b r", p=bn), ["COSK"])
        self.ld("sp", "sink", SINK[0:bn, 0:nb, :], d["sink"][p0:p0 + NT, :].rearrange("(b p) r -> p b r", p=bn), ["SINK"])
        base = KV0 + 3072
        SETS = []
        for i in range(2):
            bo = base + 4096 * i
            SETS.append(dict(CKVF=self.view(bo, [KVL], F32), KPEF=self.view(bo + 1024, [32], F32),
                             CKVB=self.view(bo + 1152, [KVL], BF16), KPEB=self.view(bo + 2560, [128], BF16),
                             T1=self.view(bo + 1728, [32], F32), T2=self.view(bo + 1856, [32], F32),
                             SS=self.view(bo + 1984, [2], F32), JUNK=self.view(bo + 2048, [KVL], BF16)))
        for i in range(2):
            kb_ = SETS[i]["KPEB"]
            P.dve(lambda e, kb_=kb_: e.memset(kb_, 0.0), writes=[("KV", i, "KPEB")])
        if sample:
            ckT = self.view(S0 + 65280, [2, 64], BF16)
            kpT = self.view(S0 + 65536, [64], BF16)
            self.SCKT, self.SKPT = ckT, kpT
        for bi, (t0, tn) in enumerate(U["blocks"]):
            si = self.rot("kvset", 2)
            S = SETS[si]
            sk = lambda n: ("KV", si, n)
            b = self.bank()
            for k in range(8):
                self.mm(b, KVL + ROPE, self.XN[:, k, t0:t0 + tn], wkv[:, k, 0:KVL + ROPE], k == 0, k == 7,
                        [("XN", k, (t0 // 512) * 512), wk], m=tn)
            ps = self.PS[b]
            P.act(lambda e, S=S, ps=ps, tn=tn: e.activation(out=S["JUNK"][0:tn, :], in_=ps[0:tn, 0:KVL], func=AF.Square,
                                                            accum_out=S["SS"][0:tn, 0:1]),
                  reads=[("ps", b)], writes=[sk("JUNK"), sk("SS")])
            P.act(lambda e, S=S, tn=tn: e.activation(out=S["SS"][0:tn, 1:2], in_=S["SS"][0:tn, 0:1], func=AF.Sqrt,
                                                     bias=self.VEC[0:tn, VC["eps"]:VC["eps"] + 1], scale=1.0 / KVL),
                  reads=[sk("SS"), "VEC"], writes=[sk("SS1")])
            P.dve(lambda e, S=S, tn=tn: e.reciprocal(out=S["SS"][0:tn, 1:2], in_=S["SS"][0:tn, 1:2]),
                  reads=[sk("SS1")], writes=[sk("SS1")])
            P.dve(lambda e, S=S, ps=ps, tn=tn: e.scalar_tensor_tensor(
                out=S["CKVF"][0:tn, :], in0=ps[0:tn, 0:KVL], scalar=S["SS"][0:tn, 1:2], in1=GKV[0:tn, :],
                op0=ALU.mult, op1=ALU.mult), reads=[("ps", b), sk("SS1"), "GKV"], writes=[sk("CKVF")])
            P.act(lambda e, S=S, tn=tn: e.activation(out=S["CKVB"][0:tn, :], in_=S["CKVF"][0:tn, :], func=AF.Copy),
                  reads=[sk("CKVF")], writes=[sk("CKVB")])
            import os
            kvl = int(os.environ.get("MK_KV", "9"))
            if kvl < 2:
                self.st(("ock", si), o["ckv_p"][j, U["seq"], p0 + t0:p0 + t0 + tn, :], S["CKVF"][0:tn, :], [sk("CKVF")])
                continue
            zpe = ps[0:tn, KVL:KVL + ROPE]
            ck = COSK[0:tn, bi, :]
            sn = SINK[0:tn, bi, :]
            P.dve(lambda e, S=S, zpe=zpe, ck=ck, tn=tn: e.tensor_tensor(out=S["T1"][0:tn, :], in0=zpe, in1=ck, op=ALU.mult),
                  reads=[("ps", b), "COSK"], writes=[sk("T1")])
            P.dve(lambda e, S=S, ps=ps, sn=sn, tn=tn: e.tensor_tensor(
                out=S["T2"][0:tn, 0:16], in0=ps[0:tn, KVL + 16:KVL + 32], in1=sn[:, 0:16], op=ALU.mult),
                reads=[("ps", b), "SINK"], writes=[sk("T2")])
            P.dve(lambda e, S=S, ps=ps, sn=sn, tn=tn: e.tensor_tensor(
                out=S["T2"][0:tn, 16:32], in0=ps[0:tn, KVL:KVL + 16], in1=sn[:, 16:32], op=ALU.mult),
                reads=[("ps", b), "SINK"], writes=[sk("T2")])
            P.dve(lambda e, S=S, tn=tn: e.tensor_tensor(out=S["KPEF"][0:tn, :], in0=S["T1"][0:tn, :], in1=S["T2"][0:tn, :],
                                                        op=ALU.add), reads=[sk("T1"), sk("T2")], writes=[sk("KPEF")])
            P.act(lambda e, S=S, tn=tn: e.activation(out=S["KPEB"][0:tn, 0:32], in_=S["KPEF"][0:tn, :], func=AF.Copy),
                  reads=[sk("KPEF")], writes=[sk("KPEB")])
            if sample:
                oc = o["ckv_s"][j, t0:t0 + tn, :]
                ok = o["kpe_s"][j, t0:t0 + tn, :]
            else:
                oc = o["ckv_p"][j, U["seq"], p0 + t0:p0 + t0 + tn, :]
                ok = o["kpe_p"][j, U["seq"], p0 + t0:p0 + t0 + tn, :]
            self.st(("ock", si), oc, S["CKVF"][0:tn, :], [sk("CKVF")])
            self.st(("okp", si), ok, S["KPEF"][0:tn, :], [sk("KPEF")])
            if kvl < 3:
                continue
            b2 = self.bank()
            pb = self.PS[b2][:, 0:256].bitcast(BF16)
            for cc in range(2):
                P.pe(lambda e, S=S, pb=pb, cc=cc, tn=tn: e.transpose(
                    pb[:, cc * 128:cc * 128 + tn], S["CKVB"][0:tn, cc * 128:(cc + 1) * 128], self.IDB[0:tn, 0:tn]),
                    reads=[sk("CKVB"), "IDB"], writes=[("ps", b2)])
            P.pe(lambda e, S=S, pb=pb, tn=tn: e.transpose(pb[:, 256:256 + tn], S["KPEB"][0:tn, :], self.IDB[0:tn, 0:tn]),
                 reads=[sk("KPEB"), "IDB"], writes=[("ps", b2)])
            if sample:
                SN = self.view(S0 + 65664, [384], BF16)
                self.SNEW = SN
                P.dve(lambda e, SN=SN: e.memset(SN[0:64, 256:384], 1.0), writes=["SNEWp"])
                P.act(lambda e, S=S, SN=SN, tn=tn: e.activation(out=SN[0:tn, 0:256], in_=S["CKVF"][0:tn, :], func=AF.Copy),
                      reads=[sk("CKVF")], writes=["SNEWc"])
                P.act(lambda e, S=S, SN=SN, tn=tn: e.activation(out=SN[0:tn, 256:288], in_=S["KPEF"][0:tn, :], func=AF.Copy),
                      reads=[sk("KPEF"), "SNEWp"], writes=["SNEWk"])
                dst_c = ckT[:, :, t0:t0 + tn]
                dst_k = kpT[0:32, t0:t0 + tn]
            else:
                dst_c = self.CKVT[:, j, :, p0 + t0:p0 + t0 + tn]
                dst_k = self.KPET[0:32, j, p0 + t0:p0 + t0 + tn]
            src_c = pb[:, 0:256].rearrange("p (a b) -> p a b", a=2)[:, :, 0:tn]
            P.dve(lambda e, dst_c=dst_c, src_c=src_c: e.tensor_copy(out=dst_c, in_=src_c),
                  reads=[("ps", b2)], writes=[("CKVT", j, t0)])
            P.dve(lambda e, dst_k=dst_k, pb=pb, tn=tn: e.tensor_copy(out=dst_k, in_=pb[0:32, 256:256 + tn]),
                  reads=[("ps", b2)], writes=[("KPET", j, t0)])

    def qn_phase(self, U, j):
        P, d = self.P, self.d
        S0 = self.S0
        wq, wk = self.load_w8(d["w_in_ab"][j], 0, QL)
        self.QN = self.view(S0 + 66560, [3, 1024], BF16)
        SQ3 = self.view(S0 + 46208, [3, 512], BF16)
        RS = self.view(S0 + 49280, [512], F32)
        for (t0, tn) in U["tiles"]:
            bs = []
            for c in range(3):
                b = self.bank()
                bs.append(b)
                for k in range(8):
                    self.mm(b, tn, wq[:, k, c * 128:(c + 1) * 128], self.XN[:, k, t0:t0 + tn], k == 0, k == 7,
                            [("XN", k, t0), wk])
                ps = self.PS[b][:, 0:tn]
                sq = SQ3[:, c, 0:tn]
                P.act(lambda e, sq=sq, ps=ps: e.activation(out=sq, in_=ps, func=AF.Square), reads=[("ps", b)],
                      writes=[("SQ3", c)])
            b4 = self.bank()
            for c in range(3):
                self.mm(b4, tn, self.ONB, SQ3[:, c, 0:tn], c == 0, c == 2, [("SQ3", c), "ONB"])
            rs = RS[:, 0:tn]
            ps4 = self.PS[b4][:, 0:tn]
            P.act(lambda e, rs=rs, ps4=ps4: e.activation(out=rs, in_=ps4, func=AF.Sqrt, bias=self.vc("eps"), scale=1.0 / QL),
                  reads=[("ps", b4), "VEC"], writes=["RSQ"])
            P.dve(lambda e, rs=rs: e.reciprocal(out=rs, in_=rs), reads=["RSQ"], writes=["RSQ"])
            for c in range(3):
                ps = self.PS[bs[c]][:, 0:tn]
                qo = self.QN[:, c, t0:t0 + tn]
                g = self.vc("g_q", j * 3 + c)
                P.dve(lambda e, qo=qo, ps=ps, g=g, rs=rs: e.scalar_tensor_tensor(
                    out=qo, in0=ps, scalar=g, in1=rs, op0=ALU.mult, op1=ALU.mult),
                    reads=[("ps", bs[c]), "RSQ", "VEC"], writes=[("QN", c, t0)])

    def q_head(self, wq, wqk, h, t0, tn, QG, hl, cq, sq, TQ, qkey):
        P = self.P
        b = self.bank()
        for c in range(3):
            self.mm(b, tn, wq[:, c, h * 128:(h + 1) * 128], self.QN[:, c, t0:t0 + tn], c == 0, c == 2,
                    [("QN", c, t0), wqk])
        ps = self.PS[b]
        qn_ = QG[0:64, hl, 0:tn]
        P.act(lambda e: e.activation(out=qn_, in_=ps[0:64, 0:tn], func=AF.Copy), reads=[("ps", b)], writes=[qkey])
        t1 = TQ[0][64:96, 0:tn]
        t2 = TQ[1][64:96, 0:tn]
        P.dve(lambda e: e.tensor_tensor(out=t1, in0=ps[64:96, 0:tn], in1=cq, op=ALU.mult),
              reads=[("ps", b), "COSQ"], writes=["TQ0"])
        P.dve(lambda e: e.tensor_tensor(out=t2, in0=ps[96:128, 0:tn], in1=sq, op=ALU.mult),
              reads=[("ps", b), "SINQ"], writes=["TQ1"])
        qp = QG[64:96, hl, 0:tn]
        P.dve(lambda e: e.tensor_tensor(out=qp, in0=t1, in1=t2, op=ALU.add), reads=["TQ0", "TQ1"], writes=[qkey])
        return b

    def attn_prompt(self, U, j):
        P, d = self.P, self.d
        S0 = self.S0
        u = U["u"]
        p0 = U["pos0"]
        nkeys = p0 + NTP
        nkb = nkeys // 128
        KG = self.view(S0, [4, 2048], BF16)
        VG = self.view(S0 + 16384, [16, 4, 128], BF16)
        QGs = [self.view(S0 + 32768 + 4096 * i, [4, 512], BF16) for i in range(1)]
        OG = self.view(S0 + 36864, [4, 512], BF16)
        EXPT = [self.view(S0 + 40960 + 1024 * i, [512], BF16) for i in range(3)]
        COSQ = self.view(S0 + 44032, [512], F32)
        SINQ = self.view(S0 + 46080, [512], F32)
        RC = self.view(S0 + 48128, [512], F32)
        TQ = [self.view(S0 + 50176 + 2048 * i, [512], F32) for i in range(2)]
        pf = self.pf.pop("attn", None) or self.pre_attn_prompt(j)
        i0, WQv, wqk, i1, WUK, WUV, wkk = pf
        vg_ones = VG[:, :, :, 64:128]
        P.pool(lambda e: e.memset(vg_ones, 1.0), writes=["VGones"])
        for g in range(2):
            i2, _, wok = self.w8()
            WOA = self.arena_w8_view(i2, [4, 1024])
            self.ld("pool", ("w8", i2), WOA[0:64, :, :],
                    d["w_out_ab"][j][g * 256:(g + 1) * 256, :].rearrange("(h v) n -> v h n", v=64), [wok])
            for hl in range(4):
                h = g * 4 + hl
                for k0 in range(0, nkeys, 512):
                    b = self.bank()
                    for cc in range(2):
                        self.mm(b, 512, WUK[:, cc, h * 96:(h + 1) * 96], self.CKVT[:, j, cc, k0:k0 + 512], cc == 0, False,
                                [wkk, ("CKVT", j, "all")], m=96)
                    self.mm(b, 512, self.PSEL[0:32, :], self.KPET[0:32, j, k0:k0 + 512], False, True,
                            ["PSEL", ("KPET", j, "all")], m=96)
                    kg = KG[0:96, hl, k0:k0 + 512]
                    ps = self.PS[b][0:96, :]
                    eng = self.evac_eng()
                    if eng == "act":
                        P.act(lambda e, kg=kg, ps=ps: e.activation(out=kg, in_=ps, func=AF.Copy), reads=[("ps", b)],
                              writes=[("KG", hl)])
                    else:
                        P.dve(lambda e, kg=kg, ps=ps: e.tensor_copy(out=kg, in_=ps), reads=[("ps", b)], writes=[("KG", hl)])
            for kb in range(nkb):
                b = self.bank()
                for cc in range(2):
                    self.mm(b, 256, self.CKVT[:, j, cc, kb * 128:(kb + 1) * 128], WUV[:, cc, g * 256:(g + 1) * 256],
                            cc == 0, cc == 1, [("W8b", i1), ("CKVT", j, "all")])
                vg = VG[:, kb, :, 0:64]
                ps = self.PS[b][:, 0:256].rearrange("p (a b) -> p a b", a=4)
                eng = self.evac_eng()
                if eng == "act":
                    P.act(lambda e, vg=vg, ps=ps: e.activation(out=vg, in_=ps, func=AF.Copy), reads=[("ps", b)],
                          writes=[("VG", kb)])
                else:
                    P.dve(lambda e, vg=vg, ps=ps: e.tensor_copy(out=vg, in_=ps), reads=[("ps", b)], writes=[("VG", kb)])
            for (t0, tn) in U["tiles"]:
                gq = (p0 + t0) // 512
                self.ld("sp", "cosq", COSQ[64:96, :], d["cosq"][:, p0 + t0:p0 + t0 + 512], ["COSQ"])
                self.ld("sp", "sinq", SINQ[64:96, :], d["sinq"][:, p0 + t0:p0 + t0 + 512], ["SINQ"])
                QG = QGs[0]
                for hl in range(4):
                    h = g * 4 + hl
                    self.q_head(WQv, wqk, h, t0, tn, QG, hl, COSQ[64:96, 0:tn], SINQ[64:96, 0:tn], TQ, ("QG", hl))
                for hl in range(4):
                    bo = self.bank(hold=True)
                    nblk = gq * 4 + 4
                    pend = None
                    for kb in range(nblk):
                        jd = kb - gq * 4
                        c0 = 0 if jd < 0 else jd * 128
                        n = 512 - c0
                        bsc = self.bank()
                        self.mm(bsc, n, KG[0:96, hl, kb * 128:(kb + 1) * 128], QG[0:96, hl, c0:512], True, True,
                                [("KG", hl), ("QG", hl)], c0=c0)
                        ei = self.rot("expt", 3)
                        ex = EXPT[ei]
                        pss = self.PS[bsc][:, c0:512]
                        exv = ex[:, c0:512]
                        P.act(lambda e, exv=exv, pss=pss: e.activation(out=exv, in_=pss, func=AF.Exp, scale=ATTN_SCALE),
                              reads=[("ps", bsc)], writes=[("EXPT", ei)])
                        if jd >= 0:
                            dv = ex[:, c0:c0 + 128]
                            P.dve(lambda e, dv=dv: e.tensor_tensor(out=dv, in0=dv, in1=self.MKB, op=ALU.mult),
                                  reads=[("EXPT", ei), "MKB"], writes=[("EXPT", ei)])
                        if pend is not None:
                            pk, pexv, pei, pc0, pn = pend
                            self.mm(bo, pn, VG[:, pk, hl, :], pexv, pk == 0, False,
                                    [("VG", pk), "VGones", ("EXPT", pei)], c0=pc0)
                        pend = (kb, exv, ei, c0, n)
                    pk, pexv, pei, pc0, pn = pend
                    self.mm(bo, pn, VG[:, pk, hl, :], pexv, pk == 0, True, [("VG", pk), "VGones", ("EXPT", pei)], c0=pc0)
                    pso = self.PS[bo]
                    rc = RC[64:128, :]
                    P.dve(lambda e, rc=rc, pso=pso: e.reciprocal(out=rc, in_=pso[64:128, :]), reads=[("ps", bo)], writes=["RC"])
                    og = OG[0:64, hl, :]
                    P.dve(lambda e, og=og, pso=pso, rc=rc: e.tensor_tensor(out=og, in0=pso[0:64, :], in1=rc, op=ALU.mult),
                          reads=[("ps", bo), "RC"], writes=[("OG", hl)])
                    self.release(bo)
                for m in range(8):
                    b = self.bank()
                    for hl in range(4):
                        self.mm(b, tn, WOA[0:64, hl, m * 128:(m + 1) * 128], OG[0:64, hl, 0:tn], hl == 0, hl == 3,
                                [wok, ("OG", hl)])
                    self.h_add(b, m, t0, tn)

    def pre_attn_prompt(self, j):
        d = self.d
        i0, _, wqk = self.w8()
        WQv = self.arena_w8_view(i0, [3, 1024])
        self.ld("pool", ("w8", i0), WQv, d["wuq"][j].rearrange("(k p) n -> p k n", p=128), [wqk])
        i1, _, wkk = self.w8()
        WUK = self.arena_w8_view(i1, [2, 768])
        WUV = self.arena_w8_view(i1, [2, 512], off=3072)
        self.ld("pool", ("w8", i1), WUK, d["wuk"][j].rearrange("(k p) n -> p k n", p=128), [wkk])
        self.ld("pool", ("w8b", i1), WUV, d["wuv"][j].rearrange("(k p) n -> p k n", p=128), [("W8b", i1)])
        return (i0, WQv, wqk, i1, WUK, WUV, wkk)

    def arena_w8_view(self, i, shape, off=0):
        return self.view(81920 + 8192 * i + off, shape, BF16)

    def h_add(self, b, m, t0, tn):
        hv = self.H[:, m, t0:t0 + tn]
        ps = self.PS[b][:, 0:tn]
        self.P.dve(lambda e: e.tensor_tensor(out=hv, in0=ps, in1=hv, op=ALU.add),
                   reads=[("ps", b)], writes=[("Hm", m, t0)])

    def lru_phase(self, U, j):
        P, d, o = self.P, self.d, self.o
        S0 = self.S0
        NT = U["NT"]
        sample = U["kind"] == "s"
        L0 = S0 + 38912
        LY = self.view(L0, [4, 1024], BF16)
        WG = self.view(L0 + 8192, [8, 128], BF16)
        self.ld("pool", "wg", WG, d["wgate"][j].rearrange("p (a b) -> p a b", a=8), ["WG"])
        o0 = L0 + 10240
        ZX = self.view(o0, [520], F32)
        XC = self.view(o0 + 2080, [512], F32)
        R = self.view(o0 + 4128, [512], F32)
        IG = self.view(o0 + 6176, [512], F32)
        A = self.view(o0 + 8224, [512], F32)
        BX = self.view(o0 + 10272, [512], F32)
        HS = self.view(o0 + 12320, [512], F32)
        XCB = self.view(o0 + 14368, [512], BF16)
        GZ = self.view(o0 + 15392, [512], BF16)
        for c in range(4):
            pfw = self.pf.pop("lru", None) if c == 0 else None
            if pfw is not None:
                (wzx, kzx), (wzg, kzg) = pfw
            else:
                wzx, kzx = self.load_w8(d["w_in_ab"][j], 672 + c * 128, 128)
                wzg, kzg = self.load_w8(d["w_in_ab"][j], 1184 + c * 128, 128)
            jc = j * 4 + c
            cwk = [self.vc("conv_w", j * 16 + k * 4 + c) for k in range(4)]
            cb = self.vc("conv_b", jc)
            brg = self.vc("b_rg", jc)
            big = self.vc("b_ig", jc)
            lc1 = self.LAMC[:, jc:jc + 1]
            lc2 = self.LAMC[:, 8 + jc:9 + jc]
            one = self.vc("one")
            for (t0, tn) in U["tiles"]:
                b = self.bank()
                for k in range(8):
                    self.mm(b, tn, wzx[:, k, 0:128], self.XN[:, k, t0:t0 + tn], k == 0, k == 7, [("XN", k, t0), kzx])
                ps = self.PS[b][:, 0:tn]
                if not sample:
                    zxv = ZX[:, 3:3 + tn]
                    P.act(lambda e, zxv=zxv, ps=ps: e.activation(out=zxv, in_=ps, func=AF.Copy), reads=[("ps", b)], writes=["ZX"])
                    hist = self.HIST[:, jc, :]
                    if U["u"] == 0 and t0 == 0:
                        P.dve(lambda e: e.memset(ZX[:, 0:3], 0.0), writes=["ZX"])
                    else:
                        P.dve(lambda e, hist=hist: e.tensor_copy(out=ZX[:, 0:3], in_=hist), reads=[("HIST", jc)], writes=["ZX"])
                    xs = [ZX[:, k:k + tn] for k in range(4)]
                    xcv = XC[:, 0:tn]
                else:
                    zx3 = ZX[:, 0:112].rearrange("p (b t) -> p b t", t=7)
                    P.act(lambda e, zx3=zx3, ps=ps: e.activation(out=zx3[:, :, 3:7], in_=ps.rearrange("p (b t) -> p b t", t=4),
                                                                 func=AF.Copy), reads=[("ps", b)], writes=["ZX"])
                    sc = self.view(o0 + 16416, [128], F32)[0:48, :]
                    self.ld("sp", "stc", sc, d["st_conv"][j][:, c * 128:(c + 1) * 128], ["SC"])
                    bt = self.bank()
                    pst = self.PS[bt]
                    P.pe(lambda e, pst=pst, sc=sc: e.transpose(pst[:, 0:48], sc, self.IDF[0:48, 0:48]),
                         reads=["SC", "IDF"], writes=[("ps", bt)])
                    P.dve(lambda e, zx3=zx3, pst=pst: e.tensor_copy(out=zx3[:, :, 0:3],
                                                                   in_=pst[:, 0:48].rearrange("p (b t) -> p b t", t=3)),
                          reads=[("ps", bt)], writes=["ZX"])
                    xs = [zx3[:, :, k:k + 4] for k in range(4)]
                    xcv = XC[:, 0:64].rearrange("p (b t) -> p b t", t=4)
                P.dve(lambda e, xcv=xcv, x3=xs[3], w3=cwk[3], cb=cb: e.tensor_scalar(xcv, x3, w3, cb, ALU.mult, ALU.add),
                      reads=["ZX", "VEC"], writes=["XC"])
                for k in (2, 1, 0):
                    P.dve(lambda e, xcv=xcv, xk=xs[k], wk_=cwk[k]: e.scalar_tensor_tensor(
                        out=xcv, in0=xk, scalar=wk_, in1=xcv, op0=ALU.mult, op1=ALU.add), reads=["ZX", "VEC"], writes=["XC"])
                xc2 = XC[:, 0:tn]
                P.act(lambda e, xc2=xc2, tn=tn: e.activation(out=XCB[:, 0:tn], in_=xc2, func=AF.Copy), reads=["XC"], writes=["XCB"])
                if not sample:
                    P.dve(lambda e, hist=hist, tn=tn: e.tensor_copy(out=hist, in_=ZX[:, tn:tn + 3]), reads=["ZX"], writes=[("HIST", jc)])
                    if U["u"] == 1 and t0 + tn == NT:
                        dst = o["conv_p"][j, U["seq"], :, c * 128:(c + 1) * 128].rearrange("t p -> p t")
                        self.P.dma("sp", "oconv", lambda e, s, dst=dst, hist=hist: self._nc_dma(e, s, dst, hist),
                                   reads=[("HIST", jc)], is_out=True)
                else:
                    CS = self.view(o0 + 17440, [48], F32)
                    OS = self.view(o0 + 17632, [128], F32)
                    P.dve(lambda e, zx3=zx3, CS=CS: e.tensor_copy(out=CS.rearrange("p (b t) -> p b t", t=3), in_=zx3[:, :, 4:7]),
                          reads=["ZX"], writes=["CS"])
                    bt2 = self.bank()
                    pst2 = self.PS[bt2]
                    P.pe(lambda e, pst2=pst2, CS=CS: e.transpose(pst2[0:48, 0:128], CS, self.IDF), reads=["CS", "IDF"],
                         writes=[("ps", bt2)])
                    P.dve(lambda e, pst2=pst2, OS=OS: e.tensor_copy(out=OS[0:48, :], in_=pst2[0:48, 0:128]), reads=[("ps", bt2)],
                          writes=["OS"])
                    dst = o["conv_s"][j].rearrange("b t n -> (b t) n")[:, c * 128:(c + 1) * 128]
                    self.P.dma("sp", "oconv", lambda e, s, dst=dst, OS=OS: e.dma_start(out=dst, in_=OS[0:48, :]).then_inc(s, 16),
                               reads=["OS"], is_out=True)
                b2 = self.bank()
                for k in range(8):
                    self.mm(b2, tn, wzg[:, k, 0:128], self.XN[:, k, t0:t0 + tn], k == 0, k == 7, [("XN", k, t0), kzg])
                ps2 = self.PS[b2][:, 0:tn]
                P.act(lambda e, ps2=ps2, tn=tn: e.activation(out=GZ[:, 0:tn], in_=ps2, func=AF.Gelu_apprx_tanh), reads=[("ps", b2)],
                      writes=["GZ"])
                b3 = self.bank()
                self.mm(b3, tn, WG[:, c * 2, :], XCB[:, 0:tn], True, True, ["WG", "XCB"])
                b4 = self.bank()
                self.mm(b4, tn, WG[:, c * 2 + 1, :], XCB[:, 0:tn], True, True, ["WG", "XCB"])
                ps3 = self.PS[b3][:, 0:tn]
                ps4 = self.PS[b4][:, 0:tn]
                rv, iv, av, bv, hv = R[:, 0:tn], IG[:, 0:tn], A[:, 0:tn], BX[:, 0:tn], HS[:, 0:tn]
                P.act(lambda e, rv=rv, ps3=ps3, brg=brg: e.activation(out=rv, in_=ps3, func=AF.Sigmoid, bias=brg),
                      reads=[("ps", b3), "VEC"], writes=["R"])
                P.act(lambda e, iv=iv, ps4=ps4, big=big: e.activation(out=iv, in_=ps4, func=AF.Sigmoid, bias=big),
                      reads=[("ps", b4), "VEC"], writes=["IG"])
                P.act(lambda e, av=av, rv=rv, lc1=lc1: e.activation(out=av, in_=rv, func=AF.Exp, scale=lc1),
                      reads=["R", "LAMC"], writes=["A"])
                P.act(lambda e, rv=rv, lc2=lc2: e.activation(out=rv, in_=rv, func=AF.Exp, scale=lc2),
                      reads=["R", "LAMC"], writes=["R"])
                P.act(lambda e, rv=rv, one=one: e.activation(out=rv, in_=rv, func=AF.Sqrt, bias=one, scale=-1.0),
                      reads=["R", "VEC"], writes=["R"])
                P.dve(lambda e, bv=bv, rv=rv, iv=iv: e.tensor_tensor(out=bv, in0=rv, in1=iv, op=ALU.mult), reads=["R", "IG"],
                      writes=["BX"])
                P.dve(lambda e, bv=bv, xc2=xc2: e.tensor_tensor(out=bv, in0=bv, in1=xc2, op=ALU.mult), reads=["BX", "XC"],
                      writes=["BX"])
                ly = LY[:, c, t0:t0 + tn]
                if not sample:
                    st = self.STL[:, jc:jc + 1]
                    if U["u"] == 0 and t0 == 0:
                        P.dve(lambda e, hv=hv, av=av, bv=bv: e.tensor_tensor_scan(
                            out=hv, data0=av, data1=bv, initial=0.0, op0=ALU.mult, op1=ALU.add), reads=["A", "BX"], writes=["HS"])
                    else:
                        P.dve(lambda e, hv=hv, av=av, bv=bv, st=st: e.tensor_tensor_scan(
                            out=hv, data0=av, data1=bv, initial=st, op0=ALU.mult, op1=ALU.add),
                            reads=["A", "BX", ("STL", jc)], writes=["HS"])
                    P.dve(lambda e, st=st, hv=hv, tn=tn: e.tensor_copy(out=st, in_=hv[:, tn - 1:tn]), reads=["HS"],
                          writes=[("STL", jc)])
                    if U["u"] == 1 and t0 + tn == NT:
                        dst = o["lru_p"][j, U["seq"], c * 128:(c + 1) * 128].rearrange("(p a) -> p a", a=1)
                        self.P.dma("sp", "olru", lambda e, s, dst=dst, st=st: self._nc_dma(e, s, dst, st),
                                   reads=[("STL", jc)], is_out=True)
                else:
                    sl = self.view(o0 + 16928, [128], F32)[0:16, :]
                    self.ld("sp", "stl", sl, d["st_lru"][j][:, c * 128:(c + 1) * 128], ["SL"])
                    bt = self.bank()
                    pst = self.PS[bt]
                    P.pe(lambda e, pst=pst, sl=sl: e.transpose(pst[:, 0:16], sl, self.IDF[0:16, 0:16]), reads=["SL", "IDF"],
                         writes=[("ps", bt)])
                    h3 = HS[:, 0:64].rearrange("p (b t) -> p b t", t=4)
                    a3 = A[:, 0:64].rearrange("p (b t) -> p b t", t=4)
                    b3v = BX[:, 0:64].rearrange("p (b t) -> p b t", t=4)
                    for t in range(4):
                        prev = pst[:, 0:16] if t == 0 else h3[:, :, t - 1]
                        rk = [("ps", bt)] if t == 0 else ["HS"]
                        P.dve(lambda e, t=t, prev=prev: e.tensor_tensor(out=h3[:, :, t], in0=prev, in1=a3[:, :, t], op=ALU.mult),
                              reads=rk + ["A"], writes=["HS"])
                        P.dve(lambda e, t=t: e.tensor_tensor(out=h3[:, :, t], in0=h3[:, :, t], in1=b3v[:, :, t], op=ALU.add),
                              reads=["HS", "BX"], writes=["HS"])
                    CL = self.view(o0 + 18144, [16], F32)
                    OL = self.view(o0 + 18208, [128], F32)
                    P.dve(lambda e, CL=CL, h3=h3: e.tensor_copy(out=CL, in_=h3[:, :, 3]), reads=["HS"], writes=["CL"])
                    bt3 = self.bank()
                    pst3 = self.PS[bt3]
                    P.pe(lambda e, pst3=pst3, CL=CL: e.transpose(pst3[0:16, 0:128], CL, self.IDF), reads=["CL", "IDF"],
                         writes=[("ps", bt3)])
                    P.dve(lambda e, pst3=pst3, OL=OL: e.tensor_copy(out=OL[0:16, :], in_=pst3[0:16, 0:128]), reads=[("ps", bt3)],
                          writes=["OL"])
                    dst = o["lru_s"][j, :, c * 128:(c + 1) * 128]
                    self.P.dma("sp", "olru", lambda e, s, dst=dst, OL=OL: e.dma_start(out=dst, in_=OL[0:16, :]).then_inc(s, 16),
                               reads=["OL"], is_out=True)
                P.dve(lambda e, ly=ly, hv=hv, tn=tn: e.tensor_tensor(out=ly, in0=hv, in1=GZ[:, 0:tn], op=ALU.mult), reads=["HS", "GZ"],
                      writes=[("LY", c)])
        for (t0, tn) in U["tiles"]:
            pass
        for mh in range(2):
            i0, _, k0 = self.w8()
            W = self.arena_w8_view(i0, [4, 512])
            self.ld("pool", ("w8", i0), W, d["w_out_ab"][j][512:1024, mh * 512:(mh + 1) * 512].rearrange("(k p) n -> p k n", p=128),
                    [k0])
            for (t0, tn) in U["tiles"]:
                for mm_ in range(4):
                    m = mh * 4 + mm_
                    b = self.bank()
                    for c in range(4):
                        self.mm(b, tn, W[:, c, mm_ * 128:(mm_ + 1) * 128], LY[:, c, t0:t0 + tn], c == 0, c == 3, [k0, ("LY", c)])
                    self.h_add(b, m, t0, tn)

    def _nc_dma(self, e, s, dst, src):
        with self.nc.allow_non_contiguous_dma(reason="tiny strided state rows"):
            e.dma_start(out=dst, in_=src).then_inc(s, 16)

    def gmlp_phase(self, U, j):
        P, d, o = self.P, self.d, self.o
        S0 = self.S0
        NT = U["NT"]
        sample = U["kind"] == "s"
        UU = self.view(S0, [8, 1024], BF16)
        VT = [self.view(S0 + 16384 + 2048 * i, [1024], BF16) for i in range(2)]
        VN = [self.view(S0 + 20480 + 4096 * i, [1024], F32) for i in range(2)]
        LNG = self.view(S0 + 28672, [1024], F32)
        LNB = self.view(S0 + 32768, [1024], F32)
        BSB = self.view(S0 + 36864, [8, 128], F32)
        WS = self.view(S0 + 40960, [8, 128], BF16)
        ST = self.view(S0 + 43008, [8], F32)
        self.ld("sp", "lng", LNG, d["lng"][j], ["LNG"])
        self.ld("sp", "lnb", LNB, d["lnb"][j], ["LNB"])
        L = 64 if sample else 128
        if sample:
            self.ld("sp", "bsb", BSB[:, :, 0:64], d["bsbs"][j].rearrange("p (g t) -> p g t", g=8), ["BSB"])
            self.ld("pool", "ws", WS[0:64, :, 0:64], d["wsTs"][j].rearrange("p (g t) -> p g t", g=8), ["WS"])
            mk = self.MKS[0:64, :]
        else:
            self.ld("sp", "bsb", BSB, d["bsb"][j].rearrange("p (g t) -> p g t", g=8), ["BSB"])
            self.ld("pool", "ws", WS, d["wsT"][j].rearrange("p (g t) -> p g t", g=8), ["WS"])
            mk = self.MKB
        for g in range(8):
            wv = WS[0:L, g, 0:L]
            P.dve(lambda e, wv=wv: e.tensor_tensor(out=wv, in0=wv, in1=mk, op=ALU.mult), reads=["WS", "MKB", "MKS"], writes=["WS"])
        for mh in range(2):
            pfw = self.pf.pop("gmlp", None) if mh == 0 else None
            wu, ku = pfw if pfw is not None else self.load_w8(d["w_in_c"][j], mh * 512, 512)
            for (t0, tn) in U["tiles"]:
                for mm_ in range(4):
                    m = mh * 4 + mm_
                    b = self.bank()
                    for k in range(8):
                        self.mm(b, tn, wu[:, k, mm_ * 128:(mm_ + 1) * 128], self.XN[:, k, t0:t0 + tn], k == 0, k == 7,
                                [("XN", k, t0), ku])
                    ps = self.PS[b][:, 0:tn]
                    uo = UU[:, m, t0:t0 + tn]
                    P.act(lambda e, uo=uo, ps=ps: e.activation(out=uo, in_=ps, func=AF.Gelu_apprx_tanh), reads=[("ps", b)],
                          writes=[("UU", m, t0)])
        wv0, kv0 = self.load_w8(d["w_in_c"][j], 1024, 512)
        wv1, kv1 = self.load_w8(d["w_in_c"][j], 1536, 512)
        for bi, (t0, tn) in enumerate(U["blocks"]):
            vi = self.rot("vn", 2)
            vn = VN[vi]
            vt = VT[vi]
            for hh, (wv, kv) in enumerate(((wv0, kv0), (wv1, kv1))):
                b = self.bank()
                for k in range(8):
                    self.mm(b, 512, self.XN[:, k, t0:t0 + tn], wv[:, k, :], k == 0, k == 7, [("XN", k, (t0 // 512) * 512), kv], m=tn)
                ps = self.PS[b][0:tn, :]
                P.act(lambda e, vn=vn, ps=ps, hh=hh, tn=tn: e.activation(out=vn[0:tn, hh * 512:(hh + 1) * 512], in_=ps,
                                                                         func=AF.Gelu_apprx_tanh),
                      reads=[("ps", b)], writes=[("VN", vi)])
            st = ST
            vnv = vn[0:tn, :]
            P.dve(lambda e, vnv=vnv, tn=tn, vt=vt: e.tensor_scalar(out=vt[0:tn, :], in0=vnv, scalar1=1.0, scalar2=0.0, op0=ALU.mult,
                                                            op1=ALU.add, accum_out=st[0:tn, 0:1]),
                  reads=[("VN", vi)], writes=[("VT", vi), "ST0"])
            P.dve(lambda e, tn=tn: e.tensor_scalar(st[0:tn, 1:2], st[0:tn, 0:1], -1.0 / 1024, None, ALU.mult), reads=["ST0"],
                  writes=["ST1"])
            P.dve(lambda e, vnv=vnv, tn=tn: e.tensor_scalar(vnv, vnv, st[0:tn, 1:2], None, ALU.add), reads=[("VN", vi), "ST1"],
                  writes=[("VN", vi)])
            P.act(lambda e, vnv=vnv, tn=tn, vt=vt: e.activation(out=vt[0:tn, :], in_=vnv, func=AF.Square, accum_out=st[0:tn, 2:3]),
                  reads=[("VN", vi)], writes=[("VT", vi), "ST2"])
            P.act(lambda e, tn=tn: e.activation(out=st[0:tn, 3:4], in_=st[0:tn, 2:3], func=AF.Sqrt,
                                                bias=self.VEC[0:tn, VC["eps"]:VC["eps"] + 1], scale=1.0 / 1024),
                  reads=["ST2", "VEC"], writes=["ST3"])
            P.dve(lambda e, tn=tn: e.reciprocal(out=st[0:tn, 3:4], in_=st[0:tn, 3:4]), reads=["ST3"], writes=["ST3"])
            P.dve(lambda e, vnv=vnv, tn=tn: e.scalar_tensor_tensor(out=vnv, in0=vnv, scalar=st[0:tn, 3:4], in1=LNG[0:tn, :],
                                                                   op0=ALU.mult, op1=ALU.mult),
                  reads=[("VN", vi), "ST3", "LNG"], writes=[("VN", vi)])
            P.dve(lambda e, vnv=vnv, tn=tn: e.tensor_tensor(out=vnv, in0=vnv, in1=LNB[0:tn, :], op=ALU.add),
                  reads=[("VN", vi), "LNB"], writes=[("VN", vi)])
            P.act(lambda e, vnv=vnv, tn=tn, vt=vt: e.activation(out=vt[0:tn, :], in_=vnv, func=AF.Copy), reads=[("VN", vi)],
                  writes=[("VT", vi)])
            if sample:
                self.st(("ovs", vi), o["v_s"][j, t0:t0 + tn, :], vnv, [("VN", vi)])
            for g in range(8):
                b = self.bank()
                self.mm(b, tn, vt[0:tn, g * 128:(g + 1) * 128], WS[0:tn, g, 0:tn], True, True, [("VT", vi), "WS"])
                ps = self.PS[b][:, 0:tn]
                uo = UU[:, g, t0:t0 + tn]
                ti = self.rot("sp_t", 2)
                tmp = self.view(S0 + 43520 + 512 * ti, [128], F32)[:, 0:tn]
                P.dve(lambda e, tmp=tmp, ps=ps, g=g, tn=tn: e.tensor_tensor(out=tmp, in0=ps, in1=BSB[:, g, 0:tn], op=ALU.add),
                      reads=[("ps", b), "BSB"], writes=[("SPT", ti)])
                P.dve(lambda e, uo=uo, tmp=tmp: e.tensor_tensor(out=uo, in0=uo, in1=tmp, op=ALU.mult),
                      reads=[("SPT", ti), ("UU", g, (t0 // 512) * 512)], writes=[("UU", g, (t0 // 512) * 512)])
        for mh in range(2):
            wo, ko = self.load_w8(d["w_out_c"][j], mh * 512, 512)
            for (t0, tn) in U["tiles"]:
                for mm_ in range(4):
                    m = mh * 4 + mm_
                    b = self.bank()
                    for k in range(8):
                        self.mm(b, tn, wo[:, k, mm_ * 128:(mm_ + 1) * 128], UU[:, k, t0:t0 + tn], k == 0, k == 7,
                                [("UU", k, t0), ko])
                    self.h_add(b, m, t0, tn)

    def ffn_phase(self, U, i):
        P, d = self.P, self.d
        S0 = self.S0
        HID = [self.view(S0 + 12288 + 4096 * q, [4, 512], BF16) for q in range(2)]
        SG = [self.view(S0 + 20480 + 2048 * q, [512], F32) for q in range(2)]
        blocks = [(c0, min(512, FFN - c0)) for c0 in range(0, FFN, 512)]
        nblk = len(blocks)

        def issue(bk):
            c0, n = blocks[bk]
            ia, _, ka = self.w8()
            WG_ = self.arena_w8_view(ia, [8, 512])
            self.ld("pool", ("w8", ia), WG_[:, :, 0:n], d["w_gate"][i][:, c0:c0 + n].rearrange("(k p) n -> p k n", p=128), [ka])
            ib, _, kb_ = self.w8()
            WU_ = self.arena_w8_view(ib, [8, 512])
            self.ld("pool", ("w8", ib), WU_[:, :, 0:n], d["w_up"][i][:, c0:c0 + n].rearrange("(k p) n -> p k n", p=128), [kb_])
            return [WG_, ka, WU_, kb_, None, n // 128]

        def issue_d(bk):
            c0, n = blocks[bk]
            wd = []
            for r0 in range(0, n, 256):
                i1, W4, k1 = self.w4()
                self.ld("pool", ("w4", i1), W4, d["w_down"][i][c0 + r0:c0 + r0 + 256, :].rearrange("(k p) n -> p k n", p=128), [k1])
                wd.append((W4, k1))
            wts[bk][4] = wd

        steps = [(bk, t0, tn) for bk in range(nblk) for (t0, tn) in U["tiles"]]
        wts = {}
        wts[0] = issue(0)
        issue_d(0)

        def gate_up(bk, t0, tn):
            WG_, ka, WU_, kb_, wd, nch = wts[bk]
            hi = self.rot("hid", 2)
            hid = HID[hi]
            for hc in range(nch):
                bg = self.bank()
                for k in range(8):
                    self.mm(bg, tn, WG_[:, k, hc * 128:(hc + 1) * 128], self.XN[:, k, t0:t0 + tn], k == 0, k == 7, [("XN", k, t0), ka])
                bu = self.bank()
                for k in range(8):
                    self.mm(bu, tn, WU_[:, k, hc * 128:(hc + 1) * 128], self.XN[:, k, t0:t0 + tn], k == 0, k == 7, [("XN", k, t0), kb_])
                si = self.rot("sg", 2)
                sg = SG[si][:, 0:tn]
                psg = self.PS[bg][:, 0:tn]
                psu = self.PS[bu][:, 0:tn]
                P.act(lambda e, sg=sg, psg=psg: e.activation(out=sg, in_=psg, func=AF.Silu), reads=[("ps", bg)], writes=[("SG", si)])
                ho = hid[:, hc, 0:tn]
                P.dve(lambda e, ho=ho, psu=psu, sg=sg: e.tensor_tensor(out=ho, in0=psu, in1=sg, op=ALU.mult),
                      reads=[("ps", bu), ("SG", si)], writes=[("HID", hi, hc)])
            return hi, hid

        def down(bk, t0, tn, hi, hid):
            WG_, ka, WU_, kb_, wd, nch = wts[bk]
            for m in range(8):
                b = self.bank()
                for hc in range(nch):
                    W4, k1 = wd[hc // 2]
                    self.mm(b, tn, W4[:, hc % 2, m * 128:(m + 1) * 128], hid[:, hc, 0:tn], hc == 0, hc == nch - 1, [k1, ("HID", hi, hc)])
                self.h_add(b, m, t0, tn)

        prev = None
        for si_, (bk, t0, tn) in enumerate(steps):
            if t0 == 0 and bk + 1 < nblk:
                wts[bk + 1] = issue(bk + 1)
            cur = gate_up(bk, t0, tn)
            import os
            if os.environ.get("MK_NOPIPE", "0") == "1":
                down(bk, t0, tn, *cur)
                continue
            if prev is not None:
                down(*prev)
            if t0 == 0 and bk + 1 < nblk:
                issue_d(bk + 1)
            prev = (bk, t0, tn) + cur
        if prev is not None:
            down(*prev)

    def peg_phase(self, U, i):
        P, d = self.P, self.d
        S0 = self.S0
        NT = U["NT"]
        PT = self.view(S0 + 24576, [2, 1024], BF16)
        PTOK = [self.view(S0 + 28672 + 512 * q, [256], BF16) for q in range(2)]
        GATE = [self.view(S0 + 29696 + 2048 * q, [512], F32) for q in range(2)]
        for (t0, tn) in U["blocks"]:
            pi = self.rot("ptok", 2)
            pt = PTOK[pi]
            if U["kind"] == "p":
                src = d["pp"][i, U["seq"], U["pos0"] + t0:U["pos0"] + t0 + tn, :]
            else:
                src = d["pps"][i, t0:t0 + tn, :]
            self.ld("pool", ("ptok", pi), pt[0:tn, :], src, [("PTOK", pi)])
            b = self.bank()
            pb = self.PS[b][:, 0:256].bitcast(BF16)
            for cc in range(2):
                P.pe(lambda e, pb=pb, pt=pt, cc=cc, tn=tn: e.transpose(pb[:, cc * 128:cc * 128 + tn], pt[0:tn, cc * 128:(cc + 1) * 128],
                                                                      self.IDB[0:tn, 0:tn]),
                     reads=[("PTOK", pi), "IDB"], writes=[("ps", b)])
            dst = PT[:, :, t0:t0 + tn]
            srcv = pb[:, 0:256].rearrange("p (a b) -> p a b", a=2)[:, :, 0:tn]
            P.dve(lambda e, dst=dst, srcv=srcv: e.tensor_copy(out=dst, in_=srcv), reads=[("ps", b)], writes=[("PT", t0)])
        i1, W4, k1 = self.w4()
        self.ld("pool", ("w4", i1), W4, d["w_pe"][i].rearrange("(k p) n -> p k n", p=128), [k1])
        for mh in range(2):
            wg, kg = self.load_w8(d["w_pg"][i], mh * 512, 512)
            for (t0, tn) in U["tiles"]:
                for mm_ in range(4):
                    m = mh * 4 + mm_
                    b = self.bank()
                    for k in range(8):
                        self.mm(b, tn, wg[:, k, mm_ * 128:(mm_ + 1) * 128], self.XN[:, k, t0:t0 + tn], k == 0, k == 7, [("XN", k, t0), kg])
                    b2 = self.bank()
                    for k in range(2):
                        self.mm(b2, tn, W4[:, k, m * 128:(m + 1) * 128], PT[:, k, t0:t0 + tn], k == 0, k == 1,
                                [k1] + [("PT", tb) for tb in range(t0, t0 + tn, 128)])
                    gi = self.rot("gate", 2)
                    gt = GATE[gi][:, 0:tn]
                    ps = self.PS[b][:, 0:tn]
                    ps2 = self.PS[b2][:, 0:tn]
                    P.act(lambda e, gt=gt, ps=ps: e.activation(out=gt, in_=ps, func=AF.Sigmoid), reads=[("ps", b)], writes=[("GATE", gi)])
                    P.dve(lambda e, gt=gt, ps2=ps2: e.tensor_tensor(out=gt, in0=ps2, in1=gt, op=ALU.mult), reads=[("ps", b2), ("GATE", gi)],
                          writes=[("GATE", gi)])
                    hv = self.H[:, m, t0:t0 + tn]
                    P.dve(lambda e, hv=hv, gt=gt: e.tensor_tensor(out=hv, in0=hv, in1=gt, op=ALU.add), reads=[("GATE", gi)],
                          writes=[("Hm", m, t0)])

    def final_phase(self, U):
        P, d, o = self.P, self.d, self.o
        S0 = self.S0
        XT = [self.view(S0 + 51328 + 4096 * i, [1024], F32) for i in range(2)]
        GF = self.view(S0 + 59520, [1024], F32)
        JK = self.view(S0 + 63616, [1024], BF16)
        SS = self.view(S0 + 65664, [4], F32)
        self.ld("sp", "gfin", GF, d["gfin"], ["GF"])
        for (t0, tn) in U["blocks"]:
            xi = self.rot("xt", 2)
            xt = XT[xi]
            for half in range(2):
                b = self.bank()
                ps = self.PS[b]
                for q in range(4):
                    m = half * 4 + q
                    P.pe(lambda e, ps=ps, q=q, m=m, t0=t0, tn=tn: e.transpose(ps[0:tn, q * 128:(q + 1) * 128], self.H[:, m, t0:t0 + tn],
                                                                            self.IDF),
                         reads=["IDF", ("Hm", m, (t0 // 512) * 512)], writes=[("ps", b)])
                xo = xt[0:tn, half * 512:(half + 1) * 512]
                pv = ps[0:tn, :]
                P.act(lambda e, xo=xo, pv=pv: e.activation(out=xo, in_=pv, func=AF.Copy), reads=[("ps", b)], writes=[("XT", xi)])
            xv = xt[0:tn, :]
            P.act(lambda e, xv=xv, tn=tn: e.activation(out=JK[0:tn, :], in_=xv, func=AF.Square, accum_out=SS[0:tn, 0:1]),
                  reads=[("XT", xi)], writes=["JK", "SSF"])
            P.act(lambda e, tn=tn: e.activation(out=SS[0:tn, 1:2], in_=SS[0:tn, 0:1], func=AF.Sqrt,
                                                bias=self.VEC[0:tn, VC["eps"]:VC["eps"] + 1], scale=1.0 / D),
                  reads=["SSF", "VEC"], writes=["SSF1"])
            P.dve(lambda e, tn=tn: e.reciprocal(out=SS[0:tn, 1:2], in_=SS[0:tn, 1:2]), reads=["SSF1"], writes=["SSF1"])
            P.dve(lambda e, xv=xv, tn=tn: e.scalar_tensor_tensor(out=xv, in0=xv, scalar=SS[0:tn, 1:2], in1=GF[0:tn, :],
                                                                 op0=ALU.mult, op1=ALU.mult),
                  reads=[("XT", xi), "SSF1", "GF"], writes=[("XT", xi)])
            if U["kind"] == "p":
                dst = o["y_p"][U["seq"], U["pos0"] + t0:U["pos0"] + t0 + tn, :]
            else:
                dst = o["y_s"][t0:t0 + tn, :]
            self.st(("oy", xi), dst, xv, [("XT", xi)])

    def attn_sample(self, U, j):
        P, d = self.P, self.d
        S0 = self.S0
        XT = self.view(S0, [3, 2048], BF16)
        G32C = [self.view(S0 + 12288 + 16384 * i, [16, 256], F32) for i in range(2)]
        G32K = [self.view(S0 + 45056 + 2048 * i, [16, 32], F32) for i in range(2)]
        XB = self.view(S0 + 49152, [16, 384], BF16)
        ES = [self.view(S0 + 61440 + 1024 * i, [512], BF16) for i in range(2)]
        EN = self.view(S0 + 63488, [512], BF16)
        OLB = self.view(S0 + 64512, [256], BF16)
        RCP = self.view(S0 + 65024, [2], F32)
        w4a, w4b, w4c = 114688, 114688 + 4096, 114688 + 8192
        QLT = self.view(w4a, [2, 16, 32], BF16)
        QPT = self.view(w4a + 2048, [16, 32], BF16)
        QG = self.view(w4a + 3072, [8, 64], BF16)
        OLT = self.view(w4b, [2, 8, 64], BF16)
        OG = self.view(w4b + 2048, [8, 64], BF16)
        MKN = self.view(w4b + 3072, [512], BF16)
        IDX = self.view(w4c, [64], I32)
        PTB = self.view(w4c + 256, [64], I32)
        TF = self.view(w4c + 512, [64], F32)
        COSQ = self.view(w4c + 768, [64], F32)
        SINQ = self.view(w4c + 1024, [64], F32)
        TQ = [self.view(w4c + 1280 + 256 * i, [64], F32) for i in range(2)]
        i0, _, wqk = self.w8()
        WQv = self.arena_w8_view(i0, [3, 1024])
        self.ld("pool", ("w8", i0), WQv, d["wuq"][j].rearrange("(k p) n -> p k n", p=128), [wqk])
        i1, _, wkk = self.w8()
        WUKN = self.arena_w8_view(i1, [8, 256])
        WUVH = self.arena_w8_view(i1, [2, 512], off=4096)
        self.ld("pool", ("w8", i1), WUKN[0:64, :, :], d["wukn"][j].rearrange("n (h c) -> n h c", h=8), [wkk])
        self.ld("pool", ("w8b", i1), WUVH, d["wuvh"][j].rearrange("(k p) n -> p k n", p=128), [("W8b", i1)])
        WOAs = []
        for g in range(2):
            i2, _, wok = self.w8()
            WOA = self.arena_w8_view(i2, [4, 1024])
            self.ld("pool", ("w8", i2), WOA[0:64, :, :],
                    d["w_out_ab"][j][g * 256:(g + 1) * 256, :].rearrange("(h v) n -> v h n", v=64), [wok])
            WOAs.append((WOA, wok))
        self.ld("pool", "mkn", MKN[0:64, :], d["maskn"], ["MKN"])
        self.ld("sp", "cosq", COSQ[64:96, :], d["cosq"][:, SEQ:SEQ + 64], ["COSQ"])
        self.ld("sp", "sinq", SINQ[64:96, :], d["sinq"][:, SEQ:SEQ + 64], ["SINQ"])
        self.ld("sp", "ptb", PTB, d["ptab"], ["PTB"])
        P.dve(lambda e: e.tensor_copy(out=TF, in_=PTB), reads=["PTB"], writes=["TF"])
        pm8 = self.vc("pm8")
        nrows = d["cache_ckv"].shape[1]
        P.dve(lambda e: e.tensor_scalar(TF, TF, 8.0, pm8, ALU.mult, ALU.add), reads=["TF", "VEC"], writes=["TF"])
        P.dve(lambda e: e.tensor_scalar(IDX, TF, float(j * nrows), None, ALU.add), reads=["TF"], writes=["IDX"])
        P.dve(lambda e: e.memset(XB, 1.0), writes=[("XB", 0)])
        for h in range(8):
            self.q_head(WQv, wqk, h, 0, 64, QG, h, COSQ[64:96, :], SINQ[64:96, :], TQ, ("QGs", h))
        for h in range(8):
            for cc in range(2):
                b = self.bank()
                self.mm(b, 64, WUKN[0:64, h, cc * 128:(cc + 1) * 128], QG[0:64, h, :], True, True, [wkk, ("QGs", h)])
                dst = QLT[:, cc, :, :].rearrange("p b (t h) -> p b t h", h=8)[:, :, :, h]
                src = self.PS[b][:, 0:64].rearrange("p (b t) -> p b t", t=4)
                P.dve(lambda e, dst=dst, src=src: e.tensor_copy(out=dst, in_=src), reads=[("ps", b)], writes=["QLT"])
            dstp = QPT[0:32, :, :].rearrange("p b (t h) -> p b t h", h=8)[:, :, :, h]
            srcp = QG[64:96, h, :].rearrange("p (b t) -> p b t", t=4)
            P.act(lambda e, dstp=dstp, srcp=srcp: e.activation(out=dstp, in_=srcp, func=AF.Copy), reads=[("QGs", h)], writes=["QPT"])
        bn_ = self.bank()
        for b_ in range(NSB):
            for cc in range(2):
                self.mm(bn_, 32, self.SCKT[:, cc, 0:64], QLT[:, cc, b_, :], cc == 0, False, ["QLT"], m=64, c0=b_ * 32)
            self.mm(bn_, 32, self.SKPT[0:32, 0:64], QPT[0:32, b_, :], False, True, ["QPT"], m=64, c0=b_ * 32)
        psn = self.PS[bn_][0:64, :]
        P.act(lambda e: e.activation(out=EN[0:64, :], in_=psn, func=AF.Exp, scale=ATTN_SCALE), reads=[("ps", bn_)], writes=["EN"])
        P.dve(lambda e: e.tensor_tensor(out=EN[0:64, :], in0=EN[0:64, :], in1=MKN[0:64, :], op=ALU.mult), reads=["EN", "MKN"],
              writes=["EN"])
        ckv_rows = d["cache_ckv"].rearrange("l r c -> (l r) c")
        kpe_rows = d["cache_kpe"].rearrange("l r c -> (l r) c")
        XBs = [XB, self.view(49152, [16, 384], BF16)]
        XTs = [XT, self.view(61440, [3, 2048], BF16)]
        P.dve(lambda e: e.memset(XBs[1], 1.0), writes=[("XB", 1)])
        items = [(b_, i) for b_ in range(NSB) for i in range(4)]
        state = {}

        def stage1(n):
            b_, i = items[n]
            q = n % 2
            xb, xt = XBs[q], XTs[q]
            col = b_ * 4 + i
            gi = self.rot("g32", 2)
            gc, gk = G32C[gi], G32K[gi]
            idx = IDX[:, col:col + 1]
            P.dma("pool", ("gc", gi), lambda e, s, gc=gc, idx=idx: e.indirect_dma_start(
                out=gc.rearrange("p a b -> p (a b)"), out_offset=None, in_=ckv_rows,
                in_offset=bass.IndirectOffsetOnAxis(ap=idx, axis=0)).then_inc(s, 16),
                reads=["IDX"], writes=[("G32C", gi)])
            P.dma("pool", ("gk", gi), lambda e, s, gk=gk, idx=idx: e.indirect_dma_start(
                out=gk.rearrange("p a b -> p (a b)"), out_offset=None, in_=kpe_rows,
                in_offset=bass.IndirectOffsetOnAxis(ap=idx, axis=0)).then_inc(s, 16),
                reads=["IDX"], writes=[("G32K", gi)])
            P.act(lambda e, gc=gc, xb=xb: e.activation(out=xb[:, :, 0:256], in_=gc, func=AF.Copy), reads=[("G32C", gi)],
                  writes=[("XB", q)])
            P.dve(lambda e, gk=gk, xb=xb: e.tensor_copy(out=xb[:, :, 256:288], in_=gk), reads=[("G32K", gi)], writes=[("XB", q)])
            for r2 in range(8):
                bt = self.bank()
                pb = self.PS[bt][:, 0:384].bitcast(BF16)
                for rr in range(2):
                    r = r2 * 2 + rr
                    for w in range(3):
                        P.pe(lambda e, pb=pb, r=r, rr=rr, w=w, xb=xb: e.transpose(
                            pb[:, (rr * 3 + w) * 128:(rr * 3 + w + 1) * 128], xb[:, r, w * 128:(w + 1) * 128], self.IDB),
                            reads=[("XB", q), "IDB"], writes=[("ps", bt)])
                dst = xt[:, :, r2 * 256:(r2 + 1) * 256].rearrange("p w (r q) -> p r w q", r=2)
                src = pb.rearrange("p (r w q) -> p r w q", r=2, w=3)
                P.dve(lambda e, dst=dst, src=src: e.tensor_copy(out=dst, in_=src), reads=[("ps", bt)], writes=[("XT", q)])

        def stage2(n):
            b_, i = items[n]
            q = n % 2
            xt = XTs[q]
            bs_ = self.bank()
            for r in range(16):
                for cc in range(2):
                    self.mm(bs_, 32, xt[:, cc, r * 128:(r + 1) * 128], QLT[:, cc, b_, :], cc == 0, False, [("XT", q), "QLT"], c0=r * 32)
                self.mm(bs_, 32, xt[0:32, 2, r * 128:(r + 1) * 128], QPT[0:32, b_, :], False, True, [("XT", q), "QPT"], c0=r * 32)
            ei = self.rot("es", 2)
            es_ = ES[ei]
            pss = self.PS[bs_][:, :]
            P.act(lambda e, es_=es_, pss=pss: e.activation(out=es_, in_=pss, func=AF.Exp, scale=ATTN_SCALE), reads=[("ps", bs_)],
                  writes=[("ES", ei)])
            state[n] = (es_, ei)

        def stage3(n):
            b_, i = items[n]
            q = n % 2
            xb = XBs[q]
            es_, ei = state.pop(n)
            if i == 0:
                state["bo"] = self.bank(hold=True)
            bo = state["bo"]
            for r in range(16):
                self.mm(bo, 289, es_[:, r * 32:(r + 1) * 32], xb[:, r, 0:289], i == 0 and r == 0, False, [("ES", ei), ("XB", q)], m=32)
            if i == 3:
                pso = self.PS[bo]
                self.mm(bo, 289, EN[0:64, b_ * 32:(b_ + 1) * 32], self.SNEW[0:64, 0:289], False, True,
                        ["EN", "SNEWc", "SNEWk", "SNEWp"], m=32)
                P.dve(lambda e, pso=pso: e.reciprocal(out=RCP[0:32, 0:1], in_=pso[0:32, 288:289]), reads=[("ps", bo)], writes=["RCP"])
                P.dve(lambda e, pso=pso: e.tensor_scalar(OLB[0:32, :], pso[0:32, 0:256], RCP[0:32, 0:1], None, ALU.mult),
                      reads=[("ps", bo), "RCP"], writes=["OLB"])
                self.release(bo)
                bt = self.bank()
                pb = self.PS[bt][:, 0:64].bitcast(BF16)
                for cc in range(2):
                    P.pe(lambda e, pb=pb, cc=cc: e.transpose(pb[:, cc * 32:(cc + 1) * 32], OLB[0:32, cc * 128:(cc + 1) * 128],
                                                             self.IDB[0:32, 0:32]), reads=["OLB", "IDB"], writes=[("ps", bt)])
                dst = OLT[:, :, :, b_ * 4:(b_ + 1) * 4]
                src = pb[:, 0:64].rearrange("p (c t h) -> p c h t", c=2, t=4)
                P.dve(lambda e, dst=dst, src=src: e.tensor_copy(out=dst, in_=src), reads=[("ps", bt)], writes=["OLT"])

        NI = len(items)
        stage1(0)
        for n in range(NI):
            stage2(n)
            if n + 1 < NI:
                stage1(n + 1)
            stage3(n)
        for h in range(8):
            b = self.bank()
            for cc in range(2):
                self.mm(b, 64, WUVH[:, cc, h * 64:(h + 1) * 64], OLT[:, cc, h, :], cc == 0, cc == 1, [("W8b", i1), "OLT"], m=64)
            ps = self.PS[b][0:64, 0:64]
            og = OG[0:64, h, :]
            P.act(lambda e, og=og, ps=ps: e.activation(out=og, in_=ps, func=AF.Copy), reads=[("ps", b)], writes=[("OGs", h)])
        for m in range(8):
            b = self.bank()
            for h in range(8):
                WOA, wok = WOAs[h // 4]
                self.mm(b, 64, WOA[0:64, h % 4, m * 128:(m + 1) * 128], OG[0:64, h, :], h == 0, h == 7, [wok, ("OGs", h)])
            self.h_add(b, m, 0, 64)


def _host_consts():
    c = {}
    c["ident"] = np.eye(128, dtype=np.float32)
    c["ones"] = np.ones((128, 128), np.float32)
    k = np.arange(128)
    c["maskc"] = (k[:, None] <= k[None, :]).astype(np.float32)
    s = np.arange(64)
    c["masks"] = ((s[:, None] // 4 == s[None, :] // 4) & (s[:, None] <= s[None, :])).astype(np.float32)
    mn = np.zeros((64, 16, 4, 8), np.float32)
    for sp_ in range(64):
        for t in range(4):
            if sp_ % 4 <= t:
                mn[sp_, sp_ // 4, t, :] = 1.0
    c["maskn"] = mn.reshape(64, 512)
    ps = np.zeros((32, 96), np.float32)
    ps[np.arange(32), 64 + np.arange(32)] = 1.0
    c["pesel"] = ps
    half = 16
    inv = np.exp(-math.log(10000.0) * np.arange(half, dtype=np.float32) / half).astype(np.float32)
    pos = np.concatenate([np.arange(SEQ), np.repeat(PAST + np.arange(4)[None, :], NSB, 0).reshape(-1)]).astype(np.float32)
    ang = (pos[:, None] * inv[None, :]).astype(np.float32)
    cos = np.cos(ang).astype(np.float32)
    sin = np.sin(ang).astype(np.float32)
    c["cosk"] = np.concatenate([cos, cos], 1)
    c["sink"] = np.concatenate([-sin, sin], 1)
    c["cosq"] = np.ascontiguousarray(c["cosk"].T)
    c["sinq"] = np.ascontiguousarray(c["sink"].T)
    return c


def _prep_shared(inp):
    f = lambda a: np.ascontiguousarray(np.asarray(a, dtype=np.float32))
    w = {}
    w["w_in_ab"] = f(inp["w_in_ab"])
    wuq = np.asarray(inp["w_uq"], np.float32).reshape(2, QL, 8, 96)
    swap = np.concatenate([wuq[..., 80:96], wuq[..., 64:80]], -1)
    w["wuq"] = f(np.concatenate([wuq, swap], -1).reshape(2, QL, 1024))
    wuk = np.asarray(inp["w_uk"], np.float32)
    t = np.zeros((2, KVL, 8, 96), np.float32)
    t[:, :, :, 0:64] = np.transpose(wuk, (0, 3, 1, 2))
    w["wuk"] = f(t.reshape(2, KVL, 768))
    w["wukn"] = f(np.transpose(wuk, (0, 2, 1, 3)).reshape(2, 64, 8 * KVL))
    wuv = np.asarray(inp["w_uv"], np.float32)
    w["wuv"] = f(np.transpose(wuv, (0, 2, 1, 3)).reshape(2, KVL, 512))
    w["wuvh"] = w["wuv"]
    wg = np.zeros((2, 128, 4, 2, 128), np.float32)
    for nm, q in (("w_rg", 0), ("w_ig", 1)):
        a = np.asarray(inp[nm], np.float32)
        for c in range(4):
            for hh in range(2):
                wg[:, hh * 64:(hh + 1) * 64, c, q, hh * 64:(hh + 1) * 64] = a[:, 2 * c + hh]
    w["wgate"] = f(wg.reshape(2, 128, 1024))
    w["w_out_ab"] = f(inp["w_out_ab"])
    w["w_in_c"] = f(inp["w_in_c"])
    ws = np.asarray(inp["w_s"], np.float32)
    w["wsT"] = f(np.transpose(ws, (0, 3, 1, 2)).reshape(2, 128, 1024))
    wss = np.transpose(ws[:, :, :4, :4], (0, 3, 1, 2))
    w["wsTs"] = f(np.tile(wss, (1, 16, 1, 16)).reshape(2, 64, 512))
    w["w_out_c"] = f(inp["w_out_c"])
    for k in ("w_gate", "w_up", "w_down", "w_pe", "w_pg"):
        w[k] = f(inp[k])
    vec = np.zeros((128, NV), np.float32)

    def put(name, arr):
        arr = np.asarray(arr, np.float32)
        n = arr.shape[0]
        x = arr.reshape(n, -1, 128)
        k = x.shape[1]
        vec[:, VC[name]:VC[name] + n * k] = np.transpose(x, (2, 0, 1)).reshape(128, n * k)

    put("g_mix", inp["g_mix"])
    put("g_ffn", inp["g_ffn"])
    put("g_pe", inp["g_pe"])
    put("g_q", inp["g_qnorm"])
    put("conv_w", np.asarray(inp["conv_w"]).reshape(8, LRU))
    put("conv_b", inp["conv_b"])
    put("b_rg", inp["b_rg"])
    put("b_ig", inp["b_ig"])
    put("lam", inp["lru_lambda"])
    vec[:, VC["eps"]] = EPS
    vec[:, VC["one"]] = 1.0
    vec[:, VC["pm8"]] = np.arange(128) % 8
    w["vecs"] = vec
    rep = lambda a: f(np.broadcast_to(np.asarray(a, np.float32)[..., None, :], a.shape[:-1] + (128, a.shape[-1])))
    w["gkv"] = rep(np.asarray(inp["g_kvnorm"]))
    w["gfin"] = rep(np.asarray(inp["g_final"]))
    w["lng"] = rep(np.asarray(inp["ln_g_c"]))
    w["lnb"] = rep(np.asarray(inp["ln_b_c"]))
    bs = np.asarray(inp["b_s"], np.float32)
    w["bsb"] = rep(bs.reshape(2, 1024))
    w["bsbs"] = rep(np.tile(bs[:, :, :4], (1, 1, 16)).reshape(2, 512))
    w["cache_ckv"] = np.asarray(inp["cache_ckv"], np.float32).reshape(2, -1, 16 * KVL)
    w["cache_kpe"] = np.asarray(inp["cache_kpe"], np.float32).reshape(2, -1, 16 * ROPE)
    w.update(_host_consts())
    return w


_NC_CACHE = {}


def _get_nc(do_sample=True):
    if do_sample not in _NC_CACHE:
        nc = bass.Bass("TRN2", target_bir_lowering=False)
        b = Builder(nc, do_sample)
        b.build()
        _NC_CACHE[do_sample] = nc
    return _NC_CACHE[do_sample]


def kernel(**inp):
    return _run(inp, True, NCORES)


def _run(inp, do_sample, ncores):
    shared = _prep_shared(inp)
    xp = np.asarray(inp["x_prompt"], np.float32)
    pp = np.asarray(inp["p_prompt"], np.float32)
    xs = np.asarray(inp["x_sample"], np.float32)
    pps = np.asarray(inp["p_sample"], np.float32)
    stl = np.asarray(inp["state_lru"], np.float32)
    stc = np.asarray(inp["state_conv"], np.float32)
    pt = np.asarray(inp["page_table"], np.int32)
    in_maps = []
    for c in range(ncores):
        m = dict(shared)
        m["xp"] = np.ascontiguousarray(xp[2 * c:2 * c + 2])
        m["pp"] = np.ascontiguousarray(pp[:, 2 * c:2 * c + 2])
        m["xs"] = np.ascontiguousarray(xs[NSB * c:NSB * (c + 1)].reshape(NTS, D))
        m["pps"] = np.ascontiguousarray(pps[:, NSB * c:NSB * (c + 1)].reshape(DEPTH, NTS, PLE))
        m["st_lru"] = np.ascontiguousarray(stl[:, NSB * c:NSB * (c + 1)])
        m["st_conv"] = np.ascontiguousarray(stc[:, NSB * c:NSB * (c + 1)].reshape(2, NSB * 3, LRU))
        ptc = pt[NSB * c:NSB * (c + 1)].reshape(NSB, 4, 16)
        m["ptab"] = np.ascontiguousarray(np.repeat(np.transpose(ptc, (2, 0, 1)), 8, axis=0).reshape(128, NSB * 4))
        in_maps.append(m)
    nc = _get_nc(do_sample)
    res = run_bass_kernel_spmd(nc, in_maps, core_ids=list(range(ncores)))
    R = res.results
    cat = lambda k, ax: np.concatenate([np.asarray(r[k], np.float32) for r in R], axis=ax)
    y_p = cat("y_p", 0)
    y_s = cat("y_s", 0).reshape(-1, 4, D)
    ckv_p = cat("ckv_p", 1)
    kpe_p = cat("kpe_p", 1)
    lru_p = cat("lru_p", 1)
    conv_p = cat("conv_p", 1)
    ckv_s = cat("ckv_s", 1).reshape(2, -1, 4, KVL)
    kpe_s = cat("kpe_s", 1).reshape(2, -1, 4, ROPE)
    lru_s = cat("lru_s", 1)
    conv_s = cat("conv_s", 1)
    v_s = cat("v_s", 1).reshape(2, -1, 4, 1024)
    return (y_p, y_s, ckv_p, kpe_p, lru_p, conv_p, ckv_s, kpe_s, lru_s, conv_s, v_s)
```
